# Optimizing a Trainium2 kernel written in Bass

```python
import math
import jax
import jax.numpy as jnp
from jax import lax
import numpy as np

D_MODEL = 2048
BATCH = 16
SEQ = 256
DEPTH = 2
DEC_BATCH = 4
DEC_SEQ = 1024
PAST_LEN = 512

GRID_W = 64
N_GROUPS = 4
D_GRP = D_MODEL // N_GROUPS
D_MIX = N_GROUPS * D_GRP
H_A = 4
DH_A = D_GRP // H_A
CHUNK = 64
H_B = 4
DV_B = D_GRP // H_B
DK_B = DV_B // 2
H_C = 4
KV_C = 2
G_C = H_C // KV_C
DH_C = D_GRP // H_C
WINDOW = 128
BLOCK = WINDOW
Q_BLOCK = 128
HY_ORDER = 2
HY_POS_DIM = 33
HY_BANDS = (HY_POS_DIM - 1) // 2
HY_FF = 64
HY_FAST = math.log(1e-2) / 0.3
HY_SLOW = math.log(1e-2) / 1.5
D_FF = 5632
ROPE_BASE = 10000.0
EPS = 1e-6
N_A = 4 * D_GRP + 4 * H_A
N_B = 3 * D_GRP
N_C = D_GRP + 2 * KV_C * DH_C
N_D = 3 * D_GRP
N_IN = N_A + N_B + N_C + N_D

kernel_name = 'hybrid_diffusion_parallel_groups_step'


def rms_norm(x, g):
    xf = x.astype(jnp.float32)
    y = xf * lax.rsqrt(jnp.mean(xf * xf, axis=-1, keepdims=True) + EPS)
    return (y * g.astype(jnp.float32)).astype(x.dtype)


def dwconv3(x, w, b):
    ch = x.shape[-1]
    y = lax.conv_general_dilated(x, w[:, None, :].astype(x.dtype), window_strides=(1,),
                                 padding=((1, 1),), dimension_numbers=('NWC', 'WIO', 'NWC'),
                                 feature_group_count=ch)
    return y + b.astype(x.dtype)


def axial_rope(x):
    n_tok, dh = x.shape[1], x.shape[-1]
    rows = n_tok // GRID_W
    t_row = jnp.repeat(jnp.arange(rows, dtype=jnp.float32), GRID_W)
    t_col = jnp.tile(jnp.arange(GRID_W, dtype=jnp.float32), rows)
    n_freq = dh // 4
    inv = ROPE_BASE ** (-jnp.arange(n_freq, dtype=jnp.float32) / n_freq)
    ang = jnp.concatenate([t_row[:, None] * inv, t_col[:, None] * inv], axis=-1)
    ang = ang.reshape((n_tok,) + (1,) * (x.ndim - 3) + (dh // 2,))
    cos, sin = jnp.cos(ang), jnp.sin(ang)
    xf = x.astype(jnp.float32)
    x1, x2 = xf[..., :dh // 2], xf[..., dh // 2:]
    return jnp.concatenate([x1 * cos - x2 * sin, x2 * cos + x1 * sin], axis=-1).astype(x.dtype)


def sink_softmax(s, sink):
    m = jnp.maximum(jnp.max(s, axis=-1, keepdims=True), sink)
    p = jnp.exp(s - m)
    return p / (jnp.sum(p, axis=-1, keepdims=True) + jnp.exp(sink - m))


def mlstm_scan(q, k, v, ig, lf, state):
    bsz, nh, n_tok, dh = q.shape
    nc = n_tok // CHUNK

    def chunks(a):
        return jnp.moveaxis(a.reshape(bsz, nh, nc, CHUNK, *a.shape[3:]), 2, 0)

    tril = jnp.tril(jnp.ones((CHUNK, CHUNK), dtype=bool))

    def step(carry, xs):
        c_mem, n_mem, m_prev = carry
        qc, kc, vc, ic, fc = xs
        b = jnp.cumsum(fc, axis=-1)
        dmat = jnp.where(tril, b[..., :, None] - b[..., None, :] + ic[..., None, :], -jnp.inf)
        inter = b + m_prev[..., None]
        m_t = jnp.maximum(inter, jnp.max(dmat, axis=-1))
        s = jnp.einsum('bhtd,bhsd->bhts', qc, kc) * jnp.exp(dmat - m_t[..., None])
        w_inter = jnp.exp(inter - m_t)
        num = jnp.einsum('bhts,bhsd->bhtd', s, vc) + w_inter[..., None] * jnp.einsum('bhtd,bhde->bhte', qc, c_mem)
        den = jnp.sum(s, axis=-1) + w_inter * jnp.einsum('bhtd,bhd->bht', qc, n_mem)
        h = num / jnp.maximum(jnp.abs(den), jnp.exp(-m_t))[..., None]
        b_last = b[..., -1]
        w_s = b_last[..., None] - b + ic
        m_new = jnp.maximum(b_last + m_prev, jnp.max(w_s, axis=-1))
        decay = jnp.exp(b_last + m_prev - m_new)
        w_k = jnp.exp(w_s - m_new[..., None])
        c_new = decay[..., None, None] * c_mem + jnp.einsum('bhs,bhsd,bhse->bhde', w_k, kc, vc)
        n_new = decay[..., None] * n_mem + jnp.einsum('bhs,bhsd->bhd', w_k, kc)
        return (c_new, n_new, m_new), h

    final, hs = lax.scan(step, state, (chunks(q), chunks(k), chunks(v), chunks(ig), chunks(lf)))
    return jnp.moveaxis(hs, 0, 2).reshape(bsz, nh, n_tok, dh), final


def mlstm_mixer(za, ig_b, fg_b, norm_g, state):
    f32 = jnp.float32
    bsz, n_tok = za.shape[:2]

    def heads(a):
        return a.reshape(bsz, n_tok, H_A, DH_A).transpose(0, 2, 1, 3).astype(f32)

    q = heads(za[..., :D_GRP]) * (DH_A ** -0.5)
    k = heads(za[..., D_GRP:2 * D_GRP])
    v = heads(za[..., 2 * D_GRP:3 * D_GRP])
    o = za[..., 3 * D_GRP:4 * D_GRP]
    gates = za[..., 4 * D_GRP:].astype(f32).reshape(bsz, n_tok, 2, 2, H_A)
    ig = jnp.transpose(gates[:, :, 0] + ig_b.astype(f32), (2, 0, 3, 1))
    lf = jnp.transpose(jax.nn.log_sigmoid(gates[:, :, 1] + fg_b.astype(f32)), (2, 0, 3, 1))
    c0, n0, m0 = (s.astype(f32) for s in state)
    h_f, (cf, nf, mf) = mlstm_scan(q, k, v, ig[0], lf[0], (c0[:, 0], n0[:, 0], m0[:, 0]))
    flip = lambda a: jnp.flip(a, axis=2)
    h_b, (cb, nb, mb) = mlstm_scan(flip(q), flip(k), flip(v), flip(ig[1]), flip(lf[1]),
                                   (c0[:, 1], n0[:, 1], m0[:, 1]))
    h = rms_norm(h_f + flip(h_b), norm_g.reshape(H_A, 1, DH_A))
    h = h.transpose(0, 2, 1, 3).reshape(bsz, n_tok, D_GRP) * jax.nn.sigmoid(o.astype(f32))
    new_state = (jnp.stack([cf, cb], axis=1), jnp.stack([nf, nb], axis=1), jnp.stack([mf, mb], axis=1))
    return h.astype(za.dtype), new_state


def diff_attend(q, k, v, lam):
    bsz, n_tok = q.shape[:2]
    qb = jnp.moveaxis(q.reshape(bsz, n_tok // Q_BLOCK, Q_BLOCK, *q.shape[2:]), 1, 0)
    scale = DK_B ** -0.5

    def blk(qi):
        s = jnp.einsum('bqhmd,bshmd->bhmqs', qi, k).astype(jnp.float32) * scale
        p = jax.nn.softmax(s, axis=-1)
        a = p[:, :, 0] - lam * p[:, :, 1]
        return jnp.einsum('bhqs,bshe->bqhe', a.astype(v.dtype), v)

    out = lax.map(blk, qb)
    return jnp.moveaxis(out, 0, 1).reshape(bsz, n_tok, *v.shape[2:])


def diff_mixer(zb, qn_g, kn_g, lam, out_g, lam_init, ctx_kv):
    bsz, n_tok = zb.shape[:2]
    q = rms_norm(zb[..., :D_GRP].reshape(bsz, n_tok, H_B, 2, DK_B), qn_g)
    k = rms_norm(zb[..., D_GRP:2 * D_GRP].reshape(bsz, n_tok, H_B, 2, DK_B), kn_g)
    v = zb[..., 2 * D_GRP:].reshape(bsz, n_tok, H_B, DV_B)
    if ctx_kv is None:
        keys, vals = k, v
    else:
        q, k = axial_rope(q), axial_rope(k)
        keys = jnp.concatenate([ctx_kv[0].astype(k.dtype), k], axis=1)
        vals = jnp.concatenate([ctx_kv[1].astype(v.dtype), v], axis=1)
    lp = lam.astype(jnp.float32)
    lam_full = jnp.exp(jnp.sum(lp[0] * lp[1])) - jnp.exp(jnp.sum(lp[2] * lp[3])) + lam_init
    out = rms_norm(diff_attend(q, keys, vals, lam_full), out_g) * (1.0 - lam_init)
    return out.reshape(bsz, n_tok, D_GRP), (k, v)


def swa_dense_ctx(q, k, v, sink):
    bsz, n_tok = q.shape[:2]
    qb = jnp.moveaxis(q.reshape(bsz, n_tok // Q_BLOCK, Q_BLOCK, KV_C, G_C, DH_C), 1, 0)
    scale = DH_C ** -0.5

    def blk(qi):
        s = jnp.einsum('bqkgd,bskd->bkgqs', qi, k).astype(jnp.float32) * scale
        p = sink_softmax(s, sink[None, :, :, None, None]).astype(v.dtype)
        return jnp.einsum('bkgqs,bskd->bqkgd', p, v)

    out = lax.map(blk, qb)
    return jnp.moveaxis(out, 0, 1).reshape(bsz, n_tok, D_GRP)


def swa_banded(q, k, v, ck, cv, sink):
    bsz, n_tok = q.shape[:2]
    nb = n_tok // BLOCK

    def band(a):
        ap = jnp.pad(a, ((0, 0), (BLOCK, BLOCK), (0, 0), (0, 0))).reshape(bsz, nb + 2, BLOCK, *a.shape[2:])
        return jnp.concatenate([ap[:, :-2], ap[:, 1:-1], ap[:, 2:]], axis=2)

    kb, vb = band(k), band(v)
    qb = q.reshape(bsz, nb, BLOCK, KV_C, G_C, DH_C)
    scale = DH_C ** -0.5
    s_band = jnp.einsum('bnqkgd,bnskd->bnkgqs', qb, kb).astype(jnp.float32) * scale
    s_ctx = jnp.einsum('bnqkgd,bskd->bnkgqs', qb, ck).astype(jnp.float32) * scale
    start = jnp.arange(nb)[:, None, None] * BLOCK
    qpos = start + jnp.arange(BLOCK)[None, :, None]
    kpos = start - BLOCK + jnp.arange(3 * BLOCK)[None, None, :]
    mask = (jnp.abs(kpos - qpos) <= WINDOW) & (kpos >= 0) & (kpos < n_tok)
    s_band = jnp.where(mask[None, :, None, None], s_band, -jnp.inf)
    p = sink_softmax(jnp.concatenate([s_band, s_ctx], axis=-1),
                     sink[None, None, :, :, None, None]).astype(v.dtype)
    n_band = 3 * BLOCK
    out = (jnp.einsum('bnkgqs,bnskd->bnqkgd', p[..., :n_band], vb)
           + jnp.einsum('bnkgqs,bskd->bnqkgd', p[..., n_band:], cv))
    return out.reshape(bsz, n_tok, D_GRP)


def swa_mixer(zc, qn_g, kn_g, sink, ctx_kv):
    bsz, n_tok = zc.shape[:2]
    q = rms_norm(zc[..., :D_GRP].reshape(bsz, n_tok, KV_C, G_C, DH_C), qn_g)
    k = rms_norm(zc[..., D_GRP:D_GRP + KV_C * DH_C].reshape(bsz, n_tok, KV_C, DH_C), kn_g)
    v = zc[..., D_GRP + KV_C * DH_C:].reshape(bsz, n_tok, KV_C, DH_C)
    sk = sink.astype(jnp.float32).reshape(KV_C, G_C)
    if ctx_kv is None:
        out = swa_dense_ctx(q, k, v, sk)
    else:
        q, k = axial_rope(q), axial_rope(k)
        out = swa_banded(q, k, v, ctx_kv[0].astype(k.dtype), ctx_kv[1].astype(v.dtype), sk)
    return out, (k, v)


def hyena_filters(n_tok, w1, b1, freq, w2, b2, w3, b3):
    f32 = jnp.float32
    t = jnp.linspace(0.0, 1.0, n_tok, dtype=f32)[:, None]
    w = (2.0 * math.pi / n_tok) * jnp.arange(n_tok, dtype=f32)[:, None]
    bands = jnp.linspace(1e-4, HY_BANDS - 1, HY_BANDS, dtype=f32)[None, :]
    feats = jnp.concatenate([t, jnp.cos(bands * w), -jnp.sin(bands * w)], axis=-1)
    fr = freq.astype(f32)
    h = jnp.sin(fr * (feats @ w1.astype(f32) + b1.astype(f32)))
    h = jnp.sin(fr * (h @ w2.astype(f32) + b2.astype(f32)))
    h = h @ w3.astype(f32) + b3.astype(f32)
    rates = jnp.abs(jnp.linspace(HY_FAST, HY_SLOW, D_GRP, dtype=f32))
    window = jnp.exp(-t * rates)
    return h.reshape(n_tok, HY_ORDER, 2, D_GRP) * window[:, None, None, :]


def bidir_long_conv(u, h_fwd, h_bwd, skip):
    n_tok, ch = u.shape[1], u.shape[2]
    g = jnp.concatenate([h_fwd, jnp.zeros((1, ch), h_fwd.dtype), h_bwd[:0:-1]], axis=0)
    uf = u.astype(jnp.float32)
    spec = jnp.fft.rfft(uf, n=2 * n_tok, axis=1) * jnp.fft.rfft(g, n=2 * n_tok, axis=0)[None]
    y = jnp.fft.irfft(spec, n=2 * n_tok, axis=1)[:, :n_tok]
    return (y + uf * skip.astype(jnp.float32)).astype(u.dtype)


def hyena_mixer(zd, conv_w, conv_b, filt, skip):
    u = dwconv3(zd, conv_w, conv_b)
    y, x1, x2 = jnp.split(u, 3, axis=-1)
    for o, gate in enumerate((x1, x2)):
        y = gate * bidir_long_conv(y, filt[:, o, 0], filt[:, o, 1], skip[o])
    return y


def trunk_layer(x, cond, P, l, ctx):
    bsz, n_tok = x.shape[:2]
    mod = (jax.nn.silu(cond) @ P['w_mod'][l] + P['b_mod'][l])[:, None, :]
    sh1, sc1, g1, sh2, sc2, g2 = jnp.split(mod, 6, axis=-1)
    h = rms_norm(x, P['norm1_g'][l]) * (1.0 + sc1) + sh1
    z = h @ P['w_in'][l]
    o1, o2, o3 = N_A, N_A + N_B, N_A + N_B + N_C
    za, zb, zc, zd = z[..., :o1], z[..., o1:o2], z[..., o2:o3], z[..., o3:]
    if ctx is None:
        m_state = (jnp.zeros((bsz, 2, H_A, DH_A, DH_A), jnp.float32),
                   jnp.zeros((bsz, 2, H_A, DH_A), jnp.float32),
                   jnp.zeros((bsz, 2, H_A), jnp.float32))
        diff_ctx, swa_ctx = None, None
    else:
        m_state = (ctx[4], ctx[5], ctx[6])
        diff_ctx, swa_ctx = (ctx[0], ctx[1]), (ctx[2], ctx[3])
    a_out, m_new = mlstm_mixer(za, P['mlstm_ig_b'][l], P['mlstm_fg_b'][l], P['mlstm_norm_g'][l], m_state)
    lam_init = 0.8 - 0.6 * math.exp(-0.3 * l)
    b_out, diff_kv = diff_mixer(zb, P['diff_qn_g'][l], P['diff_kn_g'][l], P['diff_lam'][l],
                                P['diff_out_g'][l], lam_init, diff_ctx)
    c_out, swa_kv = swa_mixer(zc, P['swa_qn_g'][l], P['swa_kn_g'][l], P['swa_sink'][l], swa_ctx)
    filt = hyena_filters(n_tok, P['hy_w1'][l], P['hy_b1'][l], P['hy_freq'][l], P['hy_w2'][l],
                         P['hy_b2'][l], P['hy_w3'][l], P['hy_b3'][l])
    d_out = hyena_mixer(zd, P['hy_conv_w'][l], P['hy_conv_b'][l], filt, P['hy_bias'][l])
    mix = jnp.concatenate([a_out, b_out, c_out, d_out], axis=-1) @ P['w_out'][l]
    x = x + g1 * mix
    h = rms_norm(x, P['norm2_g'][l]) * (1.0 + sc2) + sh2
    u = dwconv3(h @ P['ffn_w_up'][l], P['ffn_conv_w'][l], P['ffn_conv_b'][l])
    ua, ub = jnp.split(u, 2, axis=-1)
    x = x + g2 * ((jax.nn.silu(ua) * ub) @ P['ffn_w_down'][l])
    return x, (diff_kv[0], diff_kv[1], swa_kv[0], swa_kv[1], m_new[0], m_new[1], m_new[2])


def setup_inputs(seed: int = 0) -> dict:
    key = jax.random.key(seed)
    ks = iter(jax.random.split(key, 48))
    f32 = jnp.float32
    nrm = lambda shape, s: jax.random.normal(next(ks), shape, f32) * s
    gain = lambda shape: 1.0 + jax.random.normal(next(ks), shape, f32) * 0.02
    return {
        'x_prompt': nrm((BATCH, SEQ, D_MODEL), 1.0),
        'x_sample': nrm((DEC_BATCH, DEC_SEQ, D_MODEL), 1.0),
        'cache_diff_k': nrm((DEC_BATCH, DEPTH, PAST_LEN, H_B, 2, DK_B), 1.0),
        'cache_diff_v': nrm((DEC_BATCH, DEPTH, PAST_LEN, H_B, DV_B), 1.0),
        'cache_swa_k': nrm((DEC_BATCH, DEPTH, PAST_LEN, KV_C, DH_C), 1.0),
        'cache_swa_v': nrm((DEC_BATCH, DEPTH, PAST_LEN, KV_C, DH_C), 1.0),
        'state_mlstm_C': nrm((DEC_BATCH, DEPTH, 2, H_A, DH_A, DH_A), 0.5),
        'state_mlstm_n': nrm((DEC_BATCH, DEPTH, 2, H_A, DH_A), 0.5),
        'state_mlstm_m': 1.0 + nrm((DEC_BATCH, DEPTH, 2, H_A), 1.0),
        'c': nrm((DEC_BATCH, D_MODEL), 1.0),
        'c_ctx': nrm((D_MODEL,), 1.0),
        'w_mod': nrm((DEPTH, D_MODEL, 6 * D_MODEL), D_MODEL ** -0.5),
        'b_mod': nrm((DEPTH, 6 * D_MODEL), 0.02),
        'norm1_g': gain((DEPTH, D_MODEL)),
        'norm2_g': gain((DEPTH, D_MODEL)),
        'w_in': nrm((DEPTH, D_MODEL, N_IN), D_MODEL ** -0.5),
        'mlstm_ig_b': nrm((DEPTH, 2, H_A), 0.1),
        'mlstm_fg_b': jax.random.uniform(next(ks), (DEPTH, 2, H_A), f32, 3.0, 6.0),
        'mlstm_norm_g': gain((DEPTH, D_GRP)),
        'diff_qn_g': gain((DEPTH, DK_B)),
        'diff_kn_g': gain((DEPTH, DK_B)),
        'diff_lam': nrm((DEPTH, 4, DK_B), 0.1),
        'diff_out_g': gain((DEPTH, DV_B)),
        'swa_qn_g': gain((DEPTH, DH_C)),
        'swa_kn_g': gain((DEPTH, DH_C)),
        'swa_sink': nrm((DEPTH, H_C), 0.5),
        'hy_conv_w': nrm((DEPTH, 3, 3 * D_GRP), 3 ** -0.5),
        'hy_conv_b': nrm((DEPTH, 3 * D_GRP), 0.02),
        'hy_w1': nrm((DEPTH, HY_POS_DIM, HY_FF), HY_POS_DIM ** -0.5),
        'hy_b1': nrm((DEPTH, HY_FF), 0.02),
        'hy_freq': 1.0 + nrm((DEPTH, HY_FF), 0.1),
        'hy_w2': nrm((DEPTH, HY_FF, HY_FF), HY_FF ** -0.5),
        'hy_b2': nrm((DEPTH, HY_FF), 0.02),
        'hy_w3': nrm((DEPTH, HY_FF, HY_ORDER * 2 * D_GRP), 0.02),
        'hy_b3': nrm((DEPTH, HY_ORDER * 2 * D_GRP), 0.002),
        'hy_bias': nrm((DEPTH, HY_ORDER, D_GRP), 0.5),
        'w_out': nrm((DEPTH, D_MIX, D_MODEL), D_MIX ** -0.5),
        'ffn_w_up': nrm((DEPTH, D_MODEL, 2 * D_FF), D_MODEL ** -0.5),
        'ffn_conv_w': nrm((DEPTH, 3, 2 * D_FF), 3 ** -0.5),
        'ffn_conv_b': nrm((DEPTH, 2 * D_FF), 0.02),
        'ffn_w_down': nrm((DEPTH, D_FF, D_MODEL), D_FF ** -0.5),
    }


def reference(x_prompt, x_sample, cache_diff_k, cache_diff_v, cache_swa_k, cache_swa_v,
              state_mlstm_C, state_mlstm_n, state_mlstm_m, c, c_ctx,
              w_mod, b_mod, norm1_g, norm2_g, w_in, mlstm_ig_b, mlstm_fg_b, mlstm_norm_g,
              diff_qn_g, diff_kn_g, diff_lam, diff_out_g, swa_qn_g, swa_kn_g, swa_sink,
              hy_conv_w, hy_conv_b, hy_w1, hy_b1, hy_freq, hy_w2, hy_b2, hy_w3, hy_b3, hy_bias,
              w_out, ffn_w_up, ffn_conv_w, ffn_conv_b, ffn_w_down):
    P = {
        'w_mod': w_mod, 'b_mod': b_mod, 'norm1_g': norm1_g, 'norm2_g': norm2_g, 'w_in': w_in,
        'mlstm_ig_b': mlstm_ig_b, 'mlstm_fg_b': mlstm_fg_b, 'mlstm_norm_g': mlstm_norm_g,
        'diff_qn_g': diff_qn_g, 'diff_kn_g': diff_kn_g, 'diff_lam': diff_lam, 'diff_out_g': diff_out_g,
        'swa_qn_g': swa_qn_g, 'swa_kn_g': swa_kn_g, 'swa_sink': swa_sink,
        'hy_conv_w': hy_conv_w, 'hy_conv_b': hy_conv_b, 'hy_w1': hy_w1, 'hy_b1': hy_b1,
        'hy_freq': hy_freq, 'hy_w2': hy_w2, 'hy_b2': hy_b2, 'hy_w3': hy_w3, 'hy_b3': hy_b3,
        'hy_bias': hy_bias, 'w_out': w_out, 'ffn_w_up': ffn_w_up, 'ffn_conv_w': ffn_conv_w,
        'ffn_conv_b': ffn_conv_b, 'ffn_w_down': ffn_w_down,
    }
    y_prompt = x_prompt
    ctx_out = []
    for l in range(DEPTH):
        y_prompt, st = trunk_layer(y_prompt, c_ctx[None, :], P, l, None)
        ctx_out.append(st)
    new_diff_k = jnp.stack([s[0] for s in ctx_out], axis=1)
    new_diff_v = jnp.stack([s[1] for s in ctx_out], axis=1)
    new_swa_k = jnp.stack([s[2] for s in ctx_out], axis=1)
    new_swa_v = jnp.stack([s[3] for s in ctx_out], axis=1)
    new_mlstm_C = jnp.stack([s[4] for s in ctx_out], axis=1)
    new_mlstm_n = jnp.stack([s[5] for s in ctx_out], axis=1)
    new_mlstm_m = jnp.stack([s[6] for s in ctx_out], axis=1)
    y_sample = x_sample
    for l in range(DEPTH):
        ctx = (cache_diff_k[:, l], cache_diff_v[:, l], cache_swa_k[:, l], cache_swa_v[:, l],
               state_mlstm_C[:, l], state_mlstm_n[:, l], state_mlstm_m[:, l])
        y_sample, _ = trunk_layer(y_sample, c, P, l, ctx)
    return (y_prompt, y_sample, new_diff_k, new_diff_v, new_swa_k, new_swa_v, new_mlstm_C, new_mlstm_n, new_mlstm_m)
```

```python
import math
import numpy as np
import concourse.bass as bass
import concourse.mybir as mybir
from concourse.bass_utils import run_bass_kernel_spmd

F32 = mybir.dt.float32
BF16 = mybir.dt.bfloat16
I32 = mybir.dt.int32
AF = mybir.ActivationFunctionType
ALU = mybir.AluOpType
AX = mybir.AxisListType

D = 2048
DEPTH = 2
NT = 1536
TB = 512
SEGS = [(0, 256, False), (256, 256, False), (512, 1024, True)]
N_IN = 6160
D_FF = 5632
MIXSEL = "ABCD"
EPS = 1e-6
PAST = 512
HY_FAST = math.log(1e-2) / 0.3
HY_SLOW = math.log(1e-2) / 1.5


class _Rec:
    def __init__(self):
        self.call = None

    def __getattr__(self, name):
        def f(*a, **kw):
            self.call = (name, a, kw)
            return self
        return f


class KB:
    def __init__(self, nc):
        self.nc = nc
        self.eng = {"pe": nc.tensor, "act": nc.scalar, "dve": nc.vector, "pool": nc.gpsimd, "sp": nc.sync}
        self.csem = {e: nc.alloc_semaphore("cs_" + e) for e in ("pe", "act", "dve", "pool")}
        self.ccnt = {e: 0 for e in self.csem}
        self.dsem = {q: [[nc.alloc_semaphore(f"ds_{q}{i}"), 0] for i in range(n)] for q, n in (("sp", 12), ("pool", 8))}
        self.dnext = {"sp": 0, "pool": 0}
        self.seen = {e: {} for e in self.eng}
        self.lastw = {}
        self.readers = {}
        self.nsb = 0
        self.sems = {}
        self.rec = None
        self.aux = []
        self.auxtag = []
        self.tag = None

    def sb(self, shape, dt, name=None):
        self.nsb += 1
        return self.nc.alloc_sbuf_tensor(name or f"sb{self.nsb}", list(shape), dt).ap()

    def _need(self, stream, tk):
        if tk is None:
            return
        sem, val = tk
        key = id(sem)
        self.sems[key] = sem
        if self.seen[stream].get(key, 0) >= val:
            return
        self.eng[stream].wait_ge(sem, val)
        self.seen[stream][key] = val

    def _deps(self, stream, reads, writes, pe_skip=False):
        for k in reads:
            tk = self.lastw.get(k)
            if tk is not None and not (pe_skip and tk[0] is self.csem["pe"]):
                self._need(stream, tk)
        for k in writes:
            tk = self.lastw.get(k)
            if tk is not None and not (pe_skip and tk[0] is self.csem["pe"]):
                self._need(stream, tk)
            for tk in self.readers.get(k, {}).values():
                if not (pe_skip and tk[0] is self.csem["pe"]):
                    self._need(stream, tk)

    def _record(self, tk, reads, writes):
        for k in reads:
            d = self.readers.setdefault(k, {})
            old = d.get(id(tk[0]))
            if old is None or old[1] < tk[1]:
                d[id(tk[0])] = tk
        for k in writes:
            self.lastw[k] = tk
            self.readers[k] = {}

    def op(self, e, fn, reads=(), writes=()):
        if self.rec is not None:
            r = _Rec()
            fn(r)
            self.rec.append(("op", e, r.call, list(reads), list(writes)))
            return None
        self._deps(e, reads, writes, pe_skip=(e == "pe"))
        ins = fn(self.eng[e])
        self.ccnt[e] += 1
        ins.then_inc(self.csem[e], 1)
        tk = (self.csem[e], self.ccnt[e])
        self._record(tk, reads, writes)
        return tk

    def dma(self, q, out, in_, reads=(), writes=(), **kw):
        if self.rec is not None:
            self.rec.append(("dma", q, out, in_, list(reads), list(writes), kw))
            return None
        self._deps(q, reads, writes)
        i = self.dnext[q]
        self.dnext[q] = (i + 1) % len(self.dsem[q])
        slot = self.dsem[q][i]
        if slot[1] > 0:
            self._need(q, (slot[0], slot[1]))
        ins = self.eng[q].dma_start(out=out, in_=in_, **kw)
        slot[1] += 16
        ins.then_inc(slot[0], 16)
        tk = (slot[0], slot[1])
        self._record(tk, reads, writes)
        return tk

    def record(self):
        self.rec = []

    def end_atom(self):
        if self.rec is None:
            return
        if self.rec:
            self.aux.append(self.rec)
            self.auxtag.append(self.tag)
        self.rec = []

    def stop_record(self):
        self.end_atom()
        self.rec = None

    def tick(self, n=1):
        if self.rec is not None:
            return
        for _ in range(n):
            if not self.aux:
                return
            atom = self.aux.pop(0)
            if self.auxtag:
                self.auxtag.pop(0)
            for it in atom:
                if it[0] == "op":
                    _, e, (name, a, kw), reads, writes = it
                    self.op(e, lambda eng: getattr(eng, name)(*a, **kw), reads, writes)
                else:
                    _, q, out, in_, reads, writes, kw = it
                    self.dma(q, out, in_, reads, writes, **kw)

    def flush(self):
        assert self.rec is None
        while self.aux:
            self.tick()

    def flush_tags(self, tags):
        assert self.rec is None
        while self.aux and self.auxtag and self.auxtag[0] in tags:
            self.tick()

    def fence(self):
        assert self.rec is None
        tks = [(self.csem[e], self.ccnt[e]) for e in self.csem if self.ccnt[e] > 0]
        for q in self.dsem:
            tks += [(sem, val) for sem, val in self.dsem[q] if val > 0]
        for st in self.eng:
            for tk in tks:
                self._need(st, tk)

    def finish(self):
        for q in self.dsem:
            for sem, val in self.dsem[q]:
                if val > 0:
                    self._need("sp", (sem, val))


def build(debug=False):
    nc = bass.Bass("TRN2", target_bir_lowering=False)
    K = KB(nc)
    dbg = {}

    def din(name, shape):
        return nc.dram_tensor(name, list(shape), F32, kind="ExternalInput").ap()

    def dout(name, shape):
        return nc.dram_tensor(name, list(shape), F32, kind="ExternalOutput").ap()

    def dscr(name, shape, dt=F32):
        if debug:
            return nc.dram_tensor(name, list(shape), dt, kind="ExternalOutput").ap()
        return nc.dram_tensor(name, list(shape), dt).ap()

    xin = din("xin", [NT, D])
    cond2 = din("cond2", [2, D])
    c_dk = din("c_dk", [DEPTH, PAST, 512])
    c_dv = din("c_dv", [DEPTH, PAST, 512])
    c_sk = din("c_sk", [DEPTH, PAST, 256])
    c_sv = din("c_sv", [DEPTH, PAST, 256])
    st_C = din("st_C", [DEPTH, 2, 4, 128, 128])
    st_n = din("st_n", [DEPTH, 2, 4, 128])
    st_m = din("st_m", [DEPTH, 8])
    w_mod = din("w_mod", [DEPTH, D, 6 * D])
    b_mod = din("b_mod", [DEPTH, 6 * D])
    norm1_g = din("norm1_g", [DEPTH, D])
    norm2_g = din("norm2_g", [DEPTH, D])
    w_in = din("w_in", [DEPTH, D, N_IN])
    ig_b = din("mlstm_ig_b", [DEPTH, 8])
    fg_b = din("mlstm_fg_b", [DEPTH, 8])
    mnorm_g = din("mlstm_norm_g", [DEPTH, 512])
    dqn_g = din("diff_qn_g", [DEPTH, 64])
    dkn_g = din("diff_kn_g", [DEPTH, 64])
    dlam = din("diff_lam", [DEPTH, 4, 64])
    dout_g = din("diff_out_g", [DEPTH, 128])
    sqn_g = din("swa_qn_g", [DEPTH, 128])
    skn_g = din("swa_kn_g", [DEPTH, 128])
    ssink = din("swa_sink", [DEPTH, 4])
    hy_cw = din("hy_conv_w", [DEPTH, 3, 1536])
    hy_cb = din("hy_conv_b", [DEPTH, 1536])
    hy_w1 = din("hy_w1", [DEPTH, 33, 64])
    hy_b1 = din("hy_b1", [DEPTH, 64])
    hy_fr = din("hy_freq", [DEPTH, 64])
    hy_w2 = din("hy_w2", [DEPTH, 64, 64])
    hy_b2 = din("hy_b2", [DEPTH, 64])
    hy_w3 = din("hy_w3", [DEPTH, 64, 2048])
    hy_b3 = din("hy_b3", [DEPTH, 2048])
    hy_bias = din("hy_bias", [DEPTH, 2, 512])
    w_out = din("w_out", [DEPTH, D, D])
    w_up = din("ffn_w_up", [DEPTH, D, 2 * D_FF])
    f_cw = din("ffn_conv_w", [DEPTH, 3, 2 * D_FF])
    f_cb = din("ffn_conv_b", [DEPTH, 2 * D_FF])
    w_dn = din("ffn_w_down", [DEPTH, D_FF, D])
    k_ident = din("k_ident", [128, 128])
    k_sel = din("k_sel", [8, 8 * 128])
    k_tri = din("k_tri", [128, 256])
    k_band = din("k_band", [128, 384])
    k_rope64 = din("k_rope64", [1024, 64])
    k_rope128 = din("k_rope128", [1024, 128])
    k_feat = {L: din(f"k_feat{L}", [34, L]) for L in (256, 1024)}
    k_win = {L: din(f"k_win{L}", [L, 512]) for L in (256, 1024)}
    k_dft = {L: din(f"k_dft{L}", [4, L, L]) for L in (256, 1024)}

    y_out = dout("y_out", [NT, D])
    o_dk = dout("o_dk", [2, DEPTH, 256, 512])
    o_dv = dout("o_dv", [2, DEPTH, 256, 512])
    o_sk = dout("o_sk", [2, DEPTH, 256, 256])
    o_sv = dout("o_sv", [2, DEPTH, 256, 256])
    o_C = dout("o_C", [2, DEPTH, 2, 4, 128, 128])
    o_n = dout("o_n", [2, DEPTH, 2, 4, 128])
    o_m = dout("o_m", [2, DEPTH, 8])

    xs = dscr("xs", [D, NT])
    zAT = dscr("zAT", [1552, NT])
    zDT = dscr("zDT", [1536, NT])
    ztm = dscr("ztm", [NT, 3584])
    ZC = dict(Av=0, Ak=512, Bq=1024, Bk=1536, Bv=2048, Cq=2560, Ck=3072, Cv=3328)
    hyG = {L: dscr(f"hyG{L}", [2, 2, L, 512]) for L in (256, 1024)}
    mixd = nc.dram_tensor("mixd", [D, NT], BF16).ap()

    ident = K.sb([128, 128], F32, "ident")
    onesb = K.sb([128, 128], BF16, "onesb")
    sel = K.sb([4, 4 * 128], F32, "sel")
    tri = K.sb([128, 256], BF16, "tri")
    band = K.sb([128, 384], BF16, "band")
    hT = K.sb([128, 16, NT], BF16, "hT")
    PS = [nc.alloc_psum_tensor(f"ps{i}", [128, 512], F32).ap() for i in range(8)]
    psk = [("ps", i) for i in range(8)]
    psi = [0]

    pslo = [False, 0]

    def nps():
        if pslo[0]:
            pslo[1] = (pslo[1] + 1) % 4
            return pslo[1]
        i = psi[0]
        psi[0] = (i + 1) % 8
        return i

    K.dma("sp", ident, k_ident, writes=["ident"])
    K.dma("sp", sel, k_sel[0:4, 0:512], writes=["sel"])
    K.dma("pool", tri, k_tri, writes=["tri"])
    K.dma("pool", band, k_band, writes=["band"])
    K.op("dve", lambda e: e.memset(onesb, 1.0), writes=["onesb"])

    cpy_rr = [0]

    def evac(out, in_, reads, writes):
        cpy_rr[0] ^= 1
        if cpy_rr[0]:
            return K.op("act", lambda e: e.activation(out=out, in_=in_, func=AF.Copy), reads=reads, writes=writes)
        return K.op("dve", lambda e: e.tensor_copy(out=out, in_=in_), reads=reads, writes=writes)

    mst = [K.sb([128, TB], BF16, f"mst{i}") for i in range(4)]
    mrr = [0]

    def mix_out(chunk, col0, n, emit):
        i = mrr[0]
        mrr[0] = (i + 1) % 4
        emit(mst[i][:, 0:n], f"mst{i}")
        K.dma("sp", mixd[chunk * 128:(chunk + 1) * 128, col0:col0 + n], mst[i][:, 0:n], reads=[f"mst{i}"], writes=["mixd"])

    def evac_act(out, in_, reads, writes):
        return K.op("act", lambda e: e.activation(out=out, in_=in_, func=AF.Copy), reads=reads, writes=writes)

    vtmp = K.sb([128, 128], F32, "vtmp")

    def load_colT(dst, src2d, nrows, key):
        K.dma("sp", vtmp[0:nrows, :], src2d, writes=["vtmp"])
        b = nps()
        K.op("pe", lambda e: e.transpose(PS[b][:, 0:nrows], vtmp[0:nrows, :], ident[0:nrows, 0:nrows]),
             reads=["vtmp", "ident"], writes=[psk[b]])
        K.op("dve", lambda e: e.tensor_copy(out=dst, in_=PS[b][:, 0:nrows]), reads=[psk[b]], writes=[key])

    arena = K.sb([128, 24576], F32, "arena")
    xres = arena.rearrange("p (c t) -> p c t", c=16)
    xblk = arena[:, 0:8192].rearrange("p (c t) -> p c t", c=16)
    xt = [arena[:, 0:2048], arena[:, 2048:4096]]
    arena2 = K.sb([128, 4608], F32, "arena2")
    wdn = arena2[:, 0:2048].bitcast(BF16).rearrange("p (k n) -> p k n", k=2)
    actTg = arena2[:, 2048:3584].bitcast(BF16).rearrange("p (k t) -> p k t", k=2)
    ubuf = [arena2[:, 3584:4096], arena2[:, 4096:4608]]
    ytile = arena2[:, 0:2048]
    stage = [arena2[:, i * 512:(i + 1) * 512] for i in range(4)]
    xo = [stage[2].rearrange("p (a b) -> p a b", a=4), stage[3].rearrange("p (a b) -> p a b", a=4)]
    for ti in range(NT // 128):
        a = ti % 2
        K.dma("sp", xt[a], xin[ti * 128:(ti + 1) * 128, :], writes=[f"xt{a}"])
        for g in range(4):
            b = nps()
            for j in range(4):
                fc = g * 4 + j
                K.op("pe", lambda e: e.transpose(PS[b][:, j * 128:(j + 1) * 128], xt[a][:, fc * 128:(fc + 1) * 128], ident),
                     reads=[f"xt{a}", "ident"], writes=[psk[b]])
            o = (ti * 4 + g) % 2
            evac(xo[o].rearrange("p a b -> p (a b)"), PS[b], [psk[b]], [f"stage{2 + o}"])
            K.dma("sp", xs[g * 512:(g + 1) * 512, ti * 128:(ti + 1) * 128].rearrange("(j p) t -> p j t", p=128),
                  xo[o], reads=[f"stage{2 + o}"], writes=["xs"])

    K.fence()
    wbuf = [K.sb([128, 16, 512], BF16, f"wbuf{i}") for i in range(2)]
    wrr = [0]
    srr = [0]
    modTs = [K.sb([128, 96, 2], F32, f"modT{i}") for i in range(DEPTH)]
    bmTs = [K.sb([128, 96], F32, f"bmT{i}") for i in range(DEPTH)]
    n1Ts = [K.sb([128, 16], F32, f"n1T{i}") for i in range(DEPTH)]
    n2Ts = [K.sb([128, 16], F32, f"n2T{i}") for i in range(DEPTH)]
    scf = K.sb([128, 32], F32, "scf")
    scb = K.sb([128, 32], BF16, "scb")
    nscales = [K.sb([128, 2, 16, 2], F32, f"nscale{i}") for i in range(DEPTH)]
    rstd = arena2[:, 2048:2560]
    tmpf = arena2[:, 2560:3072]
    sqb = arena2[:, 3072:3328].bitcast(BF16)

    load_colT(scf, cond2.rearrange("j (kc p) -> (j kc) p", p=128), 32, "scf")
    K.op("act", lambda e: e.activation(out=scb, in_=scf, func=AF.Silu), reads=["scf"], writes=["scb"])

    def load_w(src_fn, ncols):
        i = wrr[0]
        wrr[0] ^= 1
        K.dma("pool", wbuf[i][:, :, 0:ncols], src_fn, writes=[f"wbuf{i}"])
        return i

    def norm_block(l, which, tb, src_is_sbuf):
        t0 = tb * TB
        cj = 0 if tb == 0 else 1
        if not src_is_sbuf:
            K.dma("sp", xblk, xs[:, t0:t0 + TB].rearrange("(c p) t -> p c t", p=128), reads=["xs"], writes=["xblk"])
        b = nps()
        for c in range(16):
            xsrc = xres[:, c, t0:t0 + TB] if src_is_sbuf else xblk[:, c, :]
            K.op("act", lambda e: e.activation(out=sqb, in_=xsrc, func=AF.Square), reads=["xblk"], writes=["sqb"])
            K.op("pe", lambda e: e.matmul(PS[b], lhsT=onesb, rhs=sqb, start=(c == 0), stop=(c == 15)),
                 reads=["sqb", "onesb"], writes=[psk[b]])
        K.op("act", lambda e: e.activation(out=rstd, in_=PS[b], func=AF.Sqrt, scale=1.0 / D, bias=epsT[:, 0:1]),
             reads=[psk[b], "epsT"], writes=["rstd"])
        K.op("dve", lambda e: e.reciprocal(out=rstd, in_=rstd), reads=["rstd"], writes=["rstd"])
        sh = 0 if which == 0 else 3
        return t0, cj, sh

    fcw = K.sb([128, 264], F32, "fcw")
    fcb = K.sb([128, 88], F32, "fcb")
    if debug:
        dbg_mix = [dout(f"dbg_mix{i}", [D, NT]) for i in range(DEPTH)]
        dbg_h2 = [dout(f"dbg_h2{i}", [D, NT]) for i in range(DEPTH)]
    epsT = K.sb([128, 1], F32, "epsT")
    K.op("dve", lambda e: e.memset(epsT, EPS), writes=["epsT"])

    def aw(off, n):
        return arena[:, off:off + n]

    def awb(off, n):
        return arena[:, off:off + n].bitcast(BF16)

    lam_t = arena[:, 24320:24576]
    lam_s = K.sb([128, 4], F32, "lam_s")
    og_s = K.sb([128, 2], F32, "og_s")
    esink = K.sb([128, 4], F32, "esink")
    mng = K.sb([128, 4], F32, "mng")

    def attn_core(l, t0, T, is_s, which):
        nq_groups = 16 if which == "B" else 6
        gsz = 64 if which == "B" else 128
        qk_cols = 1024 if which == "B" else 768
        zq = ZC["Bq"] if which == "B" else ZC["Cq"]
        zv = ZC["Bv"] if which == "B" else ZC["Cv"]
        vw = 512 if which == "B" else 256
        nh_k = 4 if which == "B" else 2
        qkraw_ = [aw(0, 1024), aw(14848, 1024)]; qkn_ = [aw(1024, 1024), aw(15872, 1024)]
        sq_ = [aw(2048, 1024), aw(16896, 1024)]; tmp2_ = [aw(3072, 1024), aw(17920, 1024)]
        qT = awb(4096, 2048).rearrange("p (h t) -> p h t", h=4)
        kT = awb(6144, 3072).rearrange("p (h t) -> p h t", h=4)
        vv = awb(9216, 3072).rearrange("p (c n) -> p c n", c=12)
        ebuf = [awb(12288, 256), awb(12544, 256)]
        rden = aw(12800, 512); acc = aw(13312, 512); o1 = aw(13824, 512)
        ss = aw(14336, 16); ropet = aw(14352, 128); gq = aw(14592, 128); gk = aw(14720, 128)
        zk_qk = ["z2064", "z2576"] if which == "B" else ["z3600", "z4112"]
        zk_v = ["z3088"] if which == "B" else ["z4112"]
        nctx = 4 if is_s else 0
        nk = nctx + T // 128
        ntc = T // 128
        gains = (dqn_g, dkn_g) if which == "B" else (sqn_g, skn_g)
        K.dma("sp", gq[:, 0:gsz], gains[0][l:l + 1, :].partition_broadcast(128), writes=["gq"])
        K.dma("sp", gk[:, 0:gsz], gains[1][l:l + 1, :].partition_broadcast(128), writes=["gk"])
        ckt = (c_dk if which == "B" else c_sk)
        cvt = (c_dv if which == "B" else c_sv)
        for j in range(nctx):
            K.dma("pool", vv[:, j, 0:vw], cvt[l, j * 128:(j + 1) * 128, :], writes=["vv"])
        for j in range(ntc):
            K.dma("pool", vv[:, nctx + j, 0:vw], ztm[t0 + j * 128:t0 + (j + 1) * 128, zv:zv + vw], reads=zk_v, writes=["vv"])
        for j in range(nctx):
            qkraw = qkraw_[j % 2]
            kq = f"qkraw{j % 2}"
            K.dma("sp", qkraw[:, 0:vw], ckt[l, j * 128:(j + 1) * 128, :], writes=[kq])
            b = nps()
            for h in range(nh_k):
                K.op("pe", lambda e: e.transpose(PS[b][:, h * 128:(h + 1) * 128], qkraw[:, h * 128:(h + 1) * 128], ident),
                     reads=[kq, "ident"], writes=[psk[b]])
            evac_act(kT[:, 0:nh_k, j * 128:(j + 1) * 128], PS[b][:, 0:nh_k * 128].rearrange("p (h t) -> p h t", h=nh_k), [psk[b]], ["kT"])
        nqg = 8 if which == "B" else 4
        for tc in range(ntc):
            r0 = t0 + tc * 128
            pp = tc % 2
            qkraw, qkn, sq, tmp2 = qkraw_[pp], qkn_[pp], sq_[pp], tmp2_[pp]
            kq, kn_, ksq, kt2 = f"qkraw{pp}", f"qkn{pp}", f"sq{pp}", f"tmp2{pp}"
            K.dma("sp", qkraw[:, 0:qk_cols], ztm[r0:r0 + 128, zq:zq + qk_cols], reads=zk_qk, writes=[kq])
            K.op("dve", lambda e: e.tensor_tensor(out=sq[:, 0:qk_cols], in0=qkraw[:, 0:qk_cols], in1=qkraw[:, 0:qk_cols], op=ALU.mult),
                 reads=[kq], writes=[ksq])
            K.op("dve", lambda e: e.reduce_sum(out=ss[:, 0:nq_groups], in_=sq[:, 0:qk_cols].rearrange("p (g d) -> p g d", d=gsz), axis=AX.X),
                 reads=[ksq], writes=["ss"])
            K.op("act", lambda e: e.activation(out=ss[:, 0:nq_groups], in_=ss[:, 0:nq_groups], func=AF.Sqrt, scale=1.0 / gsz, bias=epsT[:, 0:1]),
                 reads=["ss", "epsT"], writes=["ss"])
            K.op("dve", lambda e: e.reciprocal(out=ss[:, 0:nq_groups], in_=ss[:, 0:nq_groups]), reads=["ss"], writes=["ss"])
            K.op("dve", lambda e: e.tensor_tensor(out=qkn[:, 0:qk_cols].rearrange("p (g d) -> p g d", d=gsz),
                                                  in0=qkraw[:, 0:qk_cols].rearrange("p (g d) -> p g d", d=gsz),
                                                  in1=ss[:, 0:nq_groups].unsqueeze(2).to_broadcast([128, nq_groups, gsz]), op=ALU.mult),
                 reads=[kq, "ss"], writes=[kn_])
            qv = qkn[:, 0:nqg * gsz].rearrange("p (g d) -> p g d", d=gsz)
            kv_ = qkn[:, nqg * gsz:qk_cols].rearrange("p (g d) -> p g d", d=gsz)
            K.op("dve", lambda e: e.tensor_tensor(out=qv, in0=qv, in1=gq[:, 0:gsz].unsqueeze(1).to_broadcast([128, nqg, gsz]), op=ALU.mult),
                 reads=[kn_, "gq"], writes=[kn_])
            K.op("dve", lambda e: e.tensor_tensor(out=kv_, in0=kv_, in1=gk[:, 0:gsz].unsqueeze(1).to_broadcast([128, nq_groups - nqg, gsz]), op=ALU.mult),
                 reads=[kn_, "gk"], writes=[kn_])
            if not is_s:
                seq = 0 if t0 == 0 else 1
                odst = (o_dk if which == "B" else o_sk)
                K.dma("sp", odst[seq, l, tc * 128:(tc + 1) * 128, :], qkn[:, nqg * gsz:qk_cols], reads=[kn_], writes=["okv"])
                src = qkn
            else:
                hs = gsz // 2
                rt = k_rope64 if which == "B" else k_rope128
                K.dma("sp", ropet[:, 0:gsz], rt[tc * 128:(tc + 1) * 128, :], writes=["ropet"])
                x1 = qkn[:, 0:qk_cols].rearrange("p (g d) -> p g d", d=gsz)[:, :, 0:hs]
                x2 = qkn[:, 0:qk_cols].rearrange("p (g d) -> p g d", d=gsz)[:, :, hs:gsz]
                o1v = tmp2[:, 0:qk_cols].rearrange("p (g d) -> p g d", d=gsz)[:, :, 0:hs]
                o2v = tmp2[:, 0:qk_cols].rearrange("p (g d) -> p g d", d=gsz)[:, :, hs:gsz]
                s1 = sq[:, 0:qk_cols].rearrange("p (g d) -> p g d", d=gsz)[:, :, 0:hs]
                s2 = sq[:, 0:qk_cols].rearrange("p (g d) -> p g d", d=gsz)[:, :, hs:gsz]
                cosb = ropet[:, 0:hs].unsqueeze(1).to_broadcast([128, nq_groups, hs])
                sinb = ropet[:, hs:gsz].unsqueeze(1).to_broadcast([128, nq_groups, hs])
                K.op("dve", lambda e: e.tensor_tensor(out=o1v, in0=x1, in1=cosb, op=ALU.mult), reads=[kn_, "ropet"], writes=[kt2])
                K.op("dve", lambda e: e.tensor_tensor(out=s1, in0=x2, in1=sinb, op=ALU.mult), reads=[kn_, "ropet"], writes=[ksq])
                K.op("dve", lambda e: e.tensor_tensor(out=o1v, in0=o1v, in1=s1, op=ALU.subtract), reads=[kt2, ksq], writes=[kt2])
                K.op("dve", lambda e: e.tensor_tensor(out=o2v, in0=x2, in1=cosb, op=ALU.mult), reads=[kn_, "ropet"], writes=[kt2])
                K.op("dve", lambda e: e.tensor_tensor(out=s2, in0=x1, in1=sinb, op=ALU.mult), reads=[kn_, "ropet"], writes=[ksq])
                K.op("dve", lambda e: e.tensor_tensor(out=o2v, in0=o2v, in1=s2, op=ALU.add), reads=[kt2, ksq], writes=[kt2])
                src = tmp2
            srck = kn_ if src is qkn else kt2
            b = nps()
            for h in range(4):
                K.op("pe", lambda e: e.transpose(PS[b][:, h * 128:(h + 1) * 128], src[:, h * 128:(h + 1) * 128], ident),
                     reads=[srck, "ident"], writes=[psk[b]])
            evac_act(qT[:, :, tc * 128:(tc + 1) * 128], PS[b].rearrange("p (h t) -> p h t", h=4), [psk[b]], ["qT"])
            b = nps()
            for h in range(nh_k):
                K.op("pe", lambda e: e.transpose(PS[b][:, h * 128:(h + 1) * 128], src[:, 512 + h * 128:512 + (h + 1) * 128], ident),
                     reads=[srck, "ident"], writes=[psk[b]])
            kc0 = nctx * 128 + tc * 128
            evac_act(kT[:, 0:nh_k, kc0:kc0 + 128], PS[b][:, 0:nh_k * 128].rearrange("p (h t) -> p h t", h=nh_k), [psk[b]], ["kT"])
        if not is_s:
            seq = 0 if t0 == 0 else 1
            odst = (o_dv if which == "B" else o_sv)
            K.dma("sp", odst[seq, l, :, :], ztm[t0:t0 + T, zv:zv + vw], reads=zk_v, writes=["okvv"])
        scale = (64 ** -0.5) if which == "B" else (128 ** -0.5)
        nqb = (T + 511) // 512
        acnt = [0]
        scnt = [0]
        for h in range(4):
            for qb in range(nqb):
                q0 = qb * 512
                Nq = min(512, T - q0)
                maps = (0, 1) if which == "B" else (0,)
                for m in maps:
                    K.tick(3)
                    bn, bd = (4, 5) if acnt[0] % 2 == 0 else (6, 7)
                    acnt[0] += 1
                    its = []
                    for j in range(nk):
                        lat = j - nctx
                        qa, qe = 0, Nq
                        if which == "C" and is_s and lat >= 0:
                            qa = max(q0, (lat - 1) * 128) - q0
                            qe = min(q0 + Nq, (lat + 2) * 128) - q0
                            if qe <= qa:
                                continue
                        its.append((j, lat, qa, qe))

                    def emit_score(k):
                        j, lat, qa, qe = its[k]
                        b = scnt[0] % 4
                        scnt[0] += 1
                        eb = k % 2
                        if which == "B":
                            lk = kT[m * 64:(m + 1) * 64, h, j * 128:(j + 1) * 128]
                            rq = qT[m * 64:(m + 1) * 64, h, q0 + qa:q0 + qe]
                        else:
                            lk = kT[:, h // 2, j * 128:(j + 1) * 128]
                            rq = qT[:, h, q0 + qa:q0 + qe]
                        K.op("pe", lambda e: e.matmul(PS[b][:, qa:qe], lhsT=lk, rhs=rq, start=True, stop=True),
                             reads=["kT", "qT"], writes=[psk[b]])
                        K.op("act", lambda e: e.activation(out=ebuf[eb][:, qa:qe], in_=PS[b][:, qa:qe], func=AF.Exp, scale=scale),
                             reads=[psk[b]], writes=[f"ebuf{eb}"])
                        if which == "C" and is_s and lat >= 0:
                            mo = (q0 + qa) - (lat - 1) * 128
                            K.op("pool", lambda e: e.tensor_tensor(out=ebuf[eb][:, qa:qe], in0=ebuf[eb][:, qa:qe], in1=band[:, mo:mo + (qe - qa)], op=ALU.mult),
                                 reads=[f"ebuf{eb}", "band"], writes=[f"ebuf{eb}"])

                    def emit_acc(k):
                        j, lat, qa, qe = its[k]
                        eb = k % 2
                        vsl = vv[:, j, h * 128:(h + 1) * 128] if which == "B" else vv[:, j, (h // 2) * 128:(h // 2 + 1) * 128]
                        K.op("pe", lambda e: e.matmul(PS[bn][:, qa:qe], lhsT=vsl, rhs=ebuf[eb][:, qa:qe], start=(k == 0), stop=(k == len(its) - 1)),
                             reads=["vv", f"ebuf{eb}"], writes=[psk[bn]])
                        K.op("pe", lambda e: e.matmul(PS[bd][:, qa:qe], lhsT=onesb, rhs=ebuf[eb][:, qa:qe], start=(k == 0), stop=(k == len(its) - 1)),
                             reads=["onesb", f"ebuf{eb}"], writes=[psk[bd]])

                    emit_score(0)
                    for k in range(len(its)):
                        if k + 1 < len(its):
                            emit_score(k + 1)
                        emit_acc(k)
                    if which == "C":
                        K.op("dve", lambda e: e.tensor_scalar(out=rden[:, 0:Nq], in0=PS[bd][:, 0:Nq], scalar1=esink[:, h:h + 1], scalar2=None, op0=ALU.add),
                             reads=[psk[bd], "esink"], writes=["rden"])
                        K.op("dve", lambda e: e.reciprocal(out=rden[:, 0:Nq], in_=rden[:, 0:Nq]), reads=["rden"], writes=["rden"])
                        mix_out(8 + h, t0 + q0, Nq, lambda o_, k_: K.op("dve", lambda e: e.tensor_tensor(out=o_, in0=PS[bn][:, 0:Nq], in1=rden[:, 0:Nq], op=ALU.mult),
                                                                 reads=[psk[bn], "rden"], writes=[k_]))
                    else:
                        K.op("dve", lambda e: e.reciprocal(out=rden[:, 0:Nq], in_=PS[bd][:, 0:Nq]), reads=[psk[bd]], writes=["rden"])
                        if m == 0:
                            K.op("dve", lambda e: e.tensor_tensor(out=acc[:, 0:Nq], in0=PS[bn][:, 0:Nq], in1=rden[:, 0:Nq], op=ALU.mult),
                                 reads=[psk[bn], "rden"], writes=["acc"])
                        else:
                            K.op("dve", lambda e: e.tensor_tensor(out=o1[:, 0:Nq], in0=PS[bn][:, 0:Nq], in1=rden[:, 0:Nq], op=ALU.mult),
                                 reads=[psk[bn], "rden"], writes=["o1"])
                            K.op("dve", lambda e: e.scalar_tensor_tensor(out=acc[:, 0:Nq], in0=o1[:, 0:Nq], scalar=lam_s[:, 2:3], in1=acc[:, 0:Nq],
                                                                         op0=ALU.mult, op1=ALU.add),
                                 reads=["o1", "lam_s", "acc"], writes=["acc"])
                if which == "B":
                    K.op("act", lambda e: e.activation(out=ebuf[0][:, 0:Nq], in_=acc[:, 0:Nq], func=AF.Square), reads=["acc"], writes=["ebuf0"])
                    b = scnt[0] % 4
                    scnt[0] += 1
                    K.op("pe", lambda e: e.matmul(PS[b][:, 0:Nq], lhsT=onesb, rhs=ebuf[0][:, 0:Nq], start=True, stop=True),
                         reads=["onesb", "ebuf0"], writes=[psk[b]])
                    K.op("act", lambda e: e.activation(out=rden[:, 0:Nq], in_=PS[b][:, 0:Nq], func=AF.Sqrt, scale=1.0 / 128, bias=epsT[:, 0:1]),
                         reads=[psk[b], "epsT"], writes=["rden"])
                    K.op("dve", lambda e: e.reciprocal(out=rden[:, 0:Nq], in_=rden[:, 0:Nq]), reads=["rden"], writes=["rden"])
                    mix_out(4 + h, t0 + q0, Nq, lambda o_, k_: K.op("dve", lambda e: e.scalar_tensor_tensor(out=o_, in0=acc[:, 0:Nq], scalar=og_s[:, 0:1],
                                                                                                 in1=rden[:, 0:Nq], op0=ALU.mult, op1=ALU.mult),
                                                             reads=["acc", "og_s", "rden"], writes=[k_]))

    def mlstm(l, t0, T, is_s):
        nch = T // 128
        blocks = [(q, min(512, T - q)) for q in range(0, T, 512)]
        gi = [aw(0, 1024)[0:4, 0:T], aw(1024, 1024)[0:4, 0:T]]
        gf = [aw(2048, 1024)[0:4, 0:T], aw(3072, 1024)[0:4, 0:T]]
        bcs = [aw(4096, 1024)[0:4, 0:T], aw(5120, 1024)[0:4, 0:T]]
        av = [aw(6144, 1024)[0:4, 0:T], aw(7168, 1024)[0:4, 0:T]]
        cm = [aw(8192, 1024)[0:4, 0:T], aw(9216, 1024)[0:4, 0:T]]
        zrow = aw(10240, 1024)[0:4, 0:T]
        aT = [aw(11264, 32).rearrange("p (j h) -> p j h", h=4), aw(11296, 32).rearrange("p (j h) -> p j h", h=4)]
        wT = [aw(11328, 32).rearrange("p (j h) -> p j h", h=4), aw(11360, 32).rearrange("p (j h) -> p j h", h=4)]
        qTb_ = [awb(11392, 512)[:, 0:T], awb(19456, 512)[:, 0:T]]
        kTb_ = [awb(11904, 512)[:, 0:T], awb(19968, 512)[:, 0:T]]
        vtm_ = [awb(12416, 528).rearrange("p (j n) -> p j n", j=8), awb(20480, 528).rearrange("p (j n) -> p j n", j=8)]
        oT_ = [aw(12944, 1024)[:, 0:T], aw(21008, 1024)[:, 0:T]]
        cB = aw(13968, 1024)[:, 0:T]
        emB = aw(14992, 1024)[:, 0:T]
        Ebuf_ = [awb(16016, 512)[:, 0:T], awb(22032, 512)[:, 0:T]]
        Pbuf_ = [awb(16528, 512)[:, 0:T], awb(22544, 512)[:, 0:T]]
        hsum = aw(17040, 1024)[:, 0:T]
        tmpn = aw(18064, 512)
        qw = awb(18576, 512)[:, 0:T]
        C0b = awb(19088, 64)
        n0rep = awb(19152, 64)
        kwb = awb(19216, 64)
        ktm = aw(19280, 128)
        smal = aw(19408, 48)
        igb = [smal[0:4, 0:1], smal[0:4, 1:2]]
        fgb = [smal[0:4, 2:3], smal[0:4, 3:4]]
        m0c = [smal[0:4, 4:5], smal[0:4, 5:6]]
        mout = [smal[0:4, 6:7], smal[0:4, 7:8]]
        n0col = smal[:, 8:9]
        rev = lambda ap: ap[:, ::-1]
        scale = 128 ** -0.5
        seq = 0 if t0 == 0 else 1
        K.op("dve", lambda e: e.memset(zrow, 0.0), writes=["zrow"])
        for i_ in range(2):
            K.op("dve", lambda e: e.memset(vtm_[i_][:, :, 128:129], 1.0), writes=[f"vtm{i_}"])
        for d in range(2):
            K.dma("sp", gi[d], zAT[1536 + d * 4:1540 + d * 4, t0:t0 + T], reads=["z2048"], writes=[f"gi{d}"])
            K.dma("sp", gf[d], zAT[1544 + d * 4:1548 + d * 4, t0:t0 + T], reads=["z2048"], writes=[f"gf{d}"])
            K.dma("sp", igb[d], ig_b[l, d * 4:(d + 1) * 4].rearrange("(p o) -> p o", o=1), writes=["smal"])
            K.dma("sp", fgb[d], fg_b[l, d * 4:(d + 1) * 4].rearrange("(p o) -> p o", o=1), writes=["smal"])
            if is_s:
                K.dma("sp", m0c[d], st_m[l, d * 4:(d + 1) * 4].rearrange("(p o) -> p o", o=1), writes=["smal"])
            K.op("dve", lambda e: e.tensor_scalar(out=gi[d], in0=gi[d], scalar1=igb[d], scalar2=None, op0=ALU.add), reads=[f"gi{d}", "smal"], writes=[f"gi{d}"])
            K.op("act", lambda e: e.activation(out=gf[d], in_=gf[d], func=AF.Sigmoid, bias=fgb[d]), reads=[f"gf{d}", "smal"], writes=[f"gf{d}"])
            K.op("act", lambda e: e.activation(out=gf[d], in_=gf[d], func=AF.Ln), reads=[f"gf{d}"], writes=[f"gf{d}"])
            o_ = (lambda ap: ap) if d == 0 else rev
            K.op("dve", lambda e: e.tensor_tensor_scan(out=o_(bcs[d]), data0=o_(gf[d]), data1=zrow, initial=0.0, op0=ALU.add, op1=ALU.add),
                 reads=[f"gf{d}", "zrow"], writes=[f"bcs{d}"])
            K.op("dve", lambda e: e.tensor_tensor(out=av[d], in0=gi[d], in1=bcs[d], op=ALU.subtract), reads=[f"gi{d}", f"bcs{d}"], writes=[f"av{d}"])
            init = m0c[d] if is_s else 0.0
            K.op("dve", lambda e: e.tensor_tensor_scan(out=o_(cm[d]), data0=o_(av[d]), data1=o_(av[d]), initial=init, op0=ALU.max, op1=ALU.max),
                 reads=[f"av{d}", "smal"], writes=[f"cm{d}"])
            K.op("dve", lambda e: e.tensor_scalar(out=cm[d], in0=cm[d], scalar1=-1.0, scalar2=None, op0=ALU.mult), reads=[f"cm{d}"], writes=[f"cm{d}"])
            K.op("dve", lambda e: e.tensor_tensor(out=gf[d], in0=cm[d], in1=bcs[d], op=ALU.subtract), reads=[f"cm{d}", f"bcs{d}"], writes=[f"gf{d}"])
            K.op("act", lambda e: e.activation(out=gf[d], in_=gf[d], func=AF.Exp), reads=[f"gf{d}"], writes=[f"gf{d}"])
            if is_s:
                K.op("act", lambda e: e.activation(out=gi[d], in_=cm[d], func=AF.Exp, bias=m0c[d]), reads=[f"cm{d}", "smal"], writes=[f"gi{d}"])
            b = nps()
            for j in range(nch):
                K.op("pe", lambda e: e.transpose(PS[b][:, j * 4:(j + 1) * 4], av[d][:, j * 128:(j + 1) * 128], ident[0:4, 0:4]),
                     reads=[f"av{d}", "ident"], writes=[psk[b]])
            K.op("dve", lambda e: e.tensor_copy(out=aT[d][:, 0:nch, :], in_=PS[b][:, 0:nch * 4].rearrange("p (j h) -> p j h", h=4)),
                 reads=[psk[b]], writes=[f"aT{d}"])
            if not is_s:
                last = slice(T - 1, T) if d == 0 else slice(0, 1)
                K.op("dve", lambda e: e.tensor_tensor(out=mout[d], in0=bcs[d][:, last], in1=cm[d][:, last], op=ALU.subtract),
                     reads=[f"bcs{d}", f"cm{d}"], writes=["smal"])
                K.dma("sp", o_m[seq, l, d * 4:(d + 1) * 4].rearrange("(p o) -> p o", o=1), mout[d], reads=["smal"], writes=["o_m"])
                K.op("act", lambda e: e.activation(out=av[d], in_=av[d], func=AF.Exp, bias=cm[d][:, last]), reads=[f"av{d}", f"cm{d}"], writes=[f"av{d}"])
                b = nps()
                for j in range(nch):
                    K.op("pe", lambda e: e.transpose(PS[b][:, j * 4:(j + 1) * 4], av[d][:, j * 128:(j + 1) * 128], ident[0:4, 0:4]),
                         reads=[f"av{d}", "ident"], writes=[psk[b]])
                K.op("dve", lambda e: e.tensor_copy(out=wT[d][:, 0:nch, :], in_=PS[b][:, 0:nch * 4].rearrange("p (j h) -> p j h", h=4)),
                     reads=[psk[b]], writes=[f"wT{d}"])
        for h in range(4):
            hp = h % 2
            qTb, kTb, vtm, oT = qTb_[hp], kTb_[hp], vtm_[hp], oT_[hp]
            kqT, kkT, kvt, koT = f"qTb{hp}", f"kTb{hp}", f"vtm{hp}", f"oT{hp}"
            K.dma("pool", qTb, zAT[h * 128:(h + 1) * 128, t0:t0 + T], reads=["z0"], writes=[kqT])
            K.dma("pool", kTb, zAT[512 + h * 128:512 + (h + 1) * 128, t0:t0 + T], reads=["z512"], writes=[kkT])
            K.dma("pool", vtm[:, 0:nch, 0:128], ztm[t0:t0 + T, ZC["Av"] + h * 128:ZC["Av"] + (h + 1) * 128].rearrange("(j p) d -> p j d", p=128),
                  reads=["z1024"], writes=[kvt])
            K.dma("sp", oT, zAT[1024 + h * 128:1024 + (h + 1) * 128, t0:t0 + T], reads=["z1536"], writes=[koT])
            K.op("act", lambda e: e.activation(out=oT, in_=oT, func=AF.Sigmoid), reads=[koT], writes=[koT])
            for d in range(2):
                K.tick(2)
                selh = sel[0:4, h * 128:(h + 1) * 128]
                for bi, (q0, n) in enumerate(blocks):
                    b = nps() % 4
                    K.op("pe", lambda e: e.matmul(PS[b][:, 0:n], lhsT=selh, rhs=cm[d][:, q0:q0 + n], start=True, stop=True),
                         reads=["sel", f"cm{d}"], writes=[psk[b]])
                    K.op("act", lambda e: e.activation(out=cB[:, q0:q0 + n], in_=PS[b][:, 0:n], func=AF.Copy), reads=[psk[b]], writes=["cB"])
                    b = nps() % 4
                    K.op("pe", lambda e: e.matmul(PS[b][:, 0:n], lhsT=selh, rhs=gf[d][:, q0:q0 + n], start=True, stop=True),
                         reads=["sel", f"gf{d}"], writes=[psk[b]])
                    K.op("act", lambda e: e.activation(out=emB[:, q0:q0 + n], in_=PS[b][:, 0:n], func=AF.Copy), reads=[psk[b]], writes=["emB"])
                bnum = [4, 5]
                bden = [6, 7]
                if is_s:
                    K.dma("pool", C0b, st_C[l, d, h], writes=["C0b"])
                    K.dma("sp", n0col, st_n[l, d, h].rearrange("(p o) -> p o", o=1), writes=["n0col"])
                    K.op("dve", lambda e: e.tensor_copy(out=n0rep, in_=n0col.to_broadcast([128, 128])), reads=["n0col"], writes=["n0rep"])
                    for bi, (q0, n) in enumerate(blocks):
                        b = nps() % 4
                        K.op("pe", lambda e: e.matmul(PS[b][:, 0:n], lhsT=selh, rhs=gi[d][:, q0:q0 + n], start=True, stop=True),
                             reads=["sel", f"gi{d}"], writes=[psk[b]])
                        K.op("dve", lambda e: e.scalar_tensor_tensor(out=qw[:, q0:q0 + n], in0=qTb[:, q0:q0 + n], scalar=scale, in1=PS[b][:, 0:n],
                                                                     op0=ALU.mult, op1=ALU.mult),
                             reads=[kqT, psk[b]], writes=["qw"])
                        K.op("pe", lambda e: e.matmul(PS[bnum[bi]][:, 0:n], lhsT=C0b, rhs=qw[:, q0:q0 + n], start=True, stop=False),
                             reads=["C0b", "qw"], writes=[psk[bnum[bi]]])
                        K.op("pe", lambda e: e.matmul(PS[bden[bi]][:, 0:n], lhsT=n0rep, rhs=qw[:, q0:q0 + n], start=True, stop=False),
                             reads=["n0rep", "qw"], writes=[psk[bden[bi]]])
                order = list(range(nch)) if d == 0 else list(range(nch - 1, -1, -1))
                started = [is_s for _ in blocks]
                pieces = {}

                def stage1(k):
                    j = order[k]
                    Ebuf, Pbuf = Ebuf_[k % 2], Pbuf_[k % 2]
                    kE, kP = f"Ebuf{k % 2}", f"Pbuf{k % 2}"
                    qa, qe = (128 * j, T) if d == 0 else (0, 128 * (j + 1))
                    K.op("act", lambda e: e.activation(out=Ebuf[:, qa:qe], in_=cB[:, qa:qe], func=AF.Exp, bias=aT[d][:, j, h:h + 1]),
                         reads=["cB", f"aT{d}"], writes=[kE])
                    tm = tri[:, 0:128] if d == 0 else tri[:, 128:256]
                    K.op("pool", lambda e: e.tensor_tensor(out=Ebuf[:, 128 * j:128 * (j + 1)], in0=Ebuf[:, 128 * j:128 * (j + 1)], in1=tm, op=ALU.mult),
                         reads=[kE, "tri"], writes=[kE])
                    pcs = []
                    for bi, (q0, n) in enumerate(blocks):
                        pa, pe_ = max(qa, q0), min(qe, q0 + n)
                        if pe_ <= pa:
                            continue
                        b = nps() % 4
                        K.op("pe", lambda e: e.matmul(PS[b][:, pa - q0:pe_ - q0], lhsT=kTb[:, j * 128:(j + 1) * 128], rhs=qTb[:, pa:pe_], start=True, stop=True),
                             reads=[kkT, kqT], writes=[psk[b]])
                        K.op("dve", lambda e: e.scalar_tensor_tensor(out=Pbuf[:, pa:pe_], in0=PS[b][:, pa - q0:pe_ - q0], scalar=scale, in1=Ebuf[:, pa:pe_],
                                                                     op0=ALU.mult, op1=ALU.mult),
                             reads=[psk[b], kE], writes=[kP])
                        pcs.append((bi, q0, n, pa, pe_))
                    pieces[k] = pcs

                def stage2(k):
                    j = order[k]
                    Pbuf = Pbuf_[k % 2]
                    kP = f"Pbuf{k % 2}"
                    for (bi, q0, n, pa, pe_) in pieces[k]:
                        lastj = ((q0 + n) // 128 - 1) if d == 0 else (q0 // 128)
                        K.op("pe", lambda e: e.matmul(PS[bnum[bi]][:, pa - q0:pe_ - q0], lhsT=vtm[:, j, 0:128], rhs=Pbuf[:, pa:pe_],
                                                      start=(not started[bi]), stop=(j == lastj)),
                             reads=[kvt, kP], writes=[psk[bnum[bi]]])
                        K.op("pe", lambda e: e.matmul(PS[bden[bi]][:, pa - q0:pe_ - q0], lhsT=onesb, rhs=Pbuf[:, pa:pe_],
                                                      start=(not started[bi]), stop=(j == lastj)),
                             reads=["onesb", kP], writes=[psk[bden[bi]]])
                        started[bi] = True

                stage1(0)
                for k in range(nch):
                    if k + 1 < nch:
                        stage1(k + 1)
                    stage2(k)
                for bi, (q0, n) in enumerate(blocks):
                    K.op("act", lambda e: e.activation(out=tmpn[:, 0:n], in_=PS[bden[bi]][:, 0:n], func=AF.Abs),
                         reads=[psk[bden[bi]]], writes=["tmpn"])
                    K.op("dve", lambda e: e.tensor_tensor(out=tmpn[:, 0:n], in0=tmpn[:, 0:n], in1=emB[:, q0:q0 + n], op=ALU.max),
                         reads=["tmpn", "emB"], writes=["tmpn"])
                    K.op("dve", lambda e: e.reciprocal(out=tmpn[:, 0:n], in_=tmpn[:, 0:n]), reads=["tmpn"], writes=["tmpn"])
                    if d == 0:
                        K.op("dve", lambda e: e.tensor_tensor(out=hsum[:, q0:q0 + n], in0=PS[bnum[bi]][:, 0:n], in1=tmpn[:, 0:n], op=ALU.mult),
                             reads=[psk[bnum[bi]], "tmpn"], writes=["hsum"])
                    else:
                        K.op("dve", lambda e: e.tensor_tensor(out=tmpn[:, 0:n], in0=PS[bnum[bi]][:, 0:n], in1=tmpn[:, 0:n], op=ALU.mult),
                             reads=[psk[bnum[bi]], "tmpn"], writes=["tmpn"])
                        K.op("dve", lambda e: e.tensor_tensor(out=hsum[:, q0:q0 + n], in0=hsum[:, q0:q0 + n], in1=tmpn[:, 0:n], op=ALU.add),
                             reads=["hsum", "tmpn"], writes=["hsum"])
                if not is_s:
                    b = nps() % 4
                    for j in range(nch):
                        K.dma("sp", ktm, ztm[t0 + j * 128:t0 + (j + 1) * 128, ZC["Ak"] + h * 128:ZC["Ak"] + (h + 1) * 128], reads=["z512"], writes=["ktm"])
                        K.op("dve", lambda e: e.tensor_scalar(out=kwb, in0=ktm, scalar1=wT[d][:, j, h:h + 1], scalar2=None, op0=ALU.mult),
                             reads=["ktm", f"wT{d}"], writes=["kwb"])
                        K.op("pe", lambda e: e.matmul(PS[b][:, 0:129], lhsT=kwb, rhs=vtm[:, j, 0:129], start=(j == 0), stop=(j == nch - 1)),
                             reads=["kwb", kvt], writes=[psk[b]])
                    s = srr[0]
                    srr[0] = (s + 1) % 4
                    K.op("act", lambda e: e.activation(out=stage[s][:, 0:129], in_=PS[b][:, 0:129], func=AF.Copy), reads=[psk[b]], writes=[f"stage{s}"])
                    K.dma("sp", o_C[seq, l, d, h], stage[s][:, 0:128], reads=[f"stage{s}"], writes=["o_C"])
                    K.dma("sp", o_n[seq, l, d, h].rearrange("(p o) -> p o", o=1), stage[s][:, 128:129], reads=[f"stage{s}"], writes=["o_n"])
            for bi, (q0, n) in enumerate(blocks):
                K.op("act", lambda e: e.activation(out=Pbuf_[0][:, q0:q0 + n], in_=hsum[:, q0:q0 + n], func=AF.Square), reads=["hsum"], writes=["Pbuf0"])
                b = nps() % 4
                K.op("pe", lambda e: e.matmul(PS[b][:, 0:n], lhsT=onesb, rhs=Pbuf_[0][:, q0:q0 + n], start=True, stop=True),
                     reads=["onesb", "Pbuf0"], writes=[psk[b]])
                K.op("act", lambda e: e.activation(out=tmpn[:, 0:n], in_=PS[b][:, 0:n], func=AF.Sqrt, scale=1.0 / 128, bias=epsT[:, 0:1]),
                     reads=[psk[b], "epsT"], writes=["tmpn"])
                K.op("dve", lambda e: e.reciprocal(out=tmpn[:, 0:n], in_=tmpn[:, 0:n]), reads=["tmpn"], writes=["tmpn"])
                K.op("dve", lambda e: e.scalar_tensor_tensor(out=tmpn[:, 0:n], in0=hsum[:, q0:q0 + n], scalar=mng[:, h:h + 1], in1=tmpn[:, 0:n],
                                                             op0=ALU.mult, op1=ALU.mult),
                     reads=["hsum", "mng", "tmpn"], writes=["tmpn"])
                mix_out(h, t0 + q0, n, lambda o_, k_: K.op("dve", lambda e: e.tensor_tensor(out=o_, in0=tmpn[:, 0:n], in1=oT[:, q0:q0 + n], op=ALU.mult),
                                                         reads=["tmpn", koT], writes=[k_]))

    zpad = arena[:, 19456:19456 + 1026]
    hcw = K.sb([128, 36], F32, "hcw")
    hcb = K.sb([128, 12], F32, "hcb")
    hsm = K.sb([128, 4], F32, "hsm")
    u2buf = arena[:, 20500:20500 + 2048].bitcast(BF16).rearrange("p (j c) -> p j c", c=512)

    def hy_filters(l, L):
        nch = L // 128
        featT = aw(0, 1024)[0:34, 0:L]
        h1s = aw(1024, 1024)[0:64, 0:L]
        h2s = aw(2048, 1024)[0:65, 0:L]
        ta = aw(3072, 512)[0:64, :]
        tki = aw(3584, 512)[0:64, :].bitcast(I32)
        tkf = aw(4096, 512)[0:64, :]
        w1a = aw(4608, 64)[0:34, :]
        w2t = aw(4672, 64)[0:64, :]
        w3a = aw(4736, 2048)[0:65, :]
        wint = aw(6784, 512)
        hfb = [aw(7296, 512), aw(7808, 512)]
        Gs = [awb(8320, 2048).rearrange("p (j c) -> p j c", c=512), awb(10368, 2048).rearrange("p (j c) -> p j c", c=512)]
        Gd = [awb(12416, 2048).rearrange("p (j c) -> p j c", c=512), awb(14464, 2048).rearrange("p (j c) -> p j c", c=512)]
        CSf = [awb(16512, 512).rearrange("p (j c) -> p j c", c=128), awb(17024, 512).rearrange("p (j c) -> p j c", c=128)]
        skp = aw(18560, 512)[0:1, :]
        frc = hsm[0:64, 0:1]
        b2c = hsm[0:64, 1:2]
        K.dma("sp", featT, k_feat[L], writes=["featT"])
        K.dma("sp", w1a[0:33, :], hy_w1[l], writes=["w1a"])
        K.dma("sp", w1a[33:34, :], hy_b1[l:l + 1, :], writes=["w1a"])
        K.dma("sp", w2t, hy_w2[l], writes=["w2t"])
        K.dma("sp", w3a[0:64, :], hy_w3[l], writes=["w3a"])
        K.dma("sp", w3a[64:65, :], hy_b3[l:l + 1, :], writes=["w3a"])
        K.dma("sp", frc, hy_fr[l].rearrange("(p o) -> p o", o=1), writes=["hsm"])
        K.dma("sp", b2c, hy_b2[l].rearrange("(p o) -> p o", o=1), writes=["hsm"])
        K.op("dve", lambda e: e.memset(h2s[64:65, :], 1.0), writes=["h2s"])
        K.end_atom()

        def sin_layer(dst, dkey, lhsT, lkey, rhs_t, rkey, add_b):
            for q0 in range(0, L, 512):
                n = min(512, L - q0)
                b = nps()
                K.op("pe", lambda e: e.matmul(PS[b][0:64, 0:n], lhsT=lhsT, rhs=rhs_t[:, q0:q0 + n], start=True, stop=True),
                     reads=[lkey, rkey], writes=[psk[b]])
                if add_b:
                    K.op("dve", lambda e: e.tensor_scalar(out=ta[:, 0:n], in0=PS[b][0:64, 0:n], scalar1=b2c, scalar2=frc, op0=ALU.add, op1=ALU.mult),
                         reads=[psk[b], "hsm"], writes=["ta"])
                else:
                    K.op("dve", lambda e: e.tensor_scalar(out=ta[:, 0:n], in0=PS[b][0:64, 0:n], scalar1=frc, scalar2=None, op0=ALU.mult),
                         reads=[psk[b], "hsm"], writes=["ta"])
                K.op("dve", lambda e: e.tensor_scalar(out=tki[:, 0:n], in0=ta[:, 0:n], scalar1=float(1 / (2 * math.pi)), scalar2=None, op0=ALU.mult), reads=["ta"], writes=["tki"])
                K.op("dve", lambda e: e.tensor_copy(out=tkf[:, 0:n], in_=tki[:, 0:n]), reads=["tki"], writes=["tkf"])
                K.op("dve", lambda e: e.scalar_tensor_tensor(out=ta[:, 0:n], in0=tkf[:, 0:n], scalar=float(-2 * math.pi), in1=ta[:, 0:n], op0=ALU.mult, op1=ALU.add),
                     reads=["tkf", "ta"], writes=["ta"])
                K.op("dve", lambda e: e.tensor_scalar(out=tkf[:, 0:n], in0=ta[:, 0:n], scalar1=float(math.pi), scalar2=float(-2 * math.pi), op0=ALU.is_gt, op1=ALU.mult), reads=["ta"], writes=["tkf"])
                K.op("dve", lambda e: e.tensor_tensor(out=ta[:, 0:n], in0=ta[:, 0:n], in1=tkf[:, 0:n], op=ALU.add), reads=["ta", "tkf"], writes=["ta"])
                K.op("dve", lambda e: e.tensor_scalar(out=tkf[:, 0:n], in0=ta[:, 0:n], scalar1=float(-math.pi), scalar2=float(2 * math.pi), op0=ALU.is_lt, op1=ALU.mult), reads=["ta"], writes=["tkf"])
                K.op("dve", lambda e: e.tensor_tensor(out=ta[:, 0:n], in0=ta[:, 0:n], in1=tkf[:, 0:n], op=ALU.add), reads=["ta", "tkf"], writes=["ta"])
                K.op("act", lambda e: e.activation(out=dst[0:64, q0:q0 + n], in_=ta[:, 0:n], func=AF.Sin), reads=["ta"], writes=[dkey])
                K.end_atom()

        sin_layer(h1s, "h1s", w1a, "w1a", featT, "featT", False)
        sin_layer(h2s, "h2s", w2t, "w2t", h1s, "h1s", True)
        for tc in range(nch):
            K.dma("sp", wint, k_win[L][tc * 128:(tc + 1) * 128, :], writes=["wint"])
            for o in range(2):
                for dr in range(2):
                    g = o * 2 + dr
                    b = nps()
                    K.op("pe", lambda e: e.matmul(PS[b], lhsT=h2s[:, tc * 128:(tc + 1) * 128], rhs=w3a[:, g * 512:(g + 1) * 512], start=True, stop=True),
                         reads=["h2s", "w3a"], writes=[psk[b]])
                    K.op("dve", lambda e: e.tensor_tensor(out=hfb[dr], in0=PS[b], in1=wint, op=ALU.mult), reads=[psk[b], "wint"], writes=[f"hfb{dr}"])
                if tc == 0:
                    K.dma("sp", skp, hy_bias[l, o:o + 1, :], writes=["skp"])
                    K.op("dve", lambda e: e.tensor_tensor(out=hfb[0][0:1, :], in0=hfb[0][0:1, :], in1=skp, op=ALU.add), reads=["hfb0", "skp"], writes=["hfb0"])
                    K.op("dve", lambda e: e.memset(hfb[1][0:1, :], 0.0), reads=["hfb1"], writes=["hfb1"])
                K.op("dve", lambda e: e.tensor_tensor(out=Gs[o][:, tc, :], in0=hfb[0], in1=hfb[1], op=ALU.add), reads=["hfb0", "hfb1"], writes=["Gs"])
                K.op("dve", lambda e: e.tensor_tensor(out=Gd[o][:, tc, :], in0=hfb[1], in1=hfb[0], op=ALU.subtract), reads=["hfb0", "hfb1"], writes=["Gd"])
                K.end_atom()
        for fc in range(nch):
            for ri in range(2):
                K.dma("pool", CSf[ri][:, 0:nch, :], k_dft[L][ri, :, fc * 128:(fc + 1) * 128].rearrange("(tc p) f -> p tc f", p=128), writes=[f"CSf{ri}"])
            for o in range(2):
                for ri in range(2):
                    src = Gs[o] if ri == 0 else Gd[o]
                    b = nps()
                    for tc in range(nch):
                        K.op("pe", lambda e: e.matmul(PS[b], lhsT=CSf[ri][:, tc, :], rhs=src[:, tc, :], start=(tc == 0), stop=(tc == nch - 1)),
                             reads=[f"CSf{ri}", "Gs", "Gd"], writes=[psk[b]])
                    s = srr[0]
                    srr[0] = (s + 1) % 4
                    K.op("act", lambda e: e.activation(out=stage[s], in_=PS[b], func=AF.Copy, scale=1.0 / L), reads=[psk[b]], writes=[f"stage{s}"])
                    K.dma("sp", hyG[L][o, ri, fc * 128:(fc + 1) * 128, :], stage[s], reads=[f"stage{s}"], writes=[f"hyG{L}"])
                    K.end_atom()

    def hyena(l, t0, T, is_s):
        L = T
        nch = L // 128
        x12 = [awb(0, 2048).rearrange("p (c t) -> p c t", c=4), awb(2048, 2048).rearrange("p (c t) -> p c t", c=4)]
        u_tm = awb(4096, 2048).rearrange("p (j c) -> p j c", c=512)
        Yre = awb(6144, 2048).rearrange("p (j c) -> p j c", c=512)
        Yim = awb(8192, 2048).rearrange("p (j c) -> p j c", c=512)
        CSb = [awb(10240, 2048).rearrange("p (j t) -> p j t", t=512), awb(12288, 2048).rearrange("p (j t) -> p j t", t=512)]
        CSf_ = [[awb(14336, 512).rearrange("p (j c) -> p j c", c=128), awb(14848, 512).rearrange("p (j c) -> p j c", c=128)],
                [awb(18432, 512).rearrange("p (j c) -> p j c", c=128), awb(18944, 512).rearrange("p (j c) -> p j c", c=128)]]
        Gt_ = [[aw(15360, 512), aw(15872, 512)], [aw(22548, 512), aw(23060, 512)]]
        tt_ = [aw(16384, 512), aw(16896, 512), aw(17408, 512), aw(17920, 512)]
        u2 = awb(17408 + 1024, 1024 - 0).rearrange("p (j c) -> p j c", c=512) if False else u2buf
        cvo = aw(16384, 1024)
        K.op("dve", lambda e: e.memset(zpad, 0.0), reads=["zpad"], writes=["zpad"])

        def to_tm(srcT, skey, cc):
            for g in range((nch + 3) // 4):
                k = min(4, nch - g * 4)
                b = nps()
                for i2 in range(k):
                    tc = g * 4 + i2
                    K.op("pe", lambda e: e.transpose(PS[b][:, i2 * 128:(i2 + 1) * 128], srcT[:, tc * 128:(tc + 1) * 128], ident),
                         reads=[skey, "ident"], writes=[psk[b]])
                evac(u_tm[:, g * 4:g * 4 + k, cc * 128:(cc + 1) * 128], PS[b][:, 0:k * 128].rearrange("p (j c) -> p j c", c=128), [psk[b]], ["u_tm"])

        for ch in range(12):
            K.dma("sp", zpad[:, 1:T + 1], zDT[ch * 128:(ch + 1) * 128, t0:t0 + T], reads=[f"z{4624 + (ch // 4) * 512}"], writes=["zpad"])
            dst = cvo[:, 0:T] if ch < 4 else x12[(ch - 4) // 4][:, ch % 4, 0:T]
            dkey = "tt" if ch < 4 else "x12"
            K.op("act", lambda e: e.activation(out=cvo[:, 0:T], in_=zpad[:, 1:T + 1], func=AF.Identity, scale=hcw[:, 12 + ch:13 + ch], bias=hcb[:, ch:ch + 1]),
                 reads=["zpad", "hcw", "hcb"], writes=["tt"])
            K.op("dve", lambda e: e.scalar_tensor_tensor(out=cvo[:, 0:T], in0=zpad[:, 0:T], scalar=hcw[:, ch:ch + 1], in1=cvo[:, 0:T], op0=ALU.mult, op1=ALU.add),
                 reads=["zpad", "hcw", "tt"], writes=["tt"])
            K.op("dve", lambda e: e.scalar_tensor_tensor(out=dst, in0=zpad[:, 2:T + 2], scalar=hcw[:, 24 + ch:25 + ch], in1=cvo[:, 0:T], op0=ALU.mult, op1=ALU.add),
                 reads=["zpad", "hcw", "tt"], writes=[dkey])
            if ch < 4:
                to_tm(cvo[:, 0:T], "tt", ch)
        for o in range(2):
            for fc in range(nch):
                K.tick()
                CSf, Gt = CSf_[fc % 2], Gt_[fc % 2]
                kcs = [f"CSf{fc % 2}{ri}" for ri in range(2)]
                kgt = [f"Gt{fc % 2}{ri}" for ri in range(2)]
                for ri in range(2):
                    K.dma("pool", CSf[ri][:, 0:nch, :], k_dft[L][ri, :, fc * 128:(fc + 1) * 128].rearrange("(tc p) f -> p tc f", p=128), writes=[kcs[ri]])
                    K.dma("sp", Gt[ri], hyG[L][o, ri, fc * 128:(fc + 1) * 128, :], reads=[f"hyG{L}"], writes=[kgt[ri]])
                bu = [nps(), nps()]
                for ri in range(2):
                    for tc in range(nch):
                        K.op("pe", lambda e: e.matmul(PS[bu[ri]], lhsT=CSf[ri][:, tc, :], rhs=u_tm[:, tc, :], start=(tc == 0), stop=(tc == nch - 1)),
                             reads=[kcs[ri], "u_tm"], writes=[psk[bu[ri]]])
                K.op("dve", lambda e: e.tensor_tensor(out=tt_[0], in0=PS[bu[0]], in1=Gt[0], op=ALU.mult), reads=[psk[bu[0]], kgt[0]], writes=["tt"])
                K.op("dve", lambda e: e.tensor_tensor(out=tt_[1], in0=PS[bu[1]], in1=Gt[1], op=ALU.mult), reads=[psk[bu[1]], kgt[1]], writes=["tt"])
                K.op("dve", lambda e: e.tensor_tensor(out=tt_[2], in0=PS[bu[1]], in1=Gt[0], op=ALU.mult), reads=[psk[bu[1]], kgt[0]], writes=["tt"])
                K.op("dve", lambda e: e.tensor_tensor(out=tt_[3], in0=PS[bu[0]], in1=Gt[1], op=ALU.mult), reads=[psk[bu[0]], kgt[1]], writes=["tt"])
                K.op("pool", lambda e: e.tensor_tensor(out=Yre[:, fc, :], in0=tt_[0], in1=tt_[1], op=ALU.add), reads=["tt"], writes=["Yre"])
                K.op("pool", lambda e: e.tensor_tensor(out=Yim[:, fc, :], in0=tt_[2], in1=tt_[3], op=ALU.subtract), reads=["tt"], writes=["Yim"])
            for q0 in range(0, T, 512):
                n = min(512, T - q0)
                for ri in range(2):
                    K.dma("pool", CSb[ri][:, 0:nch, 0:n], k_dft[L][2 + ri, :, q0:q0 + n].rearrange("(fc p) t -> p fc t", p=128), writes=[f"CSb{ri}"])
                for cc in range(4):
                    b = nps()
                    for fc in range(nch):
                        K.op("pe", lambda e: e.matmul(PS[b][:, 0:n], lhsT=Yre[:, fc, cc * 128:(cc + 1) * 128], rhs=CSb[0][:, fc, 0:n], start=(fc == 0), stop=False),
                             reads=["Yre", "CSb0"], writes=[psk[b]])
                        K.op("pe", lambda e: e.matmul(PS[b][:, 0:n], lhsT=Yim[:, fc, cc * 128:(cc + 1) * 128], rhs=CSb[1][:, fc, 0:n], start=False, stop=(fc == nch - 1)),
                             reads=["Yim", "CSb1"], writes=[psk[b]])
                    if o == 0:
                        K.op("dve", lambda e: e.tensor_tensor(out=cvo[:, q0:q0 + n], in0=PS[b][:, 0:n], in1=x12[0][:, cc, q0:q0 + n], op=ALU.mult),
                             reads=[psk[b], "x12"], writes=["tt"])
                        for i2 in range(n // 128):
                            tc = q0 // 128 + i2
                            b2 = nps()
                            K.op("pe", lambda e: e.transpose(PS[b2][:, 0:128], cvo[:, tc * 128:(tc + 1) * 128], ident), reads=["tt", "ident"], writes=[psk[b2]])
                            evac(u2[:, tc, cc * 128:(cc + 1) * 128], PS[b2][:, 0:128], [psk[b2]], ["u2"])
                    else:
                        mix_out(12 + cc, t0 + q0, n, lambda o_, k_: K.op("dve", lambda e: e.tensor_tensor(out=o_, in0=PS[b][:, 0:n], in1=x12[1][:, cc, q0:q0 + n], op=ALU.mult),
                                                                      reads=[psk[b], "x12"], writes=[k_]))
            if o == 0:
                K.op("pool", lambda e: e.tensor_copy(out=u_tm[:, 0:nch, :], in_=u2[:, 0:nch, :]), reads=["u2", "u_tm"], writes=["u_tm"])

    def mix_params(l):
        lam_init = 0.8 - 0.6 * math.exp(-0.3 * l)
        K.dma("sp", lam_t, dlam[l:l + 1].rearrange("o a b -> o (a b)").partition_broadcast(128), writes=["lam_t"])
        K.op("dve", lambda e: e.tensor_tensor(out=lam_t[:, 0:64], in0=lam_t[:, 0:64], in1=lam_t[:, 64:128], op=ALU.mult), reads=["lam_t"], writes=["lam_t"])
        K.op("dve", lambda e: e.tensor_tensor(out=lam_t[:, 128:192], in0=lam_t[:, 128:192], in1=lam_t[:, 192:256], op=ALU.mult), reads=["lam_t"], writes=["lam_t"])
        K.op("dve", lambda e: e.reduce_sum(out=lam_s[:, 0:2], in_=lam_t.rearrange("p (a b) -> p a b", b=128)[:, :, 0:64], axis=AX.X),
             reads=["lam_t"], writes=["lam_s"])
        K.op("act", lambda e: e.activation(out=lam_s[:, 0:2], in_=lam_s[:, 0:2], func=AF.Exp), reads=["lam_s"], writes=["lam_s"])
        K.op("dve", lambda e: e.tensor_tensor(out=lam_s[:, 2:3], in0=lam_s[:, 1:2], in1=lam_s[:, 0:1], op=ALU.subtract), reads=["lam_s"], writes=["lam_s"])
        K.op("dve", lambda e: e.tensor_scalar(out=lam_s[:, 2:3], in0=lam_s[:, 2:3], scalar1=-lam_init, scalar2=None, op0=ALU.add), reads=["lam_s"], writes=["lam_s"])
        load_colT(og_s[:, 0:1], dout_g[l:l + 1, :], 1, "og_s")
        K.op("dve", lambda e: e.tensor_scalar(out=og_s[:, 0:1], in0=og_s[:, 0:1], scalar1=1.0 - lam_init, scalar2=None, op0=ALU.mult), reads=["og_s"], writes=["og_s"])
        K.dma("sp", esink, ssink[l:l + 1, :].partition_broadcast(128), writes=["esink"])
        K.op("act", lambda e: e.activation(out=esink, in_=esink, func=AF.Exp), reads=["esink"], writes=["esink"])
        load_colT(mng, mnorm_g[l].rearrange("(c p) -> c p", p=128), 4, "mng")
        for r in range(3):
            load_colT(hcw[:, r * 12:(r + 1) * 12], hy_cw[l, r].rearrange("(c p) -> c p", p=128), 12, "hcw")
        load_colT(hcb, hy_cb[l].rearrange("(c p) -> c p", p=128), 12, "hcb")

    def MIX(l):
        mix_params(l)
        for which, tags in (("B", ()), ("C", ("ipC",)), ("A", ("ipC", "ipA")), ("D", ("ipC", "ipA", "ipD"))):
            if which not in MIXSEL:
                continue
            K.flush_tags(tags)
            K.fence()
            for (t0, T, is_s) in SEGS:
                if which in "BC":
                    attn_core(l, t0, T, is_s, which)
                elif which == "A":
                    mlstm(l, t0, T, is_s)
                else:
                    hyena(l, t0, T, is_s)
                K.fence()

    def mod_load(l, g):
        i = g % 2
        wv_ = w_mod[l].rearrange("(kc p) n -> p kc n", p=128)
        K.dma("pool", wbuf[i], wv_[:, :, g * 512:(g + 1) * 512], writes=[f"wbuf{i}"])

    def mod_compute(l, g):
        i = g % 2
        b = g % 4
        for j in range(4):
            for kc in range(16):
                K.op("pe", lambda e: e.matmul(PS[b][:, j * 2:j * 2 + 2], lhsT=wbuf[i][:, kc, j * 128:(j + 1) * 128],
                                              rhs=scb.rearrange("p (j k) -> p k j", j=2)[:, kc, :],
                                              start=(kc == 0), stop=(kc == 15)),
                     reads=[f"wbuf{i}", "scb"], writes=[psk[b]])
        K.op("dve", lambda e: e.tensor_tensor(out=modTs[l][:, g * 4:(g + 1) * 4, :], in0=PS[b][:, 0:8].rearrange("p (c j) -> p c j", j=2),
                                              in1=bmTs[l][:, g * 4:(g + 1) * 4].unsqueeze(2).to_broadcast([128, 4, 2]), op=ALU.add),
             reads=[psk[b], f"bmT{l}"], writes=[f"modT{l}"])
        for which, gT, gk, glast in ((0, n1Ts[l], f"n1T{l}", 7), (1, n2Ts[l], f"n2T{l}", 19)):
            if g == glast:
                base = 16 if which == 0 else 64
                K.op("dve", lambda e: e.scalar_tensor_tensor(out=nscales[l][:, which], in0=modTs[l][:, base:base + 16, :], scalar=1.0,
                                                             in1=gT.unsqueeze(2).to_broadcast([128, 16, 2]),
                                                             op0=ALU.add, op1=ALU.mult),
                     reads=[f"modT{l}", gk], writes=[f"nscale{l}"])

    def mod_atoms(l, g0, g1):
        seq = []
        for g in range(g0, g1):
            seq.append(("L", g))
        out_ = []
        ng = g1 - g0
        for k in range(ng + 1):
            if k < ng:
                out_.append(("L", g0 + k))
            if k >= 1:
                out_.append(("C", g0 + k - 1))
        for kind, g in out_:
            if kind == "L":
                mod_load(l, g)
            else:
                mod_compute(l, g)
            K.end_atom()

    for l_ in range(DEPTH):
        load_colT(bmTs[l_], b_mod[l_].rearrange("(c p) -> c p", p=128), 96, f"bmT{l_}")
        load_colT(n1Ts[l_], norm1_g[l_].rearrange("(c p) -> c p", p=128), 16, f"n1T{l_}")
        load_colT(n2Ts[l_], norm2_g[l_].rearrange("(c p) -> c p", p=128), 16, f"n2T{l_}")
    mod_load(0, 0)
    for g in range(8):
        if g + 1 < 8:
            mod_load(0, g + 1)
        mod_compute(0, g)
    K.tag = "mod"
    K.record()
    mod_atoms(0, 8, 24)
    mod_atoms(1, 0, 24)
    K.stop_record()
    mod_queue = K.aux
    K.aux = []
    K.auxtag = []

    for l in range(DEPTH):
        modT = modTs[l]
        nscale = nscales[l]
        mk = f"modT{l}"
        nk_ = f"nscale{l}"
        for tb in range(3):
            t0, cj, sh = norm_block(l, 0, tb, False)
            for c in range(16):
                K.op("dve", lambda e: e.tensor_tensor(out=tmpf, in0=xblk[:, c, :], in1=rstd, op=ALU.mult),
                     reads=["xblk", "rstd"], writes=["tmpf"])
                K.op("act", lambda e: e.activation(out=hT[:, c, t0:t0 + TB], in_=tmpf, func=AF.Identity,
                                                   scale=nscale[:, 0, c, cj:cj + 1], bias=modT[:, sh * 16 + c, cj:cj + 1]),
                     reads=["tmpf", nk_, mk], writes=["hT"])
        if debug:
            dbg[f"hT{l}"] = (hT, [128, 16, NT], "hT")

        K.fence()
        wv = w_in[l].rearrange("(kc p) n -> p kc n", p=128)
        groups = {
            0: (512, (zAT, 0), None), 512: (512, (zAT, 512), ZC["Ak"]), 1024: (512, None, ZC["Av"]),
            1536: (512, (zAT, 1024), None), 2048: (16, (zAT, 1536), None),
            2064: (512, None, ZC["Bq"]), 2576: (512, None, ZC["Bk"]), 3088: (512, None, ZC["Bv"]),
            3600: (512, None, ZC["Cq"]), 4112: (512, None, ZC["Ck"]),
            4624: (512, (zDT, 0), None), 5136: (512, (zDT, 512), None), 5648: (512, (zDT, 1024), None),
        }

        def inproj_group(c0, nt):
            ncols, fm, tm = groups[c0]
            zkey = f"z{c0}"
            i = load_w(wv[:, :, c0:c0 + ncols], ncols)
            K.end_atom()
            if fm is not None:
                dst, r0 = fm
                for j in range((ncols + 127) // 128):
                    m = min(128, ncols - j * 128)
                    for tb in range(3):
                        b = nps()
                        for kc in range(16):
                            K.op("pe", lambda e: e.matmul(PS[b][0:m, :], lhsT=wbuf[i][:, kc, j * 128:j * 128 + m],
                                                          rhs=hT[:, kc, tb * TB:(tb + 1) * TB], start=(kc == 0), stop=(kc == 15)),
                                 reads=[f"wbuf{i}", "hT"], writes=[psk[b]])
                        s = srr[0]
                        srr[0] = (s + 1) % 4
                        evac(stage[s][0:m, :], PS[b][0:m, :], [psk[b]], [f"stage{s}"])
                        K.dma("sp", dst[r0 + j * 128:r0 + j * 128 + m, tb * TB:(tb + 1) * TB], stage[s][0:m, :],
                              reads=[f"stage{s}"], writes=[zkey])
                        K.end_atom()
                        K.tick(nt)
            if tm is not None:
                for tc in range(NT // 128):
                    b = nps()
                    for kc in range(16):
                        K.op("pe", lambda e: e.matmul(PS[b][:, 0:ncols], lhsT=hT[:, kc, tc * 128:(tc + 1) * 128],
                                                      rhs=wbuf[i][:, kc, 0:ncols], start=(kc == 0), stop=(kc == 15)),
                             reads=[f"wbuf{i}", "hT"], writes=[psk[b]])
                    s = srr[0]
                    srr[0] = (s + 1) % 4
                    evac(stage[s][:, 0:ncols], PS[b][:, 0:ncols], [psk[b]], [f"stage{s}"])
                    K.dma("sp", ztm[tc * 128:(tc + 1) * 128, tm:tm + ncols], stage[s][:, 0:ncols],
                          reads=[f"stage{s}"], writes=[zkey])
                    K.end_atom()
                    K.tick(nt)

        if "D" in MIXSEL:
            K.tag = "filt"
            K.record()
            for L in (256, 1024):
                hy_filters(l, L)
            K.stop_record()
        for c0 in (2064, 2576, 3088):
            inproj_group(c0, 2)
        K.flush()
        K.fence()
        pslo[0] = True
        K.record()
        K.tag = "ipC"
        for c0 in (3600, 4112):
            inproj_group(c0, 0)
        K.tag = "ipA"
        for c0 in (0, 512, 1024, 1536, 2048):
            inproj_group(c0, 0)
        K.tag = "ipD"
        for c0 in (4624, 5136, 5648):
            inproj_group(c0, 0)
        K.stop_record()
        pslo[0] = False
        if l == 0:
            K.aux += mod_queue
            K.auxtag += ["mod"] * len(mod_queue)

        MIX(l)
        K.flush()
        K.fence()
        K.dma("sp", hT, mixd.rearrange("(c p) t -> p c t", p=128), reads=["mixd"], writes=["hT"])
        if debug:
            dbg[f"mixT{l}"] = 1
            K.dma("pool", dbg_mix[l].rearrange("(c p) t -> p c t", p=128), hT, reads=["hT"], writes=["dbgmix"])
            K.fence()

        wv = w_out[l].rearrange("(kc p) n -> p kc n", p=128)
        for tb in range(3):
            K.dma("sp", xres[:, :, tb * TB:(tb + 1) * TB], xs[:, tb * TB:(tb + 1) * TB].rearrange("(c p) t -> p c t", p=128),
                  reads=["xs"], writes=["xblk"])
        for g in range(4):
            i = load_w(wv[:, :, g * 512:(g + 1) * 512], 512)
            for j in range(4):
                oc = g * 4 + j
                for tb in range(3):
                    t0 = tb * TB
                    cj = 0 if tb == 0 else 1
                    b = nps()
                    for kc in range(16):
                        K.op("pe", lambda e: e.matmul(PS[b], lhsT=wbuf[i][:, kc, j * 128:(j + 1) * 128],
                                                      rhs=hT[:, kc, t0:t0 + TB], start=(kc == 0), stop=(kc == 15)),
                             reads=[f"wbuf{i}", "hT"], writes=[psk[b]])
                    K.op("dve", lambda e: e.scalar_tensor_tensor(out=xres[:, oc, t0:t0 + TB], in0=PS[b], scalar=modT[:, 32 + oc, cj:cj + 1],
                                                                 in1=xres[:, oc, t0:t0 + TB], op0=ALU.mult, op1=ALU.add),
                         reads=[psk[b], mk, "xblk"], writes=["xblk"])
        if debug:
            for tb in range(3):
                K.dma("sp", xs[:, tb * TB:(tb + 1) * TB].rearrange("(c p) t -> p c t", p=128), xres[:, :, tb * TB:(tb + 1) * TB],
                      reads=["xblk"], writes=["xs"])
        for tb in range(3):
            t0, cj, _ = norm_block(l, 1, tb, True)
            for c in range(16):
                K.op("dve", lambda e: e.tensor_tensor(out=tmpf, in0=xres[:, c, t0:t0 + TB], in1=rstd, op=ALU.mult),
                     reads=["xblk", "rstd"], writes=["tmpf"])
                K.op("act", lambda e: e.activation(out=hT[:, c, t0:t0 + TB], in_=tmpf, func=AF.Identity,
                                                   scale=nscale[:, 1, c, cj:cj + 1], bias=modT[:, 48 + c, cj:cj + 1]),
                     reads=["tmpf", nk_, mk], writes=["hT"])
        if debug == 2:
            break
        K.fence()

        for r in range(3):
            load_colT(fcw[:, r * 88:(r + 1) * 88], f_cw[l, r].rearrange("(c p) -> c p", p=128), 88, "fcw")
        load_colT(fcb, f_cb[l].rearrange("(c p) -> c p", p=128), 88, "fcb")
        wvu = w_up[l].rearrange("(kc p) n -> p kc n", p=128)
        urr = 0
        dpar = 0
        for g in range(22):
            i = wrr[0]
            wrr[0] ^= 1
            K.dma("pool", wbuf[i][:, :, 0:256], wvu[:, :, g * 256:(g + 1) * 256], writes=[f"wbuf{i}"])
            K.dma("pool", wbuf[i][:, :, 256:512], wvu[:, :, D_FF + g * 256:D_FF + (g + 1) * 256], writes=[f"wbuf{i}"])
            K.dma("pool", wdn, w_dn[l, g * 256:(g + 1) * 256, :].rearrange("(k p) n -> p k n", p=128), writes=["wdn"])
            for jj in range(2):
                j = g * 2 + jj
                for half in range(2):
                    fch = j + 44 * half
                    wc = half * 256 + jj * 128
                    bset = [0, 1, 2] if half == 0 else [3, 4, 5]
                    for tb in range(3):
                        b = bset[tb]
                        for kc in range(16):
                            K.op("pe", lambda e: e.matmul(PS[b], lhsT=wbuf[i][:, kc, wc:wc + 128], rhs=hT[:, kc, tb * TB:(tb + 1) * TB],
                                                          start=(kc == 0), stop=(kc == 15)),
                                 reads=[f"wbuf{i}", "hT"], writes=[psk[b]])
                    for tb in range(3):
                        b = bset[tb]
                        t0 = tb * TB
                        u = ubuf[urr]
                        uk = f"ubuf{urr}"
                        urr ^= 1
                        ranges = [(0, 256), (256, 512)] if tb == 0 else [(0, 512)]
                        K.op("act", lambda e: e.activation(out=u, in_=PS[b], func=AF.Identity, scale=fcw[:, 88 + fch:88 + fch + 1],
                                                           bias=fcb[:, fch:fch + 1]),
                             reads=[psk[b], "fcw", "fcb"], writes=[uk])
                        for (ra, rb) in ranges:
                            K.op("dve", lambda e: e.scalar_tensor_tensor(out=u[:, ra + 1:rb], in0=PS[b][:, ra:rb - 1],
                                                                         scalar=fcw[:, fch:fch + 1], in1=u[:, ra + 1:rb],
                                                                         op0=ALU.mult, op1=ALU.add),
                                 reads=[psk[b], "fcw", uk], writes=[uk])
                            K.op("dve", lambda e: e.scalar_tensor_tensor(out=u[:, ra:rb - 1], in0=PS[b][:, ra + 1:rb],
                                                                         scalar=fcw[:, 176 + fch:176 + fch + 1], in1=u[:, ra:rb - 1],
                                                                         op0=ALU.mult, op1=ALU.add),
                                 reads=[psk[b], "fcw", uk], writes=[uk])
                        if tb == 1:
                            bn_ = bset[2]
                            K.op("dve", lambda e: e.scalar_tensor_tensor(out=u[:, TB - 1:TB], in0=PS[bn_][:, 0:1],
                                                                         scalar=fcw[:, 176 + fch:176 + fch + 1], in1=u[:, TB - 1:TB],
                                                                         op0=ALU.mult, op1=ALU.add),
                                 reads=[psk[bn_], "fcw", uk], writes=[uk])
                        if tb == 2:
                            bp_ = bset[1]
                            K.op("dve", lambda e: e.scalar_tensor_tensor(out=u[:, 0:1], in0=PS[bp_][:, TB - 1:TB],
                                                                         scalar=fcw[:, fch:fch + 1], in1=u[:, 0:1],
                                                                         op0=ALU.mult, op1=ALU.add),
                                 reads=[psk[bp_], "fcw", uk], writes=[uk])
                        if half == 0:
                            K.op("act", lambda e: e.activation(out=actTg[:, jj, t0:t0 + TB], in_=u, func=AF.Silu), reads=[uk], writes=["actTg"])
                        else:
                            K.op("pool", lambda e: e.tensor_tensor(out=actTg[:, jj, t0:t0 + TB], in0=actTg[:, jj, t0:t0 + TB], in1=u, op=ALU.mult),
                                 reads=[uk, "actTg"], writes=["actTg"])
            for oc in range(16):
                for tb in range(3):
                    t0 = tb * TB
                    cj = 0 if tb == 0 else 1
                    b = 6 + dpar
                    dpar ^= 1
                    for kc in range(2):
                        K.op("pe", lambda e: e.matmul(PS[b], lhsT=wdn[:, kc, oc * 128:(oc + 1) * 128], rhs=actTg[:, kc, t0:t0 + TB],
                                                      start=(kc == 0), stop=(kc == 1)),
                             reads=["wdn", "actTg"], writes=[psk[b]])
                    K.op("dve", lambda e: e.scalar_tensor_tensor(out=xres[:, oc, t0:t0 + TB], in0=PS[b], scalar=modT[:, 80 + oc, cj:cj + 1],
                                                                 in1=xres[:, oc, t0:t0 + TB], op0=ALU.mult, op1=ALU.add),
                         reads=[psk[b], mk, "xblk"], writes=["xblk"])
        K.fence()
        if l < DEPTH - 1 or debug:
            for tb in range(3):
                K.dma("sp", xs[:, tb * TB:(tb + 1) * TB].rearrange("(c p) t -> p c t", p=128), xres[:, :, tb * TB:(tb + 1) * TB],
                      reads=["xblk"], writes=["xs"])
        if l == DEPTH - 1:
            for tq in range(NT // 128):
                for g in range(4):
                    b = nps()
                    for jq in range(4):
                        fc = g * 4 + jq
                        K.op("pe", lambda e: e.transpose(PS[b][:, jq * 128:(jq + 1) * 128], xres[:, fc, tq * 128:(tq + 1) * 128], ident),
                             reads=["xblk", "ident"], writes=[psk[b]])
                    evac(ytile[:, g * 512:(g + 1) * 512], PS[b], [psk[b]], ["ytile"])
                K.dma("sp", y_out[tq * 128:(tq + 1) * 128, :], ytile, reads=["ytile"], writes=["y_out"])
        K.fence()

    K.finish()
    return nc, dbg


def host_consts():
    c = {}
    c["k_ident"] = np.eye(128, dtype=np.float32)
    sel = np.zeros((8, 8, 128), np.float32)
    for p in range(8):
        sel[p, p, :] = 1.0
    c["k_sel"] = sel.reshape(8, 8 * 128)
    s = np.arange(128)[:, None]
    t = np.arange(128)[None, :]
    c["k_tri"] = np.concatenate([(s <= t), (s >= t)], axis=1).astype(np.float32)
    tq = np.arange(384)[None, :] - 128
    c["k_band"] = (np.abs(s - tq) <= 128).astype(np.float32)
    for dh, nm in ((64, "k_rope64"), (128, "k_rope128")):
        n_tok = 1024
        t_row = np.repeat(np.arange(n_tok // 64, dtype=np.float32), 64)
        t_col = np.tile(np.arange(64, dtype=np.float32), n_tok // 64)
        n_freq = dh // 4
        inv = (10000.0 ** (-np.arange(n_freq, dtype=np.float32) / n_freq)).astype(np.float32)
        ang = np.concatenate([t_row[:, None] * inv, t_col[:, None] * inv], axis=-1).astype(np.float32)
        c[nm] = np.concatenate([np.cos(ang), np.sin(ang)], axis=1).astype(np.float32)
    for L in (256, 1024):
        tt = np.linspace(0.0, 1.0, L, dtype=np.float32)[:, None]
        w = (np.float32(2.0 * math.pi / L) * np.arange(L, dtype=np.float32))[:, None]
        bands = np.linspace(1e-4, 15, 16, dtype=np.float32)[None, :]
        feats = np.concatenate([tt, np.cos(bands * w), -np.sin(bands * w), np.ones((L, 1), np.float32)], axis=-1)
        c[f"k_feat{L}"] = np.ascontiguousarray(feats.T).astype(np.float32)
        rates = np.abs(np.linspace(HY_FAST, HY_SLOW, 512, dtype=np.float32))
        c[f"k_win{L}"] = np.exp(-tt * rates).astype(np.float32)
        f = np.arange(L, dtype=np.float64)[:, None] + 0.5
        tt64 = np.arange(L, dtype=np.float64)[None, :]
        th = np.pi * f * tt64 / L
        C = np.cos(th)
        S = np.sin(th)
        c[f"k_dft{L}"] = np.stack([C.T, S.T, C, S]).astype(np.float32)
    return c


def make_in_maps(inputs, n_cores=8):
    consts = host_consts()
    maps = []
    f = lambda a: np.ascontiguousarray(a, dtype=np.float32)
    for c in range(n_cores):
        b = c % 4
        m = dict(consts)
        m["xin"] = f(np.concatenate([inputs["x_prompt"][2 * c], inputs["x_prompt"][2 * c + 1], inputs["x_sample"][b]], axis=0))
        m["cond2"] = f(np.stack([inputs["c_ctx"], inputs["c"][b]]))
        m["c_dk"] = f(inputs["cache_diff_k"][b].reshape(DEPTH, PAST, 512))
        m["c_dv"] = f(inputs["cache_diff_v"][b].reshape(DEPTH, PAST, 512))
        m["c_sk"] = f(inputs["cache_swa_k"][b].reshape(DEPTH, PAST, 256))
        m["c_sv"] = f(inputs["cache_swa_v"][b].reshape(DEPTH, PAST, 256))
        m["st_C"] = f(inputs["state_mlstm_C"][b])
        m["st_n"] = f(inputs["state_mlstm_n"][b])
        m["st_m"] = f(inputs["state_mlstm_m"][b].reshape(DEPTH, 8))
        for k in ("w_mod", "b_mod", "norm1_g", "norm2_g", "w_in", "mlstm_norm_g", "diff_qn_g", "diff_kn_g", "diff_lam",
                  "diff_out_g", "swa_qn_g", "swa_kn_g", "swa_sink", "hy_conv_w", "hy_conv_b", "hy_w1", "hy_b1", "hy_freq",
                  "hy_w2", "hy_b2", "hy_w3", "hy_b3", "hy_bias", "w_out", "ffn_w_up", "ffn_conv_w", "ffn_conv_b", "ffn_w_down"):
            m[k] = f(inputs[k])
        m["mlstm_ig_b"] = f(inputs["mlstm_ig_b"].reshape(DEPTH, 8))
        m["mlstm_fg_b"] = f(inputs["mlstm_fg_b"].reshape(DEPTH, 8))
        maps.append(m)
    return maps


_CACHE = {}


def kernel(**inputs):
    inputs = {k: np.asarray(v) for k, v in inputs.items()}
    if "nc" not in _CACHE:
        _CACHE["nc"] = build(debug=False)[0]
    nc = _CACHE["nc"]
    maps = make_in_maps(inputs, 8)
    res = run_bass_kernel_spmd(nc, maps, core_ids=list(range(8)))
    R = res.results
    y_prompt = np.zeros((16, 256, D), np.float32)
    y_sample = np.zeros((4, 1024, D), np.float32)
    ndk = np.zeros((16, DEPTH, 256, 4, 2, 64), np.float32)
    ndv = np.zeros((16, DEPTH, 256, 4, 128), np.float32)
    nsk = np.zeros((16, DEPTH, 256, 2, 128), np.float32)
    nsv = np.zeros((16, DEPTH, 256, 2, 128), np.float32)
    nC = np.zeros((16, DEPTH, 2, 4, 128, 128), np.float32)
    nn = np.zeros((16, DEPTH, 2, 4, 128), np.float32)
    nm = np.zeros((16, DEPTH, 2, 4), np.float32)
    for c in range(8):
        r = R[c]
        y = np.asarray(r["y_out"])
        y_prompt[2 * c] = y[0:256]
        y_prompt[2 * c + 1] = y[256:512]
        if c < 4:
            y_sample[c] = y[512:1536]
        for s in range(2):
            b = 2 * c + s
            ndk[b] = np.asarray(r["o_dk"])[s].reshape(DEPTH, 256, 4, 2, 64)
            ndv[b] = np.asarray(r["o_dv"])[s].reshape(DEPTH, 256, 4, 128)
            nsk[b] = np.asarray(r["o_sk"])[s].reshape(DEPTH, 256, 2, 128)
            nsv[b] = np.asarray(r["o_sv"])[s].reshape(DEPTH, 256, 2, 128)
            nC[b] = np.asarray(r["o_C"])[s]
            nn[b] = np.asarray(r["o_n"])[s]
            nm[b] = np.asarray(r["o_m"])[s].reshape(DEPTH, 2, 4)
    return (y_prompt, y_sample, ndk, ndv, nsk, nsv, nC, nn, nm)
```

```python
import math
import numpy as np
import concourse.bass as bass
import concourse.mybir as mybir
from concourse.bass_utils import run_bass_kernel_spmd

F32 = mybir.dt.float32
BF16 = mybir.dt.bfloat16
I32 = mybir.dt.int32
AF = mybir.ActivationFunctionType
ALU = mybir.AluOpType
AX = mybir.AxisListType

D = 2048
DEPTH = 2
NT = 1536
TB = 512
SEGS = [(0, 256, False), (256, 256, False), (512, 1024, True)]
N_IN = 6160
D_FF = 5632
MIXSEL = "ABCD"
EPS = 1e-6
PAST = 512
HY_FAST = math.log(1e-2) / 0.3
HY_SLOW = math.log(1e-2) / 1.5


class _Rec:
    def __init__(self):
        self.call = None

    def __getattr__(self, name):
        def f(*a, **kw):
            self.call = (name, a, kw)
            return self
        return f


class KB:
    def __init__(self, nc):
        self.nc = nc
        self.eng = {"pe": nc.tensor, "act": nc.scalar, "dve": nc.vector, "pool": nc.gpsimd, "sp": nc.sync}
        self.csem = {e: nc.alloc_semaphore("cs_" + e) for e in ("pe", "act", "dve", "pool")}
        self.ccnt = {e: 0 for e in self.csem}
        self.dsem = {q: [[nc.alloc_semaphore(f"ds_{q}{i}"), 0] for i in range(n)] for q, n in (("sp", 12), ("pool", 8))}
        self.dnext = {"sp": 0, "pool": 0}
        self.seen = {e: {} for e in self.eng}
        self.lastw = {}
        self.readers = {}
        self.nsb = 0
        self.sems = {}
        self.rec = None
        self.aux = []

    def sb(self, shape, dt, name=None):
        self.nsb += 1
        return self.nc.alloc_sbuf_tensor(name or f"sb{self.nsb}", list(shape), dt).ap()

    def _need(self, stream, tk):
        if tk is None:
            return
        sem, val = tk
        key = id(sem)
        self.sems[key] = sem
        if self.seen[stream].get(key, 0) >= val:
            return
        self.eng[stream].wait_ge(sem, val)
        self.seen[stream][key] = val

    def _deps(self, stream, reads, writes, pe_skip=False):
        for k in reads:
            tk = self.lastw.get(k)
            if tk is not None and not (pe_skip and tk[0] is self.csem["pe"]):
                self._need(stream, tk)
        for k in writes:
            tk = self.lastw.get(k)
            if tk is not None and not (pe_skip and tk[0] is self.csem["pe"]):
                self._need(stream, tk)
            for tk in self.readers.get(k, {}).values():
                if not (pe_skip and tk[0] is self.csem["pe"]):
                    self._need(stream, tk)

    def _record(self, tk, reads, writes):
        for k in reads:
            d = self.readers.setdefault(k, {})
            old = d.get(id(tk[0]))
            if old is None or old[1] < tk[1]:
                d[id(tk[0])] = tk
        for k in writes:
            self.lastw[k] = tk
            self.readers[k] = {}

    def op(self, e, fn, reads=(), writes=()):
        if self.rec is not None:
            r = _Rec()
            fn(r)
            self.rec.append(("op", e, r.call, list(reads), list(writes)))
            return None
        self._deps(e, reads, writes, pe_skip=(e == "pe"))
        ins = fn(self.eng[e])
        self.ccnt[e] += 1
        ins.then_inc(self.csem[e], 1)
        tk = (self.csem[e], self.ccnt[e])
        self._record(tk, reads, writes)
        return tk

    def dma(self, q, out, in_, reads=(), writes=(), **kw):
        if self.rec is not None:
            self.rec.append(("dma", q, out, in_, list(reads), list(writes), kw))
            return None
        self._deps(q, reads, writes)
        i = self.dnext[q]
        self.dnext[q] = (i + 1) % len(self.dsem[q])
        slot = self.dsem[q][i]
        if slot[1] > 0:
            self._need(q, (slot[0], slot[1]))
        ins = self.eng[q].dma_start(out=out, in_=in_, **kw)
        slot[1] += 16
        ins.then_inc(slot[0], 16)
        tk = (slot[0], slot[1])
        self._record(tk, reads, writes)
        return tk

    def record(self):
        self.rec = []

    def end_atom(self):
        if self.rec:
            self.aux.append(self.rec)
        self.rec = []

    def stop_record(self):
        self.end_atom()
        self.rec = None

    def tick(self, n=1):
        if self.rec is not None:
            return
        for _ in range(n):
            if not self.aux:
                return
            atom = self.aux.pop(0)
            for it in atom:
                if it[0] == "op":
                    _, e, (name, a, kw), reads, writes = it
                    self.op(e, lambda eng: getattr(eng, name)(*a, **kw), reads, writes)
                else:
                    _, q, out, in_, reads, writes, kw = it
                    self.dma(q, out, in_, reads, writes, **kw)

    def flush(self):
        while self.aux:
            self.tick()

    def fence(self):
        assert self.rec is None
        tks = [(self.csem[e], self.ccnt[e]) for e in self.csem if self.ccnt[e] > 0]
        for q in self.dsem:
            tks += [(sem, val) for sem, val in self.dsem[q] if val > 0]
        for st in self.eng:
            for tk in tks:
                self._need(st, tk)

    def finish(self):
        for q in self.dsem:
            for sem, val in self.dsem[q]:
                if val > 0:
                    self._need("sp", (sem, val))


def build(debug=False):
    nc = bass.Bass("TRN2", target_bir_lowering=False)
    K = KB(nc)
    dbg = {}

    def din(name, shape):
        return nc.dram_tensor(name, list(shape), F32, kind="ExternalInput").ap()

    def dout(name, shape):
        return nc.dram_tensor(name, list(shape), F32, kind="ExternalOutput").ap()

    def dscr(name, shape, dt=F32):
        if debug:
            return nc.dram_tensor(name, list(shape), dt, kind="ExternalOutput").ap()
        return nc.dram_tensor(name, list(shape), dt).ap()

    xin = din("xin", [NT, D])
    cond2 = din("cond2", [2, D])
    c_dk = din("c_dk", [DEPTH, PAST, 512])
    c_dv = din("c_dv", [DEPTH, PAST, 512])
    c_sk = din("c_sk", [DEPTH, PAST, 256])
    c_sv = din("c_sv", [DEPTH, PAST, 256])
    st_C = din("st_C", [DEPTH, 2, 4, 128, 128])
    st_n = din("st_n", [DEPTH, 2, 4, 128])
    st_m = din("st_m", [DEPTH, 8])
    w_mod = din("w_mod", [DEPTH, D, 6 * D])
    b_mod = din("b_mod", [DEPTH, 6 * D])
    norm1_g = din("norm1_g", [DEPTH, D])
    norm2_g = din("norm2_g", [DEPTH, D])
    w_in = din("w_in", [DEPTH, D, N_IN])
    ig_b = din("mlstm_ig_b", [DEPTH, 8])
    fg_b = din("mlstm_fg_b", [DEPTH, 8])
    mnorm_g = din("mlstm_norm_g", [DEPTH, 512])
    dqn_g = din("diff_qn_g", [DEPTH, 64])
    dkn_g = din("diff_kn_g", [DEPTH, 64])
    dlam = din("diff_lam", [DEPTH, 4, 64])
    dout_g = din("diff_out_g", [DEPTH, 128])
    sqn_g = din("swa_qn_g", [DEPTH, 128])
    skn_g = din("swa_kn_g", [DEPTH, 128])
    ssink = din("swa_sink", [DEPTH, 4])
    hy_cw = din("hy_conv_w", [DEPTH, 3, 1536])
    hy_cb = din("hy_conv_b", [DEPTH, 1536])
    hy_w1 = din("hy_w1", [DEPTH, 33, 64])
    hy_b1 = din("hy_b1", [DEPTH, 64])
    hy_fr = din("hy_freq", [DEPTH, 64])
    hy_w2 = din("hy_w2", [DEPTH, 64, 64])
    hy_b2 = din("hy_b2", [DEPTH, 64])
    hy_w3 = din("hy_w3", [DEPTH, 64, 2048])
    hy_b3 = din("hy_b3", [DEPTH, 2048])
    hy_bias = din("hy_bias", [DEPTH, 2, 512])
    w_out = din("w_out", [DEPTH, D, D])
    w_up = din("ffn_w_up", [DEPTH, D, 2 * D_FF])
    f_cw = din("ffn_conv_w", [DEPTH, 3, 2 * D_FF])
    f_cb = din("ffn_conv_b", [DEPTH, 2 * D_FF])
    w_dn = din("ffn_w_down", [DEPTH, D_FF, D])
    k_ident = din("k_ident", [128, 128])
    k_sel = din("k_sel", [8, 8 * 128])
    k_tri = din("k_tri", [128, 256])
    k_band = din("k_band", [128, 384])
    k_rope64 = din("k_rope64", [1024, 64])
    k_rope128 = din("k_rope128", [1024, 128])
    k_feat = {L: din(f"k_feat{L}", [34, L]) for L in (256, 1024)}
    k_win = {L: din(f"k_win{L}", [L, 512]) for L in (256, 1024)}
    k_dft = {L: din(f"k_dft{L}", [4, L, L]) for L in (256, 1024)}

    y_out = dout("y_out", [NT, D])
    o_dk = dout("o_dk", [2, DEPTH, 256, 512])
    o_dv = dout("o_dv", [2, DEPTH, 256, 512])
    o_sk = dout("o_sk", [2, DEPTH, 256, 256])
    o_sv = dout("o_sv", [2, DEPTH, 256, 256])
    o_C = dout("o_C", [2, DEPTH, 2, 4, 128, 128])
    o_n = dout("o_n", [2, DEPTH, 2, 4, 128])
    o_m = dout("o_m", [2, DEPTH, 8])

    xs = dscr("xs", [D, NT])
    zAT = dscr("zAT", [1552, NT])
    zDT = dscr("zDT", [1536, NT])
    ztm = dscr("ztm", [NT, 3584])
    ZC = dict(Av=0, Ak=512, Bq=1024, Bk=1536, Bv=2048, Cq=2560, Ck=3072, Cv=3328)
    hyG = {L: dscr(f"hyG{L}", [2, 2, L, 512]) for L in (256, 1024)}

    ident = K.sb([128, 128], F32, "ident")
    onesb = K.sb([128, 128], BF16, "onesb")
    sel = K.sb([4, 4 * 128], F32, "sel")
    tri = K.sb([128, 256], BF16, "tri")
    band = K.sb([128, 384], BF16, "band")
    hT = K.sb([128, 16, NT], BF16, "hT")
    PS = [nc.alloc_psum_tensor(f"ps{i}", [128, 512], F32).ap() for i in range(8)]
    psk = [("ps", i) for i in range(8)]
    psi = [0]

    def nps():
        i = psi[0]
        psi[0] = (i + 1) % 8
        return i

    K.dma("sp", ident, k_ident, writes=["ident"])
    K.dma("sp", sel, k_sel[0:4, 0:512], writes=["sel"])
    K.dma("pool", tri, k_tri, writes=["tri"])
    K.dma("pool", band, k_band, writes=["band"])
    K.op("dve", lambda e: e.memset(onesb, 1.0), writes=["onesb"])

    cpy_rr = [0]

    def evac(out, in_, reads, writes):
        cpy_rr[0] ^= 1
        if cpy_rr[0]:
            return K.op("act", lambda e: e.activation(out=out, in_=in_, func=AF.Copy), reads=reads, writes=writes)
        return K.op("dve", lambda e: e.tensor_copy(out=out, in_=in_), reads=reads, writes=writes)

    def evac_act(out, in_, reads, writes):
        return K.op("act", lambda e: e.activation(out=out, in_=in_, func=AF.Copy), reads=reads, writes=writes)

    vtmp = K.sb([128, 128], F32, "vtmp")

    def load_colT(dst, src2d, nrows, key):
        K.dma("sp", vtmp[0:nrows, :], src2d, writes=["vtmp"])
        b = nps()
        K.op("pe", lambda e: e.transpose(PS[b][:, 0:nrows], vtmp[0:nrows, :], ident[0:nrows, 0:nrows]),
             reads=["vtmp", "ident"], writes=[psk[b]])
        K.op("dve", lambda e: e.tensor_copy(out=dst, in_=PS[b][:, 0:nrows]), reads=[psk[b]], writes=[key])

    arena = K.sb([128, 24576], F32, "arena")
    xres = arena.rearrange("p (c t) -> p c t", c=16)
    xblk = arena[:, 0:8192].rearrange("p (c t) -> p c t", c=16)
    xt = [arena[:, 0:2048], arena[:, 2048:4096]]
    arena2 = K.sb([128, 4608], F32, "arena2")
    wdn = arena2[:, 0:2048].bitcast(BF16).rearrange("p (k n) -> p k n", k=2)
    actTg = arena2[:, 2048:3584].bitcast(BF16).rearrange("p (k t) -> p k t", k=2)
    ubuf = [arena2[:, 3584:4096], arena2[:, 4096:4608]]
    ytile = arena2[:, 0:2048]
    stage = [arena2[:, i * 512:(i + 1) * 512] for i in range(4)]
    xo = [stage[2].rearrange("p (a b) -> p a b", a=4), stage[3].rearrange("p (a b) -> p a b", a=4)]
    for ti in range(NT // 128):
        a = ti % 2
        K.dma("sp", xt[a], xin[ti * 128:(ti + 1) * 128, :], writes=[f"xt{a}"])
        for g in range(4):
            b = nps()
            for j in range(4):
                fc = g * 4 + j
                K.op("pe", lambda e: e.transpose(PS[b][:, j * 128:(j + 1) * 128], xt[a][:, fc * 128:(fc + 1) * 128], ident),
                     reads=[f"xt{a}", "ident"], writes=[psk[b]])
            o = (ti * 4 + g) % 2
            evac(xo[o].rearrange("p a b -> p (a b)"), PS[b], [psk[b]], [f"stage{2 + o}"])
            K.dma("sp", xs[g * 512:(g + 1) * 512, ti * 128:(ti + 1) * 128].rearrange("(j p) t -> p j t", p=128),
                  xo[o], reads=[f"stage{2 + o}"], writes=["xs"])

    K.fence()
    wbuf = [K.sb([128, 16, 512], BF16, f"wbuf{i}") for i in range(2)]
    wrr = [0]
    srr = [0]
    modTs = [K.sb([128, 96, 2], F32, f"modT{i}") for i in range(DEPTH)]
    bmTs = [K.sb([128, 96], F32, f"bmT{i}") for i in range(DEPTH)]
    n1Ts = [K.sb([128, 16], F32, f"n1T{i}") for i in range(DEPTH)]
    n2Ts = [K.sb([128, 16], F32, f"n2T{i}") for i in range(DEPTH)]
    scf = K.sb([128, 32], F32, "scf")
    scb = K.sb([128, 32], BF16, "scb")
    nscales = [K.sb([128, 2, 16, 2], F32, f"nscale{i}") for i in range(DEPTH)]
    rstd = arena2[:, 2048:2560]
    tmpf = arena2[:, 2560:3072]
    sqb = arena2[:, 3072:3328].bitcast(BF16)

    load_colT(scf, cond2.rearrange("j (kc p) -> (j kc) p", p=128), 32, "scf")
    K.op("act", lambda e: e.activation(out=scb, in_=scf, func=AF.Silu), reads=["scf"], writes=["scb"])

    def load_w(src_fn, ncols):
        i = wrr[0]
        wrr[0] ^= 1
        K.dma("pool", wbuf[i][:, :, 0:ncols], src_fn, writes=[f"wbuf{i}"])
        return i

    def norm_block(l, which, tb, src_is_sbuf):
        t0 = tb * TB
        cj = 0 if tb == 0 else 1
        if not src_is_sbuf:
            K.dma("sp", xblk, xs[:, t0:t0 + TB].rearrange("(c p) t -> p c t", p=128), reads=["xs"], writes=["xblk"])
        b = nps()
        for c in range(16):
            xsrc = xres[:, c, t0:t0 + TB] if src_is_sbuf else xblk[:, c, :]
            K.op("act", lambda e: e.activation(out=sqb, in_=xsrc, func=AF.Square), reads=["xblk"], writes=["sqb"])
            K.op("pe", lambda e: e.matmul(PS[b], lhsT=onesb, rhs=sqb, start=(c == 0), stop=(c == 15)),
                 reads=["sqb", "onesb"], writes=[psk[b]])
        K.op("act", lambda e: e.activation(out=rstd, in_=PS[b], func=AF.Sqrt, scale=1.0 / D, bias=epsT[:, 0:1]),
             reads=[psk[b], "epsT"], writes=["rstd"])
        K.op("dve", lambda e: e.reciprocal(out=rstd, in_=rstd), reads=["rstd"], writes=["rstd"])
        sh = 0 if which == 0 else 3
        return t0, cj, sh

    fcw = K.sb([128, 264], F32, "fcw")
    fcb = K.sb([128, 88], F32, "fcb")
    if debug:
        dbg_mix = [dout(f"dbg_mix{i}", [D, NT]) for i in range(DEPTH)]
        dbg_h2 = [dout(f"dbg_h2{i}", [D, NT]) for i in range(DEPTH)]
    epsT = K.sb([128, 1], F32, "epsT")
    K.op("dve", lambda e: e.memset(epsT, EPS), writes=["epsT"])

    def aw(off, n):
        return arena[:, off:off + n]

    def awb(off, n):
        return arena[:, off:off + n].bitcast(BF16)

    lam_t = arena[:, 24320:24576]
    lam_s = K.sb([128, 4], F32, "lam_s")
    og_s = K.sb([128, 2], F32, "og_s")
    esink = K.sb([128, 4], F32, "esink")
    mng = K.sb([128, 4], F32, "mng")

    def attn_core(l, t0, T, is_s, which):
        nq_groups = 16 if which == "B" else 6
        gsz = 64 if which == "B" else 128
        qk_cols = 1024 if which == "B" else 768
        zq = ZC["Bq"] if which == "B" else ZC["Cq"]
        zv = ZC["Bv"] if which == "B" else ZC["Cv"]
        vw = 512 if which == "B" else 256
        nh_k = 4 if which == "B" else 2
        qkraw_ = [aw(0, 1024), aw(14848, 1024)]; qkn_ = [aw(1024, 1024), aw(15872, 1024)]
        sq_ = [aw(2048, 1024), aw(16896, 1024)]; tmp2_ = [aw(3072, 1024), aw(17920, 1024)]
        qT = awb(4096, 2048).rearrange("p (h t) -> p h t", h=4)
        kT = awb(6144, 3072).rearrange("p (h t) -> p h t", h=4)
        vv = awb(9216, 3072).rearrange("p (c n) -> p c n", c=12)
        ebuf = [awb(12288, 256), awb(12544, 256)]
        rden = aw(12800, 512); acc = aw(13312, 512); o1 = aw(13824, 512)
        ss = aw(14336, 16); ropet = aw(14352, 128); gq = aw(14592, 128); gk = aw(14720, 128)
        nctx = 4 if is_s else 0
        nk = nctx + T // 128
        ntc = T // 128
        gains = (dqn_g, dkn_g) if which == "B" else (sqn_g, skn_g)
        K.dma("sp", gq[:, 0:gsz], gains[0][l:l + 1, :].partition_broadcast(128), writes=["gq"])
        K.dma("sp", gk[:, 0:gsz], gains[1][l:l + 1, :].partition_broadcast(128), writes=["gk"])
        ckt = (c_dk if which == "B" else c_sk)
        cvt = (c_dv if which == "B" else c_sv)
        for j in range(nctx):
            K.dma("pool", vv[:, j, 0:vw], cvt[l, j * 128:(j + 1) * 128, :], writes=["vv"])
        for j in range(ntc):
            K.dma("pool", vv[:, nctx + j, 0:vw], ztm[t0 + j * 128:t0 + (j + 1) * 128, zv:zv + vw], reads=["ztm"], writes=["vv"])
        for j in range(nctx):
            qkraw = qkraw_[j % 2]
            kq = f"qkraw{j % 2}"
            K.dma("sp", qkraw[:, 0:vw], ckt[l, j * 128:(j + 1) * 128, :], writes=[kq])
            b = nps()
            for h in range(nh_k):
                K.op("pe", lambda e: e.transpose(PS[b][:, h * 128:(h + 1) * 128], qkraw[:, h * 128:(h + 1) * 128], ident),
                     reads=[kq, "ident"], writes=[psk[b]])
            evac_act(kT[:, 0:nh_k, j * 128:(j + 1) * 128], PS[b][:, 0:nh_k * 128].rearrange("p (h t) -> p h t", h=nh_k), [psk[b]], ["kT"])
        nqg = 8 if which == "B" else 4
        for tc in range(ntc):
            r0 = t0 + tc * 128
            pp = tc % 2
            qkraw, qkn, sq, tmp2 = qkraw_[pp], qkn_[pp], sq_[pp], tmp2_[pp]
            kq, kn_, ksq, kt2 = f"qkraw{pp}", f"qkn{pp}", f"sq{pp}", f"tmp2{pp}"
            K.dma("sp", qkraw[:, 0:qk_cols], ztm[r0:r0 + 128, zq:zq + qk_cols], reads=["ztm"], writes=[kq])
            K.op("dve", lambda e: e.tensor_tensor(out=sq[:, 0:qk_cols], in0=qkraw[:, 0:qk_cols], in1=qkraw[:, 0:qk_cols], op=ALU.mult),
                 reads=[kq], writes=[ksq])
            K.op("dve", lambda e: e.reduce_sum(out=ss[:, 0:nq_groups], in_=sq[:, 0:qk_cols].rearrange("p (g d) -> p g d", d=gsz), axis=AX.X),
                 reads=[ksq], writes=["ss"])
            K.op("act", lambda e: e.activation(out=ss[:, 0:nq_groups], in_=ss[:, 0:nq_groups], func=AF.Sqrt, scale=1.0 / gsz, bias=epsT[:, 0:1]),
                 reads=["ss", "epsT"], writes=["ss"])
            K.op("dve", lambda e: e.reciprocal(out=ss[:, 0:nq_groups], in_=ss[:, 0:nq_groups]), reads=["ss"], writes=["ss"])
            K.op("dve", lambda e: e.tensor_tensor(out=qkn[:, 0:qk_cols].rearrange("p (g d) -> p g d", d=gsz),
                                                  in0=qkraw[:, 0:qk_cols].rearrange("p (g d) -> p g d", d=gsz),
                                                  in1=ss[:, 0:nq_groups].unsqueeze(2).to_broadcast([128, nq_groups, gsz]), op=ALU.mult),
                 reads=[kq, "ss"], writes=[kn_])
            qv = qkn[:, 0:nqg * gsz].rearrange("p (g d) -> p g d", d=gsz)
            kv_ = qkn[:, nqg * gsz:qk_cols].rearrange("p (g d) -> p g d", d=gsz)
            K.op("dve", lambda e: e.tensor_tensor(out=qv, in0=qv, in1=gq[:, 0:gsz].unsqueeze(1).to_broadcast([128, nqg, gsz]), op=ALU.mult),
                 reads=[kn_, "gq"], writes=[kn_])
            K.op("dve", lambda e: e.tensor_tensor(out=kv_, in0=kv_, in1=gk[:, 0:gsz].unsqueeze(1).to_broadcast([128, nq_groups - nqg, gsz]), op=ALU.mult),
                 reads=[kn_, "gk"], writes=[kn_])
            if not is_s:
                seq = 0 if t0 == 0 else 1
                odst = (o_dk if which == "B" else o_sk)
                K.dma("sp", odst[seq, l, tc * 128:(tc + 1) * 128, :], qkn[:, nqg * gsz:qk_cols], reads=[kn_], writes=["okv"])
                src = qkn
            else:
                hs = gsz // 2
                rt = k_rope64 if which == "B" else k_rope128
                K.dma("sp", ropet[:, 0:gsz], rt[tc * 128:(tc + 1) * 128, :], writes=["ropet"])
                x1 = qkn[:, 0:qk_cols].rearrange("p (g d) -> p g d", d=gsz)[:, :, 0:hs]
                x2 = qkn[:, 0:qk_cols].rearrange("p (g d) -> p g d", d=gsz)[:, :, hs:gsz]
                o1v = tmp2[:, 0:qk_cols].rearrange("p (g d) -> p g d", d=gsz)[:, :, 0:hs]
                o2v = tmp2[:, 0:qk_cols].rearrange("p (g d) -> p g d", d=gsz)[:, :, hs:gsz]
                s1 = sq[:, 0:qk_cols].rearrange("p (g d) -> p g d", d=gsz)[:, :, 0:hs]
                s2 = sq[:, 0:qk_cols].rearrange("p (g d) -> p g d", d=gsz)[:, :, hs:gsz]
                cosb = ropet[:, 0:hs].unsqueeze(1).to_broadcast([128, nq_groups, hs])
                sinb = ropet[:, hs:gsz].unsqueeze(1).to_broadcast([128, nq_groups, hs])
                K.op("dve", lambda e: e.tensor_tensor(out=o1v, in0=x1, in1=cosb, op=ALU.mult), reads=[kn_, "ropet"], writes=[kt2])
                K.op("dve", lambda e: e.tensor_tensor(out=s1, in0=x2, in1=sinb, op=ALU.mult), reads=[kn_, "ropet"], writes=[ksq])
                K.op("dve", lambda e: e.tensor_tensor(out=o1v, in0=o1v, in1=s1, op=ALU.subtract), reads=[kt2, ksq], writes=[kt2])
                K.op("dve", lambda e: e.tensor_tensor(out=o2v, in0=x2, in1=cosb, op=ALU.mult), reads=[kn_, "ropet"], writes=[kt2])
                K.op("dve", lambda e: e.tensor_tensor(out=s2, in0=x1, in1=sinb, op=ALU.mult), reads=[kn_, "ropet"], writes=[ksq])
                K.op("dve", lambda e: e.tensor_tensor(out=o2v, in0=o2v, in1=s2, op=ALU.add), reads=[kt2, ksq], writes=[kt2])
                src = tmp2
            srck = kn_ if src is qkn else kt2
            b = nps()
            for h in range(4):
                K.op("pe", lambda e: e.transpose(PS[b][:, h * 128:(h + 1) * 128], src[:, h * 128:(h + 1) * 128], ident),
                     reads=[srck, "ident"], writes=[psk[b]])
            evac_act(qT[:, :, tc * 128:(tc + 1) * 128], PS[b].rearrange("p (h t) -> p h t", h=4), [psk[b]], ["qT"])
            b = nps()
            for h in range(nh_k):
                K.op("pe", lambda e: e.transpose(PS[b][:, h * 128:(h + 1) * 128], src[:, 512 + h * 128:512 + (h + 1) * 128], ident),
                     reads=[srck, "ident"], writes=[psk[b]])
            kc0 = nctx * 128 + tc * 128
            evac_act(kT[:, 0:nh_k, kc0:kc0 + 128], PS[b][:, 0:nh_k * 128].rearrange("p (h t) -> p h t", h=nh_k), [psk[b]], ["kT"])
        if not is_s:
            seq = 0 if t0 == 0 else 1
            odst = (o_dv if which == "B" else o_sv)
            K.dma("sp", odst[seq, l, :, :], ztm[t0:t0 + T, zv:zv + vw], reads=["ztm"], writes=["okvv"])
        scale = (64 ** -0.5) if which == "B" else (128 ** -0.5)
        nqb = (T + 511) // 512
        acnt = [0]
        scnt = [0]
        for h in range(4):
            for qb in range(nqb):
                q0 = qb * 512
                Nq = min(512, T - q0)
                maps = (0, 1) if which == "B" else (0,)
                for m in maps:
                    K.tick()
                    bn, bd = (4, 5) if acnt[0] % 2 == 0 else (6, 7)
                    acnt[0] += 1
                    its = []
                    for j in range(nk):
                        lat = j - nctx
                        qa, qe = 0, Nq
                        if which == "C" and is_s and lat >= 0:
                            qa = max(q0, (lat - 1) * 128) - q0
                            qe = min(q0 + Nq, (lat + 2) * 128) - q0
                            if qe <= qa:
                                continue
                        its.append((j, lat, qa, qe))

                    def emit_score(k):
                        j, lat, qa, qe = its[k]
                        b = scnt[0] % 4
                        scnt[0] += 1
                        eb = k % 2
                        if which == "B":
                            lk = kT[m * 64:(m + 1) * 64, h, j * 128:(j + 1) * 128]
                            rq = qT[m * 64:(m + 1) * 64, h, q0 + qa:q0 + qe]
                        else:
                            lk = kT[:, h // 2, j * 128:(j + 1) * 128]
                            rq = qT[:, h, q0 + qa:q0 + qe]
                        K.op("pe", lambda e: e.matmul(PS[b][:, qa:qe], lhsT=lk, rhs=rq, start=True, stop=True),
                             reads=["kT", "qT"], writes=[psk[b]])
                        K.op("act", lambda e: e.activation(out=ebuf[eb][:, qa:qe], in_=PS[b][:, qa:qe], func=AF.Exp, scale=scale),
                             reads=[psk[b]], writes=[f"ebuf{eb}"])
                        if which == "C" and is_s and lat >= 0:
                            mo = (q0 + qa) - (lat - 1) * 128
                            K.op("pool", lambda e: e.tensor_tensor(out=ebuf[eb][:, qa:qe], in0=ebuf[eb][:, qa:qe], in1=band[:, mo:mo + (qe - qa)], op=ALU.mult),
                                 reads=[f"ebuf{eb}", "band"], writes=[f"ebuf{eb}"])

                    def emit_acc(k):
                        j, lat, qa, qe = its[k]
                        eb = k % 2
                        vsl = vv[:, j, h * 128:(h + 1) * 128] if which == "B" else vv[:, j, (h // 2) * 128:(h // 2 + 1) * 128]
                        K.op("pe", lambda e: e.matmul(PS[bn][:, qa:qe], lhsT=vsl, rhs=ebuf[eb][:, qa:qe], start=(k == 0), stop=(k == len(its) - 1)),
                             reads=["vv", f"ebuf{eb}"], writes=[psk[bn]])
                        K.op("pe", lambda e: e.matmul(PS[bd][:, qa:qe], lhsT=onesb, rhs=ebuf[eb][:, qa:qe], start=(k == 0), stop=(k == len(its) - 1)),
                             reads=["onesb", f"ebuf{eb}"], writes=[psk[bd]])

                    emit_score(0)
                    for k in range(len(its)):
                        if k + 1 < len(its):
                            emit_score(k + 1)
                        emit_acc(k)
                    if which == "C":
                        K.op("dve", lambda e: e.tensor_scalar(out=rden[:, 0:Nq], in0=PS[bd][:, 0:Nq], scalar1=esink[:, h:h + 1], scalar2=None, op0=ALU.add),
                             reads=[psk[bd], "esink"], writes=["rden"])
                        K.op("dve", lambda e: e.reciprocal(out=rden[:, 0:Nq], in_=rden[:, 0:Nq]), reads=["rden"], writes=["rden"])
                        K.op("dve", lambda e: e.tensor_tensor(out=hT[:, 8 + h, t0 + q0:t0 + q0 + Nq], in0=PS[bn][:, 0:Nq], in1=rden[:, 0:Nq], op=ALU.mult),
                             reads=[psk[bn], "rden"], writes=["hT"])
                    else:
                        K.op("dve", lambda e: e.reciprocal(out=rden[:, 0:Nq], in_=PS[bd][:, 0:Nq]), reads=[psk[bd]], writes=["rden"])
                        if m == 0:
                            K.op("dve", lambda e: e.tensor_tensor(out=acc[:, 0:Nq], in0=PS[bn][:, 0:Nq], in1=rden[:, 0:Nq], op=ALU.mult),
                                 reads=[psk[bn], "rden"], writes=["acc"])
                        else:
                            K.op("dve", lambda e: e.tensor_tensor(out=o1[:, 0:Nq], in0=PS[bn][:, 0:Nq], in1=rden[:, 0:Nq], op=ALU.mult),
                                 reads=[psk[bn], "rden"], writes=["o1"])
                            K.op("dve", lambda e: e.scalar_tensor_tensor(out=acc[:, 0:Nq], in0=o1[:, 0:Nq], scalar=lam_s[:, 2:3], in1=acc[:, 0:Nq],
                                                                         op0=ALU.mult, op1=ALU.add),
                                 reads=["o1", "lam_s", "acc"], writes=["acc"])
                if which == "B":
                    K.op("act", lambda e: e.activation(out=ebuf[0][:, 0:Nq], in_=acc[:, 0:Nq], func=AF.Square), reads=["acc"], writes=["ebuf0"])
                    b = scnt[0] % 4
                    scnt[0] += 1
                    K.op("pe", lambda e: e.matmul(PS[b][:, 0:Nq], lhsT=onesb, rhs=ebuf[0][:, 0:Nq], start=True, stop=True),
                         reads=["onesb", "ebuf0"], writes=[psk[b]])
                    K.op("act", lambda e: e.activation(out=rden[:, 0:Nq], in_=PS[b][:, 0:Nq], func=AF.Sqrt, scale=1.0 / 128, bias=epsT[:, 0:1]),
                         reads=[psk[b], "epsT"], writes=["rden"])
                    K.op("dve", lambda e: e.reciprocal(out=rden[:, 0:Nq], in_=rden[:, 0:Nq]), reads=["rden"], writes=["rden"])
                    K.op("dve", lambda e: e.scalar_tensor_tensor(out=hT[:, 4 + h, t0 + q0:t0 + q0 + Nq], in0=acc[:, 0:Nq], scalar=og_s[:, 0:1],
                                                                 in1=rden[:, 0:Nq], op0=ALU.mult, op1=ALU.mult),
                         reads=["acc", "og_s", "rden"], writes=["hT"])

    def mlstm(l, t0, T, is_s):
        nch = T // 128
        blocks = [(q, min(512, T - q)) for q in range(0, T, 512)]
        gi = [aw(0, 1024)[0:4, 0:T], aw(1024, 1024)[0:4, 0:T]]
        gf = [aw(2048, 1024)[0:4, 0:T], aw(3072, 1024)[0:4, 0:T]]
        bcs = [aw(4096, 1024)[0:4, 0:T], aw(5120, 1024)[0:4, 0:T]]
        av = [aw(6144, 1024)[0:4, 0:T], aw(7168, 1024)[0:4, 0:T]]
        cm = [aw(8192, 1024)[0:4, 0:T], aw(9216, 1024)[0:4, 0:T]]
        zrow = aw(10240, 1024)[0:4, 0:T]
        aT = [aw(11264, 32).rearrange("p (j h) -> p j h", h=4), aw(11296, 32).rearrange("p (j h) -> p j h", h=4)]
        wT = [aw(11328, 32).rearrange("p (j h) -> p j h", h=4), aw(11360, 32).rearrange("p (j h) -> p j h", h=4)]
        qTb_ = [awb(11392, 512)[:, 0:T], awb(19456, 512)[:, 0:T]]
        kTb_ = [awb(11904, 512)[:, 0:T], awb(19968, 512)[:, 0:T]]
        vtm_ = [awb(12416, 528).rearrange("p (j n) -> p j n", j=8), awb(20480, 528).rearrange("p (j n) -> p j n", j=8)]
        oT_ = [aw(12944, 1024)[:, 0:T], aw(21008, 1024)[:, 0:T]]
        cB = aw(13968, 1024)[:, 0:T]
        emB = aw(14992, 1024)[:, 0:T]
        Ebuf_ = [awb(16016, 512)[:, 0:T], awb(22032, 512)[:, 0:T]]
        Pbuf_ = [awb(16528, 512)[:, 0:T], awb(22544, 512)[:, 0:T]]
        hsum = aw(17040, 1024)[:, 0:T]
        tmpn = aw(18064, 512)
        qw = awb(18576, 512)[:, 0:T]
        C0b = awb(19088, 64)
        n0rep = awb(19152, 64)
        kwb = awb(19216, 64)
        ktm = aw(19280, 128)
        smal = aw(19408, 48)
        igb = [smal[0:4, 0:1], smal[0:4, 1:2]]
        fgb = [smal[0:4, 2:3], smal[0:4, 3:4]]
        m0c = [smal[0:4, 4:5], smal[0:4, 5:6]]
        mout = [smal[0:4, 6:7], smal[0:4, 7:8]]
        n0col = smal[:, 8:9]
        rev = lambda ap: ap[:, ::-1]
        scale = 128 ** -0.5
        seq = 0 if t0 == 0 else 1
        K.op("dve", lambda e: e.memset(zrow, 0.0), writes=["zrow"])
        for i_ in range(2):
            K.op("dve", lambda e: e.memset(vtm_[i_][:, :, 128:129], 1.0), writes=[f"vtm{i_}"])
        for d in range(2):
            K.dma("sp", gi[d], zAT[1536 + d * 4:1540 + d * 4, t0:t0 + T], reads=["zAT"], writes=[f"gi{d}"])
            K.dma("sp", gf[d], zAT[1544 + d * 4:1548 + d * 4, t0:t0 + T], reads=["zAT"], writes=[f"gf{d}"])
            K.dma("sp", igb[d], ig_b[l, d * 4:(d + 1) * 4].rearrange("(p o) -> p o", o=1), writes=["smal"])
            K.dma("sp", fgb[d], fg_b[l, d * 4:(d + 1) * 4].rearrange("(p o) -> p o", o=1), writes=["smal"])
            if is_s:
                K.dma("sp", m0c[d], st_m[l, d * 4:(d + 1) * 4].rearrange("(p o) -> p o", o=1), writes=["smal"])
            K.op("dve", lambda e: e.tensor_scalar(out=gi[d], in0=gi[d], scalar1=igb[d], scalar2=None, op0=ALU.add), reads=[f"gi{d}", "smal"], writes=[f"gi{d}"])
            K.op("act", lambda e: e.activation(out=gf[d], in_=gf[d], func=AF.Sigmoid, bias=fgb[d]), reads=[f"gf{d}", "smal"], writes=[f"gf{d}"])
            K.op("act", lambda e: e.activation(out=gf[d], in_=gf[d], func=AF.Ln), reads=[f"gf{d}"], writes=[f"gf{d}"])
            o_ = (lambda ap: ap) if d == 0 else rev
            K.op("dve", lambda e: e.tensor_tensor_scan(out=o_(bcs[d]), data0=o_(gf[d]), data1=zrow, initial=0.0, op0=ALU.add, op1=ALU.add),
                 reads=[f"gf{d}", "zrow"], writes=[f"bcs{d}"])
            K.op("dve", lambda e: e.tensor_tensor(out=av[d], in0=gi[d], in1=bcs[d], op=ALU.subtract), reads=[f"gi{d}", f"bcs{d}"], writes=[f"av{d}"])
            init = m0c[d] if is_s else 0.0
            K.op("dve", lambda e: e.tensor_tensor_scan(out=o_(cm[d]), data0=o_(av[d]), data1=o_(av[d]), initial=init, op0=ALU.max, op1=ALU.max),
                 reads=[f"av{d}", "smal"], writes=[f"cm{d}"])
            K.op("dve", lambda e: e.tensor_scalar(out=cm[d], in0=cm[d], scalar1=-1.0, scalar2=None, op0=ALU.mult), reads=[f"cm{d}"], writes=[f"cm{d}"])
            K.op("dve", lambda e: e.tensor_tensor(out=gf[d], in0=cm[d], in1=bcs[d], op=ALU.subtract), reads=[f"cm{d}", f"bcs{d}"], writes=[f"gf{d}"])
            K.op("act", lambda e: e.activation(out=gf[d], in_=gf[d], func=AF.Exp), reads=[f"gf{d}"], writes=[f"gf{d}"])
            if is_s:
                K.op("act", lambda e: e.activation(out=gi[d], in_=cm[d], func=AF.Exp, bias=m0c[d]), reads=[f"cm{d}", "smal"], writes=[f"gi{d}"])
            b = nps()
            for j in range(nch):
                K.op("pe", lambda e: e.transpose(PS[b][:, j * 4:(j + 1) * 4], av[d][:, j * 128:(j + 1) * 128], ident[0:4, 0:4]),
                     reads=[f"av{d}", "ident"], writes=[psk[b]])
            K.op("dve", lambda e: e.tensor_copy(out=aT[d][:, 0:nch, :], in_=PS[b][:, 0:nch * 4].rearrange("p (j h) -> p j h", h=4)),
                 reads=[psk[b]], writes=[f"aT{d}"])
            if not is_s:
                last = slice(T - 1, T) if d == 0 else slice(0, 1)
                K.op("dve", lambda e: e.tensor_tensor(out=mout[d], in0=bcs[d][:, last], in1=cm[d][:, last], op=ALU.subtract),
                     reads=[f"bcs{d}", f"cm{d}"], writes=["smal"])
                K.dma("sp", o_m[seq, l, d * 4:(d + 1) * 4].rearrange("(p o) -> p o", o=1), mout[d], reads=["smal"], writes=["o_m"])
                K.op("act", lambda e: e.activation(out=av[d], in_=av[d], func=AF.Exp, bias=cm[d][:, last]), reads=[f"av{d}", f"cm{d}"], writes=[f"av{d}"])
                b = nps()
                for j in range(nch):
                    K.op("pe", lambda e: e.transpose(PS[b][:, j * 4:(j + 1) * 4], av[d][:, j * 128:(j + 1) * 128], ident[0:4, 0:4]),
                         reads=[f"av{d}", "ident"], writes=[psk[b]])
                K.op("dve", lambda e: e.tensor_copy(out=wT[d][:, 0:nch, :], in_=PS[b][:, 0:nch * 4].rearrange("p (j h) -> p j h", h=4)),
                     reads=[psk[b]], writes=[f"wT{d}"])
        for h in range(4):
            hp = h % 2
            qTb, kTb, vtm, oT = qTb_[hp], kTb_[hp], vtm_[hp], oT_[hp]
            kqT, kkT, kvt, koT = f"qTb{hp}", f"kTb{hp}", f"vtm{hp}", f"oT{hp}"
            K.dma("pool", qTb, zAT[h * 128:(h + 1) * 128, t0:t0 + T], reads=["zAT"], writes=[kqT])
            K.dma("pool", kTb, zAT[512 + h * 128:512 + (h + 1) * 128, t0:t0 + T], reads=["zAT"], writes=[kkT])
            K.dma("pool", vtm[:, 0:nch, 0:128], ztm[t0:t0 + T, ZC["Av"] + h * 128:ZC["Av"] + (h + 1) * 128].rearrange("(j p) d -> p j d", p=128),
                  reads=["ztm"], writes=[kvt])
            K.dma("sp", oT, zAT[1024 + h * 128:1024 + (h + 1) * 128, t0:t0 + T], reads=["zAT"], writes=[koT])
            K.op("act", lambda e: e.activation(out=oT, in_=oT, func=AF.Sigmoid), reads=[koT], writes=[koT])
            for d in range(2):
                K.tick()
                selh = sel[0:4, h * 128:(h + 1) * 128]
                for bi, (q0, n) in enumerate(blocks):
                    b = nps() % 4
                    K.op("pe", lambda e: e.matmul(PS[b][:, 0:n], lhsT=selh, rhs=cm[d][:, q0:q0 + n], start=True, stop=True),
                         reads=["sel", f"cm{d}"], writes=[psk[b]])
                    K.op("act", lambda e: e.activation(out=cB[:, q0:q0 + n], in_=PS[b][:, 0:n], func=AF.Copy), reads=[psk[b]], writes=["cB"])
                    b = nps() % 4
                    K.op("pe", lambda e: e.matmul(PS[b][:, 0:n], lhsT=selh, rhs=gf[d][:, q0:q0 + n], start=True, stop=True),
                         reads=["sel", f"gf{d}"], writes=[psk[b]])
                    K.op("act", lambda e: e.activation(out=emB[:, q0:q0 + n], in_=PS[b][:, 0:n], func=AF.Copy), reads=[psk[b]], writes=["emB"])
                bnum = [4, 5]
                bden = [6, 7]
                if is_s:
                    K.dma("pool", C0b, st_C[l, d, h], writes=["C0b"])
                    K.dma("sp", n0col, st_n[l, d, h].rearrange("(p o) -> p o", o=1), writes=["n0col"])
                    K.op("dve", lambda e: e.tensor_copy(out=n0rep, in_=n0col.to_broadcast([128, 128])), reads=["n0col"], writes=["n0rep"])
                    for bi, (q0, n) in enumerate(blocks):
                        b = nps() % 4
                        K.op("pe", lambda e: e.matmul(PS[b][:, 0:n], lhsT=selh, rhs=gi[d][:, q0:q0 + n], start=True, stop=True),
                             reads=["sel", f"gi{d}"], writes=[psk[b]])
                        K.op("dve", lambda e: e.scalar_tensor_tensor(out=qw[:, q0:q0 + n], in0=qTb[:, q0:q0 + n], scalar=scale, in1=PS[b][:, 0:n],
                                                                     op0=ALU.mult, op1=ALU.mult),
                             reads=[kqT, psk[b]], writes=["qw"])
                        K.op("pe", lambda e: e.matmul(PS[bnum[bi]][:, 0:n], lhsT=C0b, rhs=qw[:, q0:q0 + n], start=True, stop=False),
                             reads=["C0b", "qw"], writes=[psk[bnum[bi]]])
                        K.op("pe", lambda e: e.matmul(PS[bden[bi]][:, 0:n], lhsT=n0rep, rhs=qw[:, q0:q0 + n], start=True, stop=False),
                             reads=["n0rep", "qw"], writes=[psk[bden[bi]]])
                order = list(range(nch)) if d == 0 else list(range(nch - 1, -1, -1))
                started = [is_s for _ in blocks]
                pieces = {}

                def stage1(k):
                    j = order[k]
                    Ebuf, Pbuf = Ebuf_[k % 2], Pbuf_[k % 2]
                    kE, kP = f"Ebuf{k % 2}", f"Pbuf{k % 2}"
                    qa, qe = (128 * j, T) if d == 0 else (0, 128 * (j + 1))
                    K.op("act", lambda e: e.activation(out=Ebuf[:, qa:qe], in_=cB[:, qa:qe], func=AF.Exp, bias=aT[d][:, j, h:h + 1]),
                         reads=["cB", f"aT{d}"], writes=[kE])
                    tm = tri[:, 0:128] if d == 0 else tri[:, 128:256]
                    K.op("pool", lambda e: e.tensor_tensor(out=Ebuf[:, 128 * j:128 * (j + 1)], in0=Ebuf[:, 128 * j:128 * (j + 1)], in1=tm, op=ALU.mult),
                         reads=[kE, "tri"], writes=[kE])
                    pcs = []
                    for bi, (q0, n) in enumerate(blocks):
                        pa, pe_ = max(qa, q0), min(qe, q0 + n)
                        if pe_ <= pa:
                            continue
                        b = nps() % 4
                        K.op("pe", lambda e: e.matmul(PS[b][:, pa - q0:pe_ - q0], lhsT=kTb[:, j * 128:(j + 1) * 128], rhs=qTb[:, pa:pe_], start=True, stop=True),
                             reads=[kkT, kqT], writes=[psk[b]])
                        K.op("dve", lambda e: e.scalar_tensor_tensor(out=Pbuf[:, pa:pe_], in0=PS[b][:, pa - q0:pe_ - q0], scalar=scale, in1=Ebuf[:, pa:pe_],
                                                                     op0=ALU.mult, op1=ALU.mult),
                             reads=[psk[b], kE], writes=[kP])
                        pcs.append((bi, q0, n, pa, pe_))
                    pieces[k] = pcs

                def stage2(k):
                    j = order[k]
                    Pbuf = Pbuf_[k % 2]
                    kP = f"Pbuf{k % 2}"
                    for (bi, q0, n, pa, pe_) in pieces[k]:
                        lastj = ((q0 + n) // 128 - 1) if d == 0 else (q0 // 128)
                        K.op("pe", lambda e: e.matmul(PS[bnum[bi]][:, pa - q0:pe_ - q0], lhsT=vtm[:, j, 0:128], rhs=Pbuf[:, pa:pe_],
                                                      start=(not started[bi]), stop=(j == lastj)),
                             reads=[kvt, kP], writes=[psk[bnum[bi]]])
                        K.op("pe", lambda e: e.matmul(PS[bden[bi]][:, pa - q0:pe_ - q0], lhsT=onesb, rhs=Pbuf[:, pa:pe_],
                                                      start=(not started[bi]), stop=(j == lastj)),
                             reads=["onesb", kP], writes=[psk[bden[bi]]])
                        started[bi] = True

                stage1(0)
                for k in range(nch):
                    if k + 1 < nch:
                        stage1(k + 1)
                    stage2(k)
                for bi, (q0, n) in enumerate(blocks):
                    K.op("act", lambda e: e.activation(out=tmpn[:, 0:n], in_=PS[bden[bi]][:, 0:n], func=AF.Abs),
                         reads=[psk[bden[bi]]], writes=["tmpn"])
                    K.op("dve", lambda e: e.tensor_tensor(out=tmpn[:, 0:n], in0=tmpn[:, 0:n], in1=emB[:, q0:q0 + n], op=ALU.max),
                         reads=["tmpn", "emB"], writes=["tmpn"])
                    K.op("dve", lambda e: e.reciprocal(out=tmpn[:, 0:n], in_=tmpn[:, 0:n]), reads=["tmpn"], writes=["tmpn"])
                    if d == 0:
                        K.op("dve", lambda e: e.tensor_tensor(out=hsum[:, q0:q0 + n], in0=PS[bnum[bi]][:, 0:n], in1=tmpn[:, 0:n], op=ALU.mult),
                             reads=[psk[bnum[bi]], "tmpn"], writes=["hsum"])
                    else:
                        K.op("dve", lambda e: e.tensor_tensor(out=tmpn[:, 0:n], in0=PS[bnum[bi]][:, 0:n], in1=tmpn[:, 0:n], op=ALU.mult),
                             reads=[psk[bnum[bi]], "tmpn"], writes=["tmpn"])
                        K.op("dve", lambda e: e.tensor_tensor(out=hsum[:, q0:q0 + n], in0=hsum[:, q0:q0 + n], in1=tmpn[:, 0:n], op=ALU.add),
                             reads=["hsum", "tmpn"], writes=["hsum"])
                if not is_s:
                    b = nps() % 4
                    for j in range(nch):
                        K.dma("sp", ktm, ztm[t0 + j * 128:t0 + (j + 1) * 128, ZC["Ak"] + h * 128:ZC["Ak"] + (h + 1) * 128], reads=["ztm"], writes=["ktm"])
                        K.op("dve", lambda e: e.tensor_scalar(out=kwb, in0=ktm, scalar1=wT[d][:, j, h:h + 1], scalar2=None, op0=ALU.mult),
                             reads=["ktm", f"wT{d}"], writes=["kwb"])
                        K.op("pe", lambda e: e.matmul(PS[b][:, 0:129], lhsT=kwb, rhs=vtm[:, j, 0:129], start=(j == 0), stop=(j == nch - 1)),
                             reads=["kwb", kvt], writes=[psk[b]])
                    s = srr[0]
                    srr[0] = (s + 1) % 4
                    K.op("act", lambda e: e.activation(out=stage[s][:, 0:129], in_=PS[b][:, 0:129], func=AF.Copy), reads=[psk[b]], writes=[f"stage{s}"])
                    K.dma("sp", o_C[seq, l, d, h], stage[s][:, 0:128], reads=[f"stage{s}"], writes=["o_C"])
                    K.dma("sp", o_n[seq, l, d, h].rearrange("(p o) -> p o", o=1), stage[s][:, 128:129], reads=[f"stage{s}"], writes=["o_n"])
            for bi, (q0, n) in enumerate(blocks):
                K.op("act", lambda e: e.activation(out=Pbuf_[0][:, q0:q0 + n], in_=hsum[:, q0:q0 + n], func=AF.Square), reads=["hsum"], writes=["Pbuf0"])
                b = nps() % 4
                K.op("pe", lambda e: e.matmul(PS[b][:, 0:n], lhsT=onesb, rhs=Pbuf_[0][:, q0:q0 + n], start=True, stop=True),
                     reads=["onesb", "Pbuf0"], writes=[psk[b]])
                K.op("act", lambda e: e.activation(out=tmpn[:, 0:n], in_=PS[b][:, 0:n], func=AF.Sqrt, scale=1.0 / 128, bias=epsT[:, 0:1]),
                     reads=[psk[b], "epsT"], writes=["tmpn"])
                K.op("dve", lambda e: e.reciprocal(out=tmpn[:, 0:n], in_=tmpn[:, 0:n]), reads=["tmpn"], writes=["tmpn"])
                K.op("dve", lambda e: e.scalar_tensor_tensor(out=tmpn[:, 0:n], in0=hsum[:, q0:q0 + n], scalar=mng[:, h:h + 1], in1=tmpn[:, 0:n],
                                                             op0=ALU.mult, op1=ALU.mult),
                     reads=["hsum", "mng", "tmpn"], writes=["tmpn"])
                K.op("dve", lambda e: e.tensor_tensor(out=hT[:, h, t0 + q0:t0 + q0 + n], in0=tmpn[:, 0:n], in1=oT[:, q0:q0 + n], op=ALU.mult),
                     reads=["tmpn", koT], writes=["hT"])

    zpad = arena[:, 19456:19456 + 1026]
    hcw = K.sb([128, 36], F32, "hcw")
    hcb = K.sb([128, 12], F32, "hcb")
    hsm = K.sb([128, 4], F32, "hsm")
    u2buf = arena[:, 20500:20500 + 2048].bitcast(BF16).rearrange("p (j c) -> p j c", c=512)

    def hy_filters(l, L):
        nch = L // 128
        featT = aw(0, 1024)[0:34, 0:L]
        h1s = aw(1024, 1024)[0:64, 0:L]
        h2s = aw(2048, 1024)[0:65, 0:L]
        ta = aw(3072, 512)[0:64, :]
        tki = aw(3584, 512)[0:64, :].bitcast(I32)
        tkf = aw(4096, 512)[0:64, :]
        w1a = aw(4608, 64)[0:34, :]
        w2t = aw(4672, 64)[0:64, :]
        w3a = aw(4736, 2048)[0:65, :]
        wint = aw(6784, 512)
        hfb = [aw(7296, 512), aw(7808, 512)]
        Gs = [awb(8320, 2048).rearrange("p (j c) -> p j c", c=512), awb(10368, 2048).rearrange("p (j c) -> p j c", c=512)]
        Gd = [awb(12416, 2048).rearrange("p (j c) -> p j c", c=512), awb(14464, 2048).rearrange("p (j c) -> p j c", c=512)]
        CSf = [awb(16512, 512).rearrange("p (j c) -> p j c", c=128), awb(17024, 512).rearrange("p (j c) -> p j c", c=128)]
        skp = aw(18560, 512)[0:1, :]
        frc = hsm[0:64, 0:1]
        b2c = hsm[0:64, 1:2]
        K.dma("sp", featT, k_feat[L], writes=["featT"])
        K.dma("sp", w1a[0:33, :], hy_w1[l], writes=["w1a"])
        K.dma("sp", w1a[33:34, :], hy_b1[l:l + 1, :], writes=["w1a"])
        K.dma("sp", w2t, hy_w2[l], writes=["w2t"])
        K.dma("sp", w3a[0:64, :], hy_w3[l], writes=["w3a"])
        K.dma("sp", w3a[64:65, :], hy_b3[l:l + 1, :], writes=["w3a"])
        K.dma("sp", frc, hy_fr[l].rearrange("(p o) -> p o", o=1), writes=["hsm"])
        K.dma("sp", b2c, hy_b2[l].rearrange("(p o) -> p o", o=1), writes=["hsm"])
        K.op("dve", lambda e: e.memset(h2s[64:65, :], 1.0), writes=["h2s"])
        K.end_atom()

        def sin_layer(dst, dkey, lhsT, lkey, rhs_t, rkey, add_b):
            for q0 in range(0, L, 512):
                n = min(512, L - q0)
                b = nps()
                K.op("pe", lambda e: e.matmul(PS[b][0:64, 0:n], lhsT=lhsT, rhs=rhs_t[:, q0:q0 + n], start=True, stop=True),
                     reads=[lkey, rkey], writes=[psk[b]])
                if add_b:
                    K.op("dve", lambda e: e.tensor_scalar(out=ta[:, 0:n], in0=PS[b][0:64, 0:n], scalar1=b2c, scalar2=frc, op0=ALU.add, op1=ALU.mult),
                         reads=[psk[b], "hsm"], writes=["ta"])
                else:
                    K.op("dve", lambda e: e.tensor_scalar(out=ta[:, 0:n], in0=PS[b][0:64, 0:n], scalar1=frc, scalar2=None, op0=ALU.mult),
                         reads=[psk[b], "hsm"], writes=["ta"])
                K.op("dve", lambda e: e.tensor_scalar(out=tki[:, 0:n], in0=ta[:, 0:n], scalar1=float(1 / (2 * math.pi)), scalar2=None, op0=ALU.mult), reads=["ta"], writes=["tki"])
                K.op("dve", lambda e: e.tensor_copy(out=tkf[:, 0:n], in_=tki[:, 0:n]), reads=["tki"], writes=["tkf"])
                K.op("dve", lambda e: e.scalar_tensor_tensor(out=ta[:, 0:n], in0=tkf[:, 0:n], scalar=float(-2 * math.pi), in1=ta[:, 0:n], op0=ALU.mult, op1=ALU.add),
                     reads=["tkf", "ta"], writes=["ta"])
                K.op("dve", lambda e: e.tensor_scalar(out=tkf[:, 0:n], in0=ta[:, 0:n], scalar1=float(math.pi), scalar2=float(-2 * math.pi), op0=ALU.is_gt, op1=ALU.mult), reads=["ta"], writes=["tkf"])
                K.op("dve", lambda e: e.tensor_tensor(out=ta[:, 0:n], in0=ta[:, 0:n], in1=tkf[:, 0:n], op=ALU.add), reads=["ta", "tkf"], writes=["ta"])
                K.op("dve", lambda e: e.tensor_scalar(out=tkf[:, 0:n], in0=ta[:, 0:n], scalar1=float(-math.pi), scalar2=float(2 * math.pi), op0=ALU.is_lt, op1=ALU.mult), reads=["ta"], writes=["tkf"])
                K.op("dve", lambda e: e.tensor_tensor(out=ta[:, 0:n], in0=ta[:, 0:n], in1=tkf[:, 0:n], op=ALU.add), reads=["ta", "tkf"], writes=["ta"])
                K.op("act", lambda e: e.activation(out=dst[0:64, q0:q0 + n], in_=ta[:, 0:n], func=AF.Sin), reads=["ta"], writes=[dkey])
                K.end_atom()

        sin_layer(h1s, "h1s", w1a, "w1a", featT, "featT", False)
        sin_layer(h2s, "h2s", w2t, "w2t", h1s, "h1s", True)
        for tc in range(nch):
            K.dma("sp", wint, k_win[L][tc * 128:(tc + 1) * 128, :], writes=["wint"])
            for o in range(2):
                for dr in range(2):
                    g = o * 2 + dr
                    b = nps()
                    K.op("pe", lambda e: e.matmul(PS[b], lhsT=h2s[:, tc * 128:(tc + 1) * 128], rhs=w3a[:, g * 512:(g + 1) * 512], start=True, stop=True),
                         reads=["h2s", "w3a"], writes=[psk[b]])
                    K.op("dve", lambda e: e.tensor_tensor(out=hfb[dr], in0=PS[b], in1=wint, op=ALU.mult), reads=[psk[b], "wint"], writes=[f"hfb{dr}"])
                if tc == 0:
                    K.dma("sp", skp, hy_bias[l, o:o + 1, :], writes=["skp"])
                    K.op("dve", lambda e: e.tensor_tensor(out=hfb[0][0:1, :], in0=hfb[0][0:1, :], in1=skp, op=ALU.add), reads=["hfb0", "skp"], writes=["hfb0"])
                    K.op("dve", lambda e: e.memset(hfb[1][0:1, :], 0.0), reads=["hfb1"], writes=["hfb1"])
                K.op("dve", lambda e: e.tensor_tensor(out=Gs[o][:, tc, :], in0=hfb[0], in1=hfb[1], op=ALU.add), reads=["hfb0", "hfb1"], writes=["Gs"])
                K.op("dve", lambda e: e.tensor_tensor(out=Gd[o][:, tc, :], in0=hfb[1], in1=hfb[0], op=ALU.subtract), reads=["hfb0", "hfb1"], writes=["Gd"])
                K.end_atom()
        for fc in range(nch):
            for ri in range(2):
                K.dma("pool", CSf[ri][:, 0:nch, :], k_dft[L][ri, :, fc * 128:(fc + 1) * 128].rearrange("(tc p) f -> p tc f", p=128), writes=[f"CSf{ri}"])
            for o in range(2):
                for ri in range(2):
                    src = Gs[o] if ri == 0 else Gd[o]
                    b = nps()
                    for tc in range(nch):
                        K.op("pe", lambda e: e.matmul(PS[b], lhsT=CSf[ri][:, tc, :], rhs=src[:, tc, :], start=(tc == 0), stop=(tc == nch - 1)),
                             reads=[f"CSf{ri}", "Gs", "Gd"], writes=[psk[b]])
                    s = srr[0]
                    srr[0] = (s + 1) % 4
                    K.op("act", lambda e: e.activation(out=stage[s], in_=PS[b], func=AF.Copy, scale=1.0 / L), reads=[psk[b]], writes=[f"stage{s}"])
                    K.dma("sp", hyG[L][o, ri, fc * 128:(fc + 1) * 128, :], stage[s], reads=[f"stage{s}"], writes=[f"hyG{L}"])
                    K.end_atom()

    def hyena(l, t0, T, is_s):
        L = T
        nch = L // 128
        x12 = [awb(0, 2048).rearrange("p (c t) -> p c t", c=4), awb(2048, 2048).rearrange("p (c t) -> p c t", c=4)]
        u_tm = awb(4096, 2048).rearrange("p (j c) -> p j c", c=512)
        Yre = awb(6144, 2048).rearrange("p (j c) -> p j c", c=512)
        Yim = awb(8192, 2048).rearrange("p (j c) -> p j c", c=512)
        CSb = [awb(10240, 2048).rearrange("p (j t) -> p j t", t=512), awb(12288, 2048).rearrange("p (j t) -> p j t", t=512)]
        CSf_ = [[awb(14336, 512).rearrange("p (j c) -> p j c", c=128), awb(14848, 512).rearrange("p (j c) -> p j c", c=128)],
                [awb(18432, 512).rearrange("p (j c) -> p j c", c=128), awb(18944, 512).rearrange("p (j c) -> p j c", c=128)]]
        Gt_ = [[aw(15360, 512), aw(15872, 512)], [aw(22548, 512), aw(23060, 512)]]
        tt_ = [aw(16384, 512), aw(16896, 512), aw(17408, 512), aw(17920, 512)]
        u2 = awb(17408 + 1024, 1024 - 0).rearrange("p (j c) -> p j c", c=512) if False else u2buf
        cvo = aw(16384, 1024)
        K.op("dve", lambda e: e.memset(zpad, 0.0), reads=["zpad"], writes=["zpad"])

        def to_tm(srcT, skey, cc):
            for g in range((nch + 3) // 4):
                k = min(4, nch - g * 4)
                b = nps()
                for i2 in range(k):
                    tc = g * 4 + i2
                    K.op("pe", lambda e: e.transpose(PS[b][:, i2 * 128:(i2 + 1) * 128], srcT[:, tc * 128:(tc + 1) * 128], ident),
                         reads=[skey, "ident"], writes=[psk[b]])
                evac(u_tm[:, g * 4:g * 4 + k, cc * 128:(cc + 1) * 128], PS[b][:, 0:k * 128].rearrange("p (j c) -> p j c", c=128), [psk[b]], ["u_tm"])

        for ch in range(12):
            K.dma("sp", zpad[:, 1:T + 1], zDT[ch * 128:(ch + 1) * 128, t0:t0 + T], reads=["zDT"], writes=["zpad"])
            dst = cvo[:, 0:T] if ch < 4 else x12[(ch - 4) // 4][:, ch % 4, 0:T]
            dkey = "tt" if ch < 4 else "x12"
            K.op("act", lambda e: e.activation(out=cvo[:, 0:T], in_=zpad[:, 1:T + 1], func=AF.Identity, scale=hcw[:, 12 + ch:13 + ch], bias=hcb[:, ch:ch + 1]),
                 reads=["zpad", "hcw", "hcb"], writes=["tt"])
            K.op("dve", lambda e: e.scalar_tensor_tensor(out=cvo[:, 0:T], in0=zpad[:, 0:T], scalar=hcw[:, ch:ch + 1], in1=cvo[:, 0:T], op0=ALU.mult, op1=ALU.add),
                 reads=["zpad", "hcw", "tt"], writes=["tt"])
            K.op("dve", lambda e: e.scalar_tensor_tensor(out=dst, in0=zpad[:, 2:T + 2], scalar=hcw[:, 24 + ch:25 + ch], in1=cvo[:, 0:T], op0=ALU.mult, op1=ALU.add),
                 reads=["zpad", "hcw", "tt"], writes=[dkey])
            if ch < 4:
                to_tm(cvo[:, 0:T], "tt", ch)
        for o in range(2):
            for fc in range(nch):
                K.tick()
                CSf, Gt = CSf_[fc % 2], Gt_[fc % 2]
                kcs = [f"CSf{fc % 2}{ri}" for ri in range(2)]
                kgt = [f"Gt{fc % 2}{ri}" for ri in range(2)]
                for ri in range(2):
                    K.dma("pool", CSf[ri][:, 0:nch, :], k_dft[L][ri, :, fc * 128:(fc + 1) * 128].rearrange("(tc p) f -> p tc f", p=128), writes=[kcs[ri]])
                    K.dma("sp", Gt[ri], hyG[L][o, ri, fc * 128:(fc + 1) * 128, :], reads=[f"hyG{L}"], writes=[kgt[ri]])
                bu = [nps(), nps()]
                for ri in range(2):
                    for tc in range(nch):
                        K.op("pe", lambda e: e.matmul(PS[bu[ri]], lhsT=CSf[ri][:, tc, :], rhs=u_tm[:, tc, :], start=(tc == 0), stop=(tc == nch - 1)),
                             reads=[kcs[ri], "u_tm"], writes=[psk[bu[ri]]])
                K.op("dve", lambda e: e.tensor_tensor(out=tt_[0], in0=PS[bu[0]], in1=Gt[0], op=ALU.mult), reads=[psk[bu[0]], kgt[0]], writes=["tt"])
                K.op("dve", lambda e: e.tensor_tensor(out=tt_[1], in0=PS[bu[1]], in1=Gt[1], op=ALU.mult), reads=[psk[bu[1]], kgt[1]], writes=["tt"])
                K.op("dve", lambda e: e.tensor_tensor(out=tt_[2], in0=PS[bu[1]], in1=Gt[0], op=ALU.mult), reads=[psk[bu[1]], kgt[0]], writes=["tt"])
                K.op("dve", lambda e: e.tensor_tensor(out=tt_[3], in0=PS[bu[0]], in1=Gt[1], op=ALU.mult), reads=[psk[bu[0]], kgt[1]], writes=["tt"])
                K.op("pool", lambda e: e.tensor_tensor(out=Yre[:, fc, :], in0=tt_[0], in1=tt_[1], op=ALU.add), reads=["tt"], writes=["Yre"])
                K.op("pool", lambda e: e.tensor_tensor(out=Yim[:, fc, :], in0=tt_[2], in1=tt_[3], op=ALU.subtract), reads=["tt"], writes=["Yim"])
            for q0 in range(0, T, 512):
                n = min(512, T - q0)
                for ri in range(2):
                    K.dma("pool", CSb[ri][:, 0:nch, 0:n], k_dft[L][2 + ri, :, q0:q0 + n].rearrange("(fc p) t -> p fc t", p=128), writes=[f"CSb{ri}"])
                for cc in range(4):
                    b = nps()
                    for fc in range(nch):
                        K.op("pe", lambda e: e.matmul(PS[b][:, 0:n], lhsT=Yre[:, fc, cc * 128:(cc + 1) * 128], rhs=CSb[0][:, fc, 0:n], start=(fc == 0), stop=False),
                             reads=["Yre", "CSb0"], writes=[psk[b]])
                        K.op("pe", lambda e: e.matmul(PS[b][:, 0:n], lhsT=Yim[:, fc, cc * 128:(cc + 1) * 128], rhs=CSb[1][:, fc, 0:n], start=False, stop=(fc == nch - 1)),
                             reads=["Yim", "CSb1"], writes=[psk[b]])
                    if o == 0:
                        K.op("dve", lambda e: e.tensor_tensor(out=cvo[:, q0:q0 + n], in0=PS[b][:, 0:n], in1=x12[0][:, cc, q0:q0 + n], op=ALU.mult),
                             reads=[psk[b], "x12"], writes=["tt"])
                        for i2 in range(n // 128):
                            tc = q0 // 128 + i2
                            b2 = nps()
                            K.op("pe", lambda e: e.transpose(PS[b2][:, 0:128], cvo[:, tc * 128:(tc + 1) * 128], ident), reads=["tt", "ident"], writes=[psk[b2]])
                            evac(u2[:, tc, cc * 128:(cc + 1) * 128], PS[b2][:, 0:128], [psk[b2]], ["u2"])
                    else:
                        K.op("dve", lambda e: e.tensor_tensor(out=hT[:, 12 + cc, t0 + q0:t0 + q0 + n], in0=PS[b][:, 0:n], in1=x12[1][:, cc, q0:q0 + n], op=ALU.mult),
                             reads=[psk[b], "x12"], writes=["hT"])
            if o == 0:
                K.op("pool", lambda e: e.tensor_copy(out=u_tm[:, 0:nch, :], in_=u2[:, 0:nch, :]), reads=["u2", "u_tm"], writes=["u_tm"])

    def mix_params(l):
        lam_init = 0.8 - 0.6 * math.exp(-0.3 * l)
        K.dma("sp", lam_t, dlam[l:l + 1].rearrange("o a b -> o (a b)").partition_broadcast(128), writes=["lam_t"])
        K.op("dve", lambda e: e.tensor_tensor(out=lam_t[:, 0:64], in0=lam_t[:, 0:64], in1=lam_t[:, 64:128], op=ALU.mult), reads=["lam_t"], writes=["lam_t"])
        K.op("dve", lambda e: e.tensor_tensor(out=lam_t[:, 128:192], in0=lam_t[:, 128:192], in1=lam_t[:, 192:256], op=ALU.mult), reads=["lam_t"], writes=["lam_t"])
        K.op("dve", lambda e: e.reduce_sum(out=lam_s[:, 0:2], in_=lam_t.rearrange("p (a b) -> p a b", b=128)[:, :, 0:64], axis=AX.X),
             reads=["lam_t"], writes=["lam_s"])
        K.op("act", lambda e: e.activation(out=lam_s[:, 0:2], in_=lam_s[:, 0:2], func=AF.Exp), reads=["lam_s"], writes=["lam_s"])
        K.op("dve", lambda e: e.tensor_tensor(out=lam_s[:, 2:3], in0=lam_s[:, 1:2], in1=lam_s[:, 0:1], op=ALU.subtract), reads=["lam_s"], writes=["lam_s"])
        K.op("dve", lambda e: e.tensor_scalar(out=lam_s[:, 2:3], in0=lam_s[:, 2:3], scalar1=-lam_init, scalar2=None, op0=ALU.add), reads=["lam_s"], writes=["lam_s"])
        load_colT(og_s[:, 0:1], dout_g[l:l + 1, :], 1, "og_s")
        K.op("dve", lambda e: e.tensor_scalar(out=og_s[:, 0:1], in0=og_s[:, 0:1], scalar1=1.0 - lam_init, scalar2=None, op0=ALU.mult), reads=["og_s"], writes=["og_s"])
        K.dma("sp", esink, ssink[l:l + 1, :].partition_broadcast(128), writes=["esink"])
        K.op("act", lambda e: e.activation(out=esink, in_=esink, func=AF.Exp), reads=["esink"], writes=["esink"])
        load_colT(mng, mnorm_g[l].rearrange("(c p) -> c p", p=128), 4, "mng")
        for r in range(3):
            load_colT(hcw[:, r * 12:(r + 1) * 12], hy_cw[l, r].rearrange("(c p) -> c p", p=128), 12, "hcw")
        load_colT(hcb, hy_cb[l].rearrange("(c p) -> c p", p=128), 12, "hcb")

    def MIX(l):
        mix_params(l)
        for which in "BCAD":
            if which not in MIXSEL:
                continue
            for (t0, T, is_s) in SEGS:
                if which in "BC":
                    attn_core(l, t0, T, is_s, which)
                elif which == "A":
                    mlstm(l, t0, T, is_s)
                else:
                    hyena(l, t0, T, is_s)
            K.fence()

    def mod_load(l, g):
        i = g % 2
        wv_ = w_mod[l].rearrange("(kc p) n -> p kc n", p=128)
        K.dma("pool", wbuf[i], wv_[:, :, g * 512:(g + 1) * 512], writes=[f"wbuf{i}"])

    def mod_compute(l, g):
        i = g % 2
        b = g % 4
        for j in range(4):
            for kc in range(16):
                K.op("pe", lambda e: e.matmul(PS[b][:, j * 2:j * 2 + 2], lhsT=wbuf[i][:, kc, j * 128:(j + 1) * 128],
                                              rhs=scb.rearrange("p (j k) -> p k j", j=2)[:, kc, :],
                                              start=(kc == 0), stop=(kc == 15)),
                     reads=[f"wbuf{i}", "scb"], writes=[psk[b]])
        K.op("dve", lambda e: e.tensor_tensor(out=modTs[l][:, g * 4:(g + 1) * 4, :], in0=PS[b][:, 0:8].rearrange("p (c j) -> p c j", j=2),
                                              in1=bmTs[l][:, g * 4:(g + 1) * 4].unsqueeze(2).to_broadcast([128, 4, 2]), op=ALU.add),
             reads=[psk[b], f"bmT{l}"], writes=[f"modT{l}"])
        for which, gT, gk, glast in ((0, n1Ts[l], f"n1T{l}", 7), (1, n2Ts[l], f"n2T{l}", 19)):
            if g == glast:
                base = 16 if which == 0 else 64
                K.op("dve", lambda e: e.scalar_tensor_tensor(out=nscales[l][:, which], in0=modTs[l][:, base:base + 16, :], scalar=1.0,
                                                             in1=gT.unsqueeze(2).to_broadcast([128, 16, 2]),
                                                             op0=ALU.add, op1=ALU.mult),
                     reads=[f"modT{l}", gk], writes=[f"nscale{l}"])

    def mod_atoms(l, g0, g1):
        seq = []
        for g in range(g0, g1):
            seq.append(("L", g))
        out_ = []
        ng = g1 - g0
        for k in range(ng + 1):
            if k < ng:
                out_.append(("L", g0 + k))
            if k >= 1:
                out_.append(("C", g0 + k - 1))
        for kind, g in out_:
            if kind == "L":
                mod_load(l, g)
            else:
                mod_compute(l, g)
            K.end_atom()

    for l_ in range(DEPTH):
        load_colT(bmTs[l_], b_mod[l_].rearrange("(c p) -> c p", p=128), 96, f"bmT{l_}")
        load_colT(n1Ts[l_], norm1_g[l_].rearrange("(c p) -> c p", p=128), 16, f"n1T{l_}")
        load_colT(n2Ts[l_], norm2_g[l_].rearrange("(c p) -> c p", p=128), 16, f"n2T{l_}")
    mod_load(0, 0)
    for g in range(8):
        if g + 1 < 8:
            mod_load(0, g + 1)
        mod_compute(0, g)
    K.record()
    mod_atoms(0, 8, 24)
    mod_atoms(1, 0, 24)
    K.stop_record()
    mod_queue = K.aux
    K.aux = []

    for l in range(DEPTH):
        modT = modTs[l]
        nscale = nscales[l]
        mk = f"modT{l}"
        nk_ = f"nscale{l}"
        for tb in range(3):
            t0, cj, sh = norm_block(l, 0, tb, False)
            for c in range(16):
                K.op("dve", lambda e: e.tensor_tensor(out=tmpf, in0=xblk[:, c, :], in1=rstd, op=ALU.mult),
                     reads=["xblk", "rstd"], writes=["tmpf"])
                K.op("act", lambda e: e.activation(out=hT[:, c, t0:t0 + TB], in_=tmpf, func=AF.Identity,
                                                   scale=nscale[:, 0, c, cj:cj + 1], bias=modT[:, sh * 16 + c, cj:cj + 1]),
                     reads=["tmpf", nk_, mk], writes=["hT"])
        if debug:
            dbg[f"hT{l}"] = (hT, [128, 16, NT], "hT")

        K.fence()
        if "D" in MIXSEL:
            K.record()
            for L in (256, 1024):
                hy_filters(l, L)
            K.stop_record()
        wv = w_in[l].rearrange("(kc p) n -> p kc n", p=128)
        groups = [
            (0, 512, (zAT, 0), None), (512, 512, (zAT, 512), ZC["Ak"]), (1024, 512, None, ZC["Av"]),
            (1536, 512, (zAT, 1024), None), (2048, 16, (zAT, 1536), None),
            (2064, 512, None, ZC["Bq"]), (2576, 512, None, ZC["Bk"]), (3088, 512, None, ZC["Bv"]),
            (3600, 512, None, ZC["Cq"]), (4112, 512, None, ZC["Ck"]),
            (4624, 512, (zDT, 0), None), (5136, 512, (zDT, 512), None), (5648, 512, (zDT, 1024), None),
        ]
        for (c0, ncols, fm, tm) in groups:
            i = load_w(wv[:, :, c0:c0 + ncols], ncols)
            if fm is not None:
                dst, r0 = fm
                for j in range((ncols + 127) // 128):
                    m = min(128, ncols - j * 128)
                    for tb in range(3):
                        b = nps()
                        for kc in range(16):
                            K.op("pe", lambda e: e.matmul(PS[b][0:m, :], lhsT=wbuf[i][:, kc, j * 128:j * 128 + m],
                                                          rhs=hT[:, kc, tb * TB:(tb + 1) * TB], start=(kc == 0), stop=(kc == 15)),
                                 reads=[f"wbuf{i}", "hT"], writes=[psk[b]])
                        s = srr[0]
                        srr[0] = (s + 1) % 4
                        evac(stage[s][0:m, :], PS[b][0:m, :], [psk[b]], [f"stage{s}"])
                        K.dma("sp", dst[r0 + j * 128:r0 + j * 128 + m, tb * TB:(tb + 1) * TB], stage[s][0:m, :],
                              reads=[f"stage{s}"], writes=[dst.tensor.name])
                        K.tick()
            if tm is not None:
                for tc in range(NT // 128):
                    b = nps()
                    for kc in range(16):
                        K.op("pe", lambda e: e.matmul(PS[b][:, 0:ncols], lhsT=hT[:, kc, tc * 128:(tc + 1) * 128],
                                                      rhs=wbuf[i][:, kc, 0:ncols], start=(kc == 0), stop=(kc == 15)),
                             reads=[f"wbuf{i}", "hT"], writes=[psk[b]])
                    s = srr[0]
                    srr[0] = (s + 1) % 4
                    evac(stage[s][:, 0:ncols], PS[b][:, 0:ncols], [psk[b]], [f"stage{s}"])
                    K.dma("sp", ztm[tc * 128:(tc + 1) * 128, tm:tm + ncols], stage[s][:, 0:ncols],
                          reads=[f"stage{s}"], writes=["ztm"])
                    K.tick()


        K.flush()
        if debug and debug < 0 and l == 0:
            break
        K.fence()
        if l == 0:
            K.aux = mod_queue
        MIX(l)
        K.flush()
        K.fence()
        if debug:
            dbg[f"mixT{l}"] = 1
            K.dma("pool", dbg_mix[l].rearrange("(c p) t -> p c t", p=128), hT, reads=["hT"], writes=["dbgmix"])
            K.fence()

        wv = w_out[l].rearrange("(kc p) n -> p kc n", p=128)
        for tb in range(3):
            K.dma("sp", xres[:, :, tb * TB:(tb + 1) * TB], xs[:, tb * TB:(tb + 1) * TB].rearrange("(c p) t -> p c t", p=128),
                  reads=["xs"], writes=["xblk"])
        for g in range(4):
            i = load_w(wv[:, :, g * 512:(g + 1) * 512], 512)
            for j in range(4):
                oc = g * 4 + j
                for tb in range(3):
                    t0 = tb * TB
                    cj = 0 if tb == 0 else 1
                    b = nps()
                    for kc in range(16):
                        K.op("pe", lambda e: e.matmul(PS[b], lhsT=wbuf[i][:, kc, j * 128:(j + 1) * 128],
                                                      rhs=hT[:, kc, t0:t0 + TB], start=(kc == 0), stop=(kc == 15)),
                             reads=[f"wbuf{i}", "hT"], writes=[psk[b]])
                    K.op("dve", lambda e: e.scalar_tensor_tensor(out=xres[:, oc, t0:t0 + TB], in0=PS[b], scalar=modT[:, 32 + oc, cj:cj + 1],
                                                                 in1=xres[:, oc, t0:t0 + TB], op0=ALU.mult, op1=ALU.add),
                         reads=[psk[b], mk, "xblk"], writes=["xblk"])
        if debug:
            for tb in range(3):
                K.dma("sp", xs[:, tb * TB:(tb + 1) * TB].rearrange("(c p) t -> p c t", p=128), xres[:, :, tb * TB:(tb + 1) * TB],
                      reads=["xblk"], writes=["xs"])
        for tb in range(3):
            t0, cj, _ = norm_block(l, 1, tb, True)
            for c in range(16):
                K.op("dve", lambda e: e.tensor_tensor(out=tmpf, in0=xres[:, c, t0:t0 + TB], in1=rstd, op=ALU.mult),
                     reads=["xblk", "rstd"], writes=["tmpf"])
                K.op("act", lambda e: e.activation(out=hT[:, c, t0:t0 + TB], in_=tmpf, func=AF.Identity,
                                                   scale=nscale[:, 1, c, cj:cj + 1], bias=modT[:, 48 + c, cj:cj + 1]),
                     reads=["tmpf", nk_, mk], writes=["hT"])
        if debug == 2:
            break
        K.fence()

        for r in range(3):
            load_colT(fcw[:, r * 88:(r + 1) * 88], f_cw[l, r].rearrange("(c p) -> c p", p=128), 88, "fcw")
        load_colT(fcb, f_cb[l].rearrange("(c p) -> c p", p=128), 88, "fcb")
        wvu = w_up[l].rearrange("(kc p) n -> p kc n", p=128)
        urr = 0
        dpar = 0
        for g in range(22):
            i = wrr[0]
            wrr[0] ^= 1
            K.dma("pool", wbuf[i][:, :, 0:256], wvu[:, :, g * 256:(g + 1) * 256], writes=[f"wbuf{i}"])
            K.dma("pool", wbuf[i][:, :, 256:512], wvu[:, :, D_FF + g * 256:D_FF + (g + 1) * 256], writes=[f"wbuf{i}"])
            K.dma("pool", wdn, w_dn[l, g * 256:(g + 1) * 256, :].rearrange("(k p) n -> p k n", p=128), writes=["wdn"])
            for jj in range(2):
                j = g * 2 + jj
                for half in range(2):
                    fch = j + 44 * half
                    wc = half * 256 + jj * 128
                    bset = [0, 1, 2] if half == 0 else [3, 4, 5]
                    for tb in range(3):
                        b = bset[tb]
                        for kc in range(16):
                            K.op("pe", lambda e: e.matmul(PS[b], lhsT=wbuf[i][:, kc, wc:wc + 128], rhs=hT[:, kc, tb * TB:(tb + 1) * TB],
                                                          start=(kc == 0), stop=(kc == 15)),
                                 reads=[f"wbuf{i}", "hT"], writes=[psk[b]])
                    for tb in range(3):
                        b = bset[tb]
                        t0 = tb * TB
                        u = ubuf[urr]
                        uk = f"ubuf{urr}"
                        urr ^= 1
                        ranges = [(0, 256), (256, 512)] if tb == 0 else [(0, 512)]
                        K.op("act", lambda e: e.activation(out=u, in_=PS[b], func=AF.Identity, scale=fcw[:, 88 + fch:88 + fch + 1],
                                                           bias=fcb[:, fch:fch + 1]),
                             reads=[psk[b], "fcw", "fcb"], writes=[uk])
                        for (ra, rb) in ranges:
                            K.op("dve", lambda e: e.scalar_tensor_tensor(out=u[:, ra + 1:rb], in0=PS[b][:, ra:rb - 1],
                                                                         scalar=fcw[:, fch:fch + 1], in1=u[:, ra + 1:rb],
                                                                         op0=ALU.mult, op1=ALU.add),
                                 reads=[psk[b], "fcw", uk], writes=[uk])
                            K.op("dve", lambda e: e.scalar_tensor_tensor(out=u[:, ra:rb - 1], in0=PS[b][:, ra + 1:rb],
                                                                         scalar=fcw[:, 176 + fch:176 + fch + 1], in1=u[:, ra:rb - 1],
                                                                         op0=ALU.mult, op1=ALU.add),
                                 reads=[psk[b], "fcw", uk], writes=[uk])
                        if tb == 1:
                            bn_ = bset[2]
                            K.op("dve", lambda e: e.scalar_tensor_tensor(out=u[:, TB - 1:TB], in0=PS[bn_][:, 0:1],
                                                                         scalar=fcw[:, 176 + fch:176 + fch + 1], in1=u[:, TB - 1:TB],
                                                                         op0=ALU.mult, op1=ALU.add),
                                 reads=[psk[bn_], "fcw", uk], writes=[uk])
                        if tb == 2:
                            bp_ = bset[1]
                            K.op("dve", lambda e: e.scalar_tensor_tensor(out=u[:, 0:1], in0=PS[bp_][:, TB - 1:TB],
                                                                         scalar=fcw[:, fch:fch + 1], in1=u[:, 0:1],
                                                                         op0=ALU.mult, op1=ALU.add),
                                 reads=[psk[bp_], "fcw", uk], writes=[uk])
                        if half == 0:
                            K.op("act", lambda e: e.activation(out=actTg[:, jj, t0:t0 + TB], in_=u, func=AF.Silu), reads=[uk], writes=["actTg"])
                        else:
                            K.op("pool", lambda e: e.tensor_tensor(out=actTg[:, jj, t0:t0 + TB], in0=actTg[:, jj, t0:t0 + TB], in1=u, op=ALU.mult),
                                 reads=[uk, "actTg"], writes=["actTg"])
            for oc in range(16):
                for tb in range(3):
                    t0 = tb * TB
                    cj = 0 if tb == 0 else 1
                    b = 6 + dpar
                    dpar ^= 1
                    for kc in range(2):
                        K.op("pe", lambda e: e.matmul(PS[b], lhsT=wdn[:, kc, oc * 128:(oc + 1) * 128], rhs=actTg[:, kc, t0:t0 + TB],
                                                      start=(kc == 0), stop=(kc == 1)),
                             reads=["wdn", "actTg"], writes=[psk[b]])
                    K.op("dve", lambda e: e.scalar_tensor_tensor(out=xres[:, oc, t0:t0 + TB], in0=PS[b], scalar=modT[:, 80 + oc, cj:cj + 1],
                                                                 in1=xres[:, oc, t0:t0 + TB], op0=ALU.mult, op1=ALU.add),
                         reads=[psk[b], mk, "xblk"], writes=["xblk"])
        K.fence()
        if l < DEPTH - 1 or debug:
            for tb in range(3):
                K.dma("sp", xs[:, tb * TB:(tb + 1) * TB].rearrange("(c p) t -> p c t", p=128), xres[:, :, tb * TB:(tb + 1) * TB],
                      reads=["xblk"], writes=["xs"])
        if l == DEPTH - 1:
            for tq in range(NT // 128):
                for g in range(4):
                    b = nps()
                    for jq in range(4):
                        fc = g * 4 + jq
                        K.op("pe", lambda e: e.transpose(PS[b][:, jq * 128:(jq + 1) * 128], xres[:, fc, tq * 128:(tq + 1) * 128], ident),
                             reads=["xblk", "ident"], writes=[psk[b]])
                    evac(ytile[:, g * 512:(g + 1) * 512], PS[b], [psk[b]], ["ytile"])
                K.dma("sp", y_out[tq * 128:(tq + 1) * 128, :], ytile, reads=["ytile"], writes=["y_out"])
        K.fence()

    K.finish()
    return nc, dbg


def host_consts():
    c = {}
    c["k_ident"] = np.eye(128, dtype=np.float32)
    sel = np.zeros((8, 8, 128), np.float32)
    for p in range(8):
        sel[p, p, :] = 1.0
    c["k_sel"] = sel.reshape(8, 8 * 128)
    s = np.arange(128)[:, None]
    t = np.arange(128)[None, :]
    c["k_tri"] = np.concatenate([(s <= t), (s >= t)], axis=1).astype(np.float32)
    tq = np.arange(384)[None, :] - 128
    c["k_band"] = (np.abs(s - tq) <= 128).astype(np.float32)
    for dh, nm in ((64, "k_rope64"), (128, "k_rope128")):
        n_tok = 1024
        t_row = np.repeat(np.arange(n_tok // 64, dtype=np.float32), 64)
        t_col = np.tile(np.arange(64, dtype=np.float32), n_tok // 64)
        n_freq = dh // 4
        inv = (10000.0 ** (-np.arange(n_freq, dtype=np.float32) / n_freq)).astype(np.float32)
        ang = np.concatenate([t_row[:, None] * inv, t_col[:, None] * inv], axis=-1).astype(np.float32)
        c[nm] = np.concatenate([np.cos(ang), np.sin(ang)], axis=1).astype(np.float32)
    for L in (256, 1024):
        tt = np.linspace(0.0, 1.0, L, dtype=np.float32)[:, None]
        w = (np.float32(2.0 * math.pi / L) * np.arange(L, dtype=np.float32))[:, None]
        bands = np.linspace(1e-4, 15, 16, dtype=np.float32)[None, :]
        feats = np.concatenate([tt, np.cos(bands * w), -np.sin(bands * w), np.ones((L, 1), np.float32)], axis=-1)
        c[f"k_feat{L}"] = np.ascontiguousarray(feats.T).astype(np.float32)
        rates = np.abs(np.linspace(HY_FAST, HY_SLOW, 512, dtype=np.float32))
        c[f"k_win{L}"] = np.exp(-tt * rates).astype(np.float32)
        f = np.arange(L, dtype=np.float64)[:, None] + 0.5
        tt64 = np.arange(L, dtype=np.float64)[None, :]
        th = np.pi * f * tt64 / L
        C = np.cos(th)
        S = np.sin(th)
        c[f"k_dft{L}"] = np.stack([C.T, S.T, C, S]).astype(np.float32)
    return c


def make_in_maps(inputs, n_cores=8):
    consts = host_consts()
    maps = []
    f = lambda a: np.ascontiguousarray(a, dtype=np.float32)
    for c in range(n_cores):
        b = c % 4
        m = dict(consts)
        m["xin"] = f(np.concatenate([inputs["x_prompt"][2 * c], inputs["x_prompt"][2 * c + 1], inputs["x_sample"][b]], axis=0))
        m["cond2"] = f(np.stack([inputs["c_ctx"], inputs["c"][b]]))
        m["c_dk"] = f(inputs["cache_diff_k"][b].reshape(DEPTH, PAST, 512))
        m["c_dv"] = f(inputs["cache_diff_v"][b].reshape(DEPTH, PAST, 512))
        m["c_sk"] = f(inputs["cache_swa_k"][b].reshape(DEPTH, PAST, 256))
        m["c_sv"] = f(inputs["cache_swa_v"][b].reshape(DEPTH, PAST, 256))
        m["st_C"] = f(inputs["state_mlstm_C"][b])
        m["st_n"] = f(inputs["state_mlstm_n"][b])
        m["st_m"] = f(inputs["state_mlstm_m"][b].reshape(DEPTH, 8))
        for k in ("w_mod", "b_mod", "norm1_g", "norm2_g", "w_in", "mlstm_norm_g", "diff_qn_g", "diff_kn_g", "diff_lam",
                  "diff_out_g", "swa_qn_g", "swa_kn_g", "swa_sink", "hy_conv_w", "hy_conv_b", "hy_w1", "hy_b1", "hy_freq",
                  "hy_w2", "hy_b2", "hy_w3", "hy_b3", "hy_bias", "w_out", "ffn_w_up", "ffn_conv_w", "ffn_conv_b", "ffn_w_down"):
            m[k] = f(inputs[k])
        m["mlstm_ig_b"] = f(inputs["mlstm_ig_b"].reshape(DEPTH, 8))
        m["mlstm_fg_b"] = f(inputs["mlstm_fg_b"].reshape(DEPTH, 8))
        maps.append(m)
    return maps


_CACHE = {}


def kernel(**inputs):
    inputs = {k: np.asarray(v) for k, v in inputs.items()}
    if "nc" not in _CACHE:
        _CACHE["nc"] = build(debug=False)[0]
    nc = _CACHE["nc"]
    maps = make_in_maps(inputs, 8)
    res = run_bass_kernel_spmd(nc, maps, core_ids=list(range(8)))
    R = res.results
    y_prompt = np.zeros((16, 256, D), np.float32)
    y_sample = np.zeros((4, 1024, D), np.float32)
    ndk = np.zeros((16, DEPTH, 256, 4, 2, 64), np.float32)
    ndv = np.zeros((16, DEPTH, 256, 4, 128), np.float32)
    nsk = np.zeros((16, DEPTH, 256, 2, 128), np.float32)
    nsv = np.zeros((16, DEPTH, 256, 2, 128), np.float32)
    nC = np.zeros((16, DEPTH, 2, 4, 128, 128), np.float32)
    nn = np.zeros((16, DEPTH, 2, 4, 128), np.float32)
    nm = np.zeros((16, DEPTH, 2, 4), np.float32)
    for c in range(8):
        r = R[c]
        y = np.asarray(r["y_out"])
        y_prompt[2 * c] = y[0:256]
        y_prompt[2 * c + 1] = y[256:512]
        if c < 4:
            y_sample[c] = y[512:1536]
        for s in range(2):
            b = 2 * c + s
            ndk[b] = np.asarray(r["o_dk"])[s].reshape(DEPTH, 256, 4, 2, 64)
            ndv[b] = np.asarray(r["o_dv"])[s].reshape(DEPTH, 256, 4, 128)
            nsk[b] = np.asarray(r["o_sk"])[s].reshape(DEPTH, 256, 2, 128)
            nsv[b] = np.asarray(r["o_sv"])[s].reshape(DEPTH, 256, 2, 128)
            nC[b] = np.asarray(r["o_C"])[s]
            nn[b] = np.asarray(r["o_n"])[s]
            nm[b] = np.asarray(r["o_m"])[s].reshape(DEPTH, 2, 4)
    return (y_prompt, y_sample, ndk, ndv, nsk, nsv, nC, nn, nm)
```

```python
import math
import numpy as np
import concourse.bass as bass
import concourse.mybir as mybir
from concourse.bass_utils import run_bass_kernel_spmd

F32 = mybir.dt.float32
BF16 = mybir.dt.bfloat16
I32 = mybir.dt.int32
AF = mybir.ActivationFunctionType
ALU = mybir.AluOpType
AX = mybir.AxisListType

D = 2048
DEPTH = 2
NT = 1536
TB = 512
SEGS = [(0, 256, False), (256, 256, False), (512, 1024, True)]
N_IN = 6160
D_FF = 5632
MIXSEL = "ABCD"
EPS = 1e-6
PAST = 512
HY_FAST = math.log(1e-2) / 0.3
HY_SLOW = math.log(1e-2) / 1.5


class _Rec:
    def __init__(self):
        self.call = None

    def __getattr__(self, name):
        def f(*a, **kw):
            self.call = (name, a, kw)
            return self
        return f


class KB:
    def __init__(self, nc):
        self.nc = nc
        self.eng = {"pe": nc.tensor, "act": nc.scalar, "dve": nc.vector, "pool": nc.gpsimd, "sp": nc.sync}
        self.csem = {e: nc.alloc_semaphore("cs_" + e) for e in ("pe", "act", "dve", "pool")}
        self.ccnt = {e: 0 for e in self.csem}
        self.dsem = {q: [[nc.alloc_semaphore(f"ds_{q}{i}"), 0] for i in range(n)] for q, n in (("sp", 12), ("pool", 8))}
        self.dnext = {"sp": 0, "pool": 0}
        self.seen = {e: {} for e in self.eng}
        self.lastw = {}
        self.readers = {}
        self.nsb = 0
        self.sems = {}
        self.rec = None
        self.aux = []

    def sb(self, shape, dt, name=None):
        self.nsb += 1
        return self.nc.alloc_sbuf_tensor(name or f"sb{self.nsb}", list(shape), dt).ap()

    def _need(self, stream, tk):
        if tk is None:
            return
        sem, val = tk
        key = id(sem)
        self.sems[key] = sem
        if self.seen[stream].get(key, 0) >= val:
            return
        self.eng[stream].wait_ge(sem, val)
        self.seen[stream][key] = val

    def _deps(self, stream, reads, writes, pe_skip=False):
        for k in reads:
            tk = self.lastw.get(k)
            if tk is not None and not (pe_skip and tk[0] is self.csem["pe"]):
                self._need(stream, tk)
        for k in writes:
            tk = self.lastw.get(k)
            if tk is not None and not (pe_skip and tk[0] is self.csem["pe"]):
                self._need(stream, tk)
            for tk in self.readers.get(k, {}).values():
                if not (pe_skip and tk[0] is self.csem["pe"]):
                    self._need(stream, tk)

    def _record(self, tk, reads, writes):
        for k in reads:
            d = self.readers.setdefault(k, {})
            old = d.get(id(tk[0]))
            if old is None or old[1] < tk[1]:
                d[id(tk[0])] = tk
        for k in writes:
            self.lastw[k] = tk
            self.readers[k] = {}

    def op(self, e, fn, reads=(), writes=()):
        if self.rec is not None:
            r = _Rec()
            fn(r)
            self.rec.append(("op", e, r.call, list(reads), list(writes)))
            return None
        self._deps(e, reads, writes, pe_skip=(e == "pe"))
        ins = fn(self.eng[e])
        self.ccnt[e] += 1
        ins.then_inc(self.csem[e], 1)
        tk = (self.csem[e], self.ccnt[e])
        self._record(tk, reads, writes)
        return tk

    def dma(self, q, out, in_, reads=(), writes=(), **kw):
        if self.rec is not None:
            self.rec.append(("dma", q, out, in_, list(reads), list(writes), kw))
            return None
        self._deps(q, reads, writes)
        i = self.dnext[q]
        self.dnext[q] = (i + 1) % len(self.dsem[q])
        slot = self.dsem[q][i]
        if slot[1] > 0:
            self._need(q, (slot[0], slot[1]))
        ins = self.eng[q].dma_start(out=out, in_=in_, **kw)
        slot[1] += 16
        ins.then_inc(slot[0], 16)
        tk = (slot[0], slot[1])
        self._record(tk, reads, writes)
        return tk

    def record(self):
        self.rec = []

    def end_atom(self):
        if self.rec:
            self.aux.append(self.rec)
        self.rec = []

    def stop_record(self):
        self.end_atom()
        self.rec = None

    def tick(self, n=1):
        if self.rec is not None:
            return
        for _ in range(n):
            if not self.aux:
                return
            atom = self.aux.pop(0)
            for it in atom:
                if it[0] == "op":
                    _, e, (name, a, kw), reads, writes = it
                    self.op(e, lambda eng: getattr(eng, name)(*a, **kw), reads, writes)
                else:
                    _, q, out, in_, reads, writes, kw = it
                    self.dma(q, out, in_, reads, writes, **kw)

    def flush(self):
        while self.aux:
            self.tick()

    def fence(self):
        assert self.rec is None
        tks = [(self.csem[e], self.ccnt[e]) for e in self.csem if self.ccnt[e] > 0]
        for q in self.dsem:
            tks += [(sem, val) for sem, val in self.dsem[q] if val > 0]
        for st in self.eng:
            for tk in tks:
                self._need(st, tk)

    def finish(self):
        for q in self.dsem:
            for sem, val in self.dsem[q]:
                if val > 0:
                    self._need("sp", (sem, val))


def build(debug=False):
    nc = bass.Bass("TRN2", target_bir_lowering=False)
    K = KB(nc)
    dbg = {}

    def din(name, shape):
        return nc.dram_tensor(name, list(shape), F32, kind="ExternalInput").ap()

    def dout(name, shape):
        return nc.dram_tensor(name, list(shape), F32, kind="ExternalOutput").ap()

    def dscr(name, shape, dt=F32):
        if debug:
            return nc.dram_tensor(name, list(shape), dt, kind="ExternalOutput").ap()
        return nc.dram_tensor(name, list(shape), dt).ap()

    xin = din("xin", [NT, D])
    cond2 = din("cond2", [2, D])
    c_dk = din("c_dk", [DEPTH, PAST, 512])
    c_dv = din("c_dv", [DEPTH, PAST, 512])
    c_sk = din("c_sk", [DEPTH, PAST, 256])
    c_sv = din("c_sv", [DEPTH, PAST, 256])
    st_C = din("st_C", [DEPTH, 2, 4, 128, 128])
    st_n = din("st_n", [DEPTH, 2, 4, 128])
    st_m = din("st_m", [DEPTH, 8])
    w_mod = din("w_mod", [DEPTH, D, 6 * D])
    b_mod = din("b_mod", [DEPTH, 6 * D])
    norm1_g = din("norm1_g", [DEPTH, D])
    norm2_g = din("norm2_g", [DEPTH, D])
    w_in = din("w_in", [DEPTH, D, N_IN])
    ig_b = din("mlstm_ig_b", [DEPTH, 8])
    fg_b = din("mlstm_fg_b", [DEPTH, 8])
    mnorm_g = din("mlstm_norm_g", [DEPTH, 512])
    dqn_g = din("diff_qn_g", [DEPTH, 64])
    dkn_g = din("diff_kn_g", [DEPTH, 64])
    dlam = din("diff_lam", [DEPTH, 4, 64])
    dout_g = din("diff_out_g", [DEPTH, 128])
    sqn_g = din("swa_qn_g", [DEPTH, 128])
    skn_g = din("swa_kn_g", [DEPTH, 128])
    ssink = din("swa_sink", [DEPTH, 4])
    hy_cw = din("hy_conv_w", [DEPTH, 3, 1536])
    hy_cb = din("hy_conv_b", [DEPTH, 1536])
    hy_w1 = din("hy_w1", [DEPTH, 33, 64])
    hy_b1 = din("hy_b1", [DEPTH, 64])
    hy_fr = din("hy_freq", [DEPTH, 64])
    hy_w2 = din("hy_w2", [DEPTH, 64, 64])
    hy_b2 = din("hy_b2", [DEPTH, 64])
    hy_w3 = din("hy_w3", [DEPTH, 64, 2048])
    hy_b3 = din("hy_b3", [DEPTH, 2048])
    hy_bias = din("hy_bias", [DEPTH, 2, 512])
    w_out = din("w_out", [DEPTH, D, D])
    w_up = din("ffn_w_up", [DEPTH, D, 2 * D_FF])
    f_cw = din("ffn_conv_w", [DEPTH, 3, 2 * D_FF])
    f_cb = din("ffn_conv_b", [DEPTH, 2 * D_FF])
    w_dn = din("ffn_w_down", [DEPTH, D_FF, D])
    k_ident = din("k_ident", [128, 128])
    k_sel = din("k_sel", [8, 8 * 128])
    k_tri = din("k_tri", [128, 256])
    k_band = din("k_band", [128, 384])
    k_rope64 = din("k_rope64", [1024, 64])
    k_rope128 = din("k_rope128", [1024, 128])
    k_feat = {L: din(f"k_feat{L}", [34, L]) for L in (256, 1024)}
    k_win = {L: din(f"k_win{L}", [L, 512]) for L in (256, 1024)}
    k_dft = {L: din(f"k_dft{L}", [4, L, L]) for L in (256, 1024)}

    y_out = dout("y_out", [NT, D])
    o_dk = dout("o_dk", [2, DEPTH, 256, 512])
    o_dv = dout("o_dv", [2, DEPTH, 256, 512])
    o_sk = dout("o_sk", [2, DEPTH, 256, 256])
    o_sv = dout("o_sv", [2, DEPTH, 256, 256])
    o_C = dout("o_C", [2, DEPTH, 2, 4, 128, 128])
    o_n = dout("o_n", [2, DEPTH, 2, 4, 128])
    o_m = dout("o_m", [2, DEPTH, 8])

    xs = dscr("xs", [D, NT])
    zAT = dscr("zAT", [1552, NT])
    zDT = dscr("zDT", [1536, NT])
    ztm = dscr("ztm", [NT, 3584])
    ZC = dict(Av=0, Ak=512, Bq=1024, Bk=1536, Bv=2048, Cq=2560, Ck=3072, Cv=3328)
    hyG = {L: dscr(f"hyG{L}", [2, 2, L, 512]) for L in (256, 1024)}

    ident = K.sb([128, 128], F32, "ident")
    onesb = K.sb([128, 128], BF16, "onesb")
    sel = K.sb([4, 4 * 128], F32, "sel")
    tri = K.sb([128, 256], BF16, "tri")
    band = K.sb([128, 384], BF16, "band")
    hT = K.sb([128, 16, NT], BF16, "hT")
    PS = [nc.alloc_psum_tensor(f"ps{i}", [128, 512], F32).ap() for i in range(8)]
    psk = [("ps", i) for i in range(8)]
    psi = [0]

    def nps():
        i = psi[0]
        psi[0] = (i + 1) % 8
        return i

    K.dma("sp", ident, k_ident, writes=["ident"])
    K.dma("sp", sel, k_sel[0:4, 0:512], writes=["sel"])
    K.dma("pool", tri, k_tri, writes=["tri"])
    K.dma("pool", band, k_band, writes=["band"])
    K.op("dve", lambda e: e.memset(onesb, 1.0), writes=["onesb"])

    cpy_rr = [0]

    def evac(out, in_, reads, writes):
        cpy_rr[0] ^= 1
        if cpy_rr[0]:
            return K.op("act", lambda e: e.activation(out=out, in_=in_, func=AF.Copy), reads=reads, writes=writes)
        return K.op("dve", lambda e: e.tensor_copy(out=out, in_=in_), reads=reads, writes=writes)

    def evac_act(out, in_, reads, writes):
        return K.op("act", lambda e: e.activation(out=out, in_=in_, func=AF.Copy), reads=reads, writes=writes)

    vtmp = K.sb([128, 128], F32, "vtmp")

    def load_colT(dst, src2d, nrows, key):
        K.dma("sp", vtmp[0:nrows, :], src2d, writes=["vtmp"])
        b = nps()
        K.op("pe", lambda e: e.transpose(PS[b][:, 0:nrows], vtmp[0:nrows, :], ident[0:nrows, 0:nrows]),
             reads=["vtmp", "ident"], writes=[psk[b]])
        K.op("dve", lambda e: e.tensor_copy(out=dst, in_=PS[b][:, 0:nrows]), reads=[psk[b]], writes=[key])

    arena = K.sb([128, 24576], F32, "arena")
    xres = arena.rearrange("p (c t) -> p c t", c=16)
    xblk = arena[:, 0:8192].rearrange("p (c t) -> p c t", c=16)
    xt = [arena[:, 0:2048], arena[:, 2048:4096]]
    arena2 = K.sb([128, 4608], F32, "arena2")
    wdn = arena2[:, 0:2048].bitcast(BF16).rearrange("p (k n) -> p k n", k=2)
    actTg = arena2[:, 2048:3584].bitcast(BF16).rearrange("p (k t) -> p k t", k=2)
    ubuf = [arena2[:, 3584:4096], arena2[:, 4096:4608]]
    ytile = arena2[:, 0:2048]
    yt_ = [arena2[:, 0:2048], arena2[:, 2048:4096]]
    stage = [arena2[:, i * 512:(i + 1) * 512] for i in range(4)]
    xo = [stage[2].rearrange("p (a b) -> p a b", a=4), stage[3].rearrange("p (a b) -> p a b", a=4)]
    for ti in range(NT // 128):
        a = ti % 2
        K.dma("sp", xt[a], xin[ti * 128:(ti + 1) * 128, :], writes=[f"xt{a}"])
        for g in range(4):
            b = nps()
            for j in range(4):
                fc = g * 4 + j
                K.op("pe", lambda e: e.transpose(PS[b][:, j * 128:(j + 1) * 128], xt[a][:, fc * 128:(fc + 1) * 128], ident),
                     reads=[f"xt{a}", "ident"], writes=[psk[b]])
            o = (ti * 4 + g) % 2
            evac(xo[o].rearrange("p a b -> p (a b)"), PS[b], [psk[b]], [f"stage{2 + o}"])
            K.dma("sp", xs[g * 512:(g + 1) * 512, ti * 128:(ti + 1) * 128].rearrange("(j p) t -> p j t", p=128),
                  xo[o], reads=[f"stage{2 + o}"], writes=["xs"])

    K.fence()
    wbuf = [K.sb([128, 16, 512], BF16, f"wbuf{i}") for i in range(2)]
    wrr = [0]
    srr = [0]
    modTs = [K.sb([128, 96, 2], F32, f"modT{i}") for i in range(DEPTH)]
    bmTs = [K.sb([128, 96], F32, f"bmT{i}") for i in range(DEPTH)]
    n1Ts = [K.sb([128, 16], F32, f"n1T{i}") for i in range(DEPTH)]
    n2Ts = [K.sb([128, 16], F32, f"n2T{i}") for i in range(DEPTH)]
    scf = K.sb([128, 32], F32, "scf")
    scb = K.sb([128, 32], BF16, "scb")
    nscales = [K.sb([128, 2, 16, 2], F32, f"nscale{i}") for i in range(DEPTH)]
    rstd = arena2[:, 2048:2560]
    tmpf = arena2[:, 2560:3072]
    sqb = arena2[:, 3072:3328].bitcast(BF16)

    load_colT(scf, cond2.rearrange("j (kc p) -> (j kc) p", p=128), 32, "scf")
    K.op("act", lambda e: e.activation(out=scb, in_=scf, func=AF.Silu), reads=["scf"], writes=["scb"])

    def load_w(src_fn, ncols):
        i = wrr[0]
        wrr[0] ^= 1
        K.dma("pool", wbuf[i][:, :, 0:ncols], src_fn, writes=[f"wbuf{i}"])
        return i

    def norm_block(l, which, tb, src_is_sbuf):
        t0 = tb * TB
        cj = 0 if tb == 0 else 1
        if not src_is_sbuf:
            K.dma("sp", xblk, xs[:, t0:t0 + TB].rearrange("(c p) t -> p c t", p=128), reads=["xs"], writes=["xblk"])
        b = nps()
        for c in range(16):
            xsrc = xres[:, c, t0:t0 + TB] if src_is_sbuf else xblk[:, c, :]
            K.op("act", lambda e: e.activation(out=sqb, in_=xsrc, func=AF.Square), reads=["xblk"], writes=["sqb"])
            K.op("pe", lambda e: e.matmul(PS[b], lhsT=onesb, rhs=sqb, start=(c == 0), stop=(c == 15)),
                 reads=["sqb", "onesb"], writes=[psk[b]])
        K.op("act", lambda e: e.activation(out=rstd, in_=PS[b], func=AF.Sqrt, scale=1.0 / D, bias=epsT[:, 0:1]),
             reads=[psk[b], "epsT"], writes=["rstd"])
        K.op("dve", lambda e: e.reciprocal(out=rstd, in_=rstd), reads=["rstd"], writes=["rstd"])
        sh = 0 if which == 0 else 3
        return t0, cj, sh

    fcw = K.sb([128, 264], F32, "fcw")
    fcb = K.sb([128, 88], F32, "fcb")
    if debug:
        dbg_mix = [dout(f"dbg_mix{i}", [D, NT]) for i in range(DEPTH)]
        dbg_h2 = [dout(f"dbg_h2{i}", [D, NT]) for i in range(DEPTH)]
    epsT = K.sb([128, 1], F32, "epsT")
    K.op("dve", lambda e: e.memset(epsT, EPS), writes=["epsT"])

    def aw(off, n):
        return arena[:, off:off + n]

    def awb(off, n):
        return arena[:, off:off + n].bitcast(BF16)

    lam_t = arena[:, 24320:24576]
    lam_s = K.sb([128, 4], F32, "lam_s")
    og_s = K.sb([128, 2], F32, "og_s")
    esink = K.sb([128, 4], F32, "esink")
    mng = K.sb([128, 4], F32, "mng")

    def attn_core(l, t0, T, is_s, which):
        nq_groups = 16 if which == "B" else 6
        gsz = 64 if which == "B" else 128
        qk_cols = 1024 if which == "B" else 768
        zq = ZC["Bq"] if which == "B" else ZC["Cq"]
        zv = ZC["Bv"] if which == "B" else ZC["Cv"]
        vw = 512 if which == "B" else 256
        nh_k = 4 if which == "B" else 2
        qkraw_ = [aw(0, 1024), aw(14848, 1024)]; qkn_ = [aw(1024, 1024), aw(15872, 1024)]
        sq_ = [aw(2048, 1024), aw(16896, 1024)]; tmp2_ = [aw(3072, 1024), aw(17920, 1024)]
        qT = awb(4096, 2048).rearrange("p (h t) -> p h t", h=4)
        kT = awb(6144, 3072).rearrange("p (h t) -> p h t", h=4)
        vv = awb(9216, 3072).rearrange("p (c n) -> p c n", c=12)
        ebuf = [awb(12288, 256), awb(12544, 256)]
        rden = aw(12800, 512); acc = aw(13312, 512); o1 = aw(13824, 512)
        ss = aw(14336, 16); ropet = aw(14352, 128); gq = aw(14592, 128); gk = aw(14720, 128)
        nctx = 4 if is_s else 0
        nk = nctx + T // 128
        ntc = T // 128
        gains = (dqn_g, dkn_g) if which == "B" else (sqn_g, skn_g)
        K.dma("sp", gq[:, 0:gsz], gains[0][l:l + 1, :].partition_broadcast(128), writes=["gq"])
        K.dma("sp", gk[:, 0:gsz], gains[1][l:l + 1, :].partition_broadcast(128), writes=["gk"])
        ckt = (c_dk if which == "B" else c_sk)
        cvt = (c_dv if which == "B" else c_sv)
        for j in range(nctx):
            K.dma("pool", vv[:, j, 0:vw], cvt[l, j * 128:(j + 1) * 128, :], writes=["vv"])
        for j in range(ntc):
            K.dma("pool", vv[:, nctx + j, 0:vw], ztm[t0 + j * 128:t0 + (j + 1) * 128, zv:zv + vw], reads=["ztm"], writes=["vv"])
        for j in range(nctx):
            qkraw = qkraw_[j % 2]
            kq = f"qkraw{j % 2}"
            K.dma("sp", qkraw[:, 0:vw], ckt[l, j * 128:(j + 1) * 128, :], writes=[kq])
            b = nps()
            for h in range(nh_k):
                K.op("pe", lambda e: e.transpose(PS[b][:, h * 128:(h + 1) * 128], qkraw[:, h * 128:(h + 1) * 128], ident),
                     reads=[kq, "ident"], writes=[psk[b]])
            evac_act(kT[:, 0:nh_k, j * 128:(j + 1) * 128], PS[b][:, 0:nh_k * 128].rearrange("p (h t) -> p h t", h=nh_k), [psk[b]], ["kT"])
        nqg = 8 if which == "B" else 4
        for tc in range(ntc):
            r0 = t0 + tc * 128
            pp = tc % 2
            qkraw, qkn, sq, tmp2 = qkraw_[pp], qkn_[pp], sq_[pp], tmp2_[pp]
            kq, kn_, ksq, kt2 = f"qkraw{pp}", f"qkn{pp}", f"sq{pp}", f"tmp2{pp}"
            K.dma("sp", qkraw[:, 0:qk_cols], ztm[r0:r0 + 128, zq:zq + qk_cols], reads=["ztm"], writes=[kq])
            K.op("dve", lambda e: e.tensor_tensor(out=sq[:, 0:qk_cols], in0=qkraw[:, 0:qk_cols], in1=qkraw[:, 0:qk_cols], op=ALU.mult),
                 reads=[kq], writes=[ksq])
            K.op("dve", lambda e: e.reduce_sum(out=ss[:, 0:nq_groups], in_=sq[:, 0:qk_cols].rearrange("p (g d) -> p g d", d=gsz), axis=AX.X),
                 reads=[ksq], writes=["ss"])
            K.op("act", lambda e: e.activation(out=ss[:, 0:nq_groups], in_=ss[:, 0:nq_groups], func=AF.Sqrt, scale=1.0 / gsz, bias=epsT[:, 0:1]),
                 reads=["ss", "epsT"], writes=["ss"])
            K.op("dve", lambda e: e.reciprocal(out=ss[:, 0:nq_groups], in_=ss[:, 0:nq_groups]), reads=["ss"], writes=["ss"])
            K.op("dve", lambda e: e.tensor_tensor(out=qkn[:, 0:qk_cols].rearrange("p (g d) -> p g d", d=gsz),
                                                  in0=qkraw[:, 0:qk_cols].rearrange("p (g d) -> p g d", d=gsz),
                                                  in1=ss[:, 0:nq_groups].unsqueeze(2).to_broadcast([128, nq_groups, gsz]), op=ALU.mult),
                 reads=[kq, "ss"], writes=[kn_])
            qv = qkn[:, 0:nqg * gsz].rearrange("p (g d) -> p g d", d=gsz)
            kv_ = qkn[:, nqg * gsz:qk_cols].rearrange("p (g d) -> p g d", d=gsz)
            K.op("dve", lambda e: e.tensor_tensor(out=qv, in0=qv, in1=gq[:, 0:gsz].unsqueeze(1).to_broadcast([128, nqg, gsz]), op=ALU.mult),
                 reads=[kn_, "gq"], writes=[kn_])
            K.op("dve", lambda e: e.tensor_tensor(out=kv_, in0=kv_, in1=gk[:, 0:gsz].unsqueeze(1).to_broadcast([128, nq_groups - nqg, gsz]), op=ALU.mult),
                 reads=[kn_, "gk"], writes=[kn_])
            if not is_s:
                seq = 0 if t0 == 0 else 1
                odst = (o_dk if which == "B" else o_sk)
                K.dma("sp", odst[seq, l, tc * 128:(tc + 1) * 128, :], qkn[:, nqg * gsz:qk_cols], reads=[kn_], writes=["okv"])
                src = qkn
            else:
                hs = gsz // 2
                rt = k_rope64 if which == "B" else k_rope128
                K.dma("sp", ropet[:, 0:gsz], rt[tc * 128:(tc + 1) * 128, :], writes=["ropet"])
                x1 = qkn[:, 0:qk_cols].rearrange("p (g d) -> p g d", d=gsz)[:, :, 0:hs]
                x2 = qkn[:, 0:qk_cols].rearrange("p (g d) -> p g d", d=gsz)[:, :, hs:gsz]
                o1v = tmp2[:, 0:qk_cols].rearrange("p (g d) -> p g d", d=gsz)[:, :, 0:hs]
                o2v = tmp2[:, 0:qk_cols].rearrange("p (g d) -> p g d", d=gsz)[:, :, hs:gsz]
                s1 = sq[:, 0:qk_cols].rearrange("p (g d) -> p g d", d=gsz)[:, :, 0:hs]
                s2 = sq[:, 0:qk_cols].rearrange("p (g d) -> p g d", d=gsz)[:, :, hs:gsz]
                cosb = ropet[:, 0:hs].unsqueeze(1).to_broadcast([128, nq_groups, hs])
                sinb = ropet[:, hs:gsz].unsqueeze(1).to_broadcast([128, nq_groups, hs])
                K.op("dve", lambda e: e.tensor_tensor(out=o1v, in0=x1, in1=cosb, op=ALU.mult), reads=[kn_, "ropet"], writes=[kt2])
                K.op("dve", lambda e: e.tensor_tensor(out=s1, in0=x2, in1=sinb, op=ALU.mult), reads=[kn_, "ropet"], writes=[ksq])
                K.op("dve", lambda e: e.tensor_tensor(out=o1v, in0=o1v, in1=s1, op=ALU.subtract), reads=[kt2, ksq], writes=[kt2])
                K.op("dve", lambda e: e.tensor_tensor(out=o2v, in0=x2, in1=cosb, op=ALU.mult), reads=[kn_, "ropet"], writes=[kt2])
                K.op("dve", lambda e: e.tensor_tensor(out=s2, in0=x1, in1=sinb, op=ALU.mult), reads=[kn_, "ropet"], writes=[ksq])
                K.op("dve", lambda e: e.tensor_tensor(out=o2v, in0=o2v, in1=s2, op=ALU.add), reads=[kt2, ksq], writes=[kt2])
                src = tmp2
            srck = kn_ if src is qkn else kt2
            b = nps()
            for h in range(4):
                K.op("pe", lambda e: e.transpose(PS[b][:, h * 128:(h + 1) * 128], src[:, h * 128:(h + 1) * 128], ident),
                     reads=[srck, "ident"], writes=[psk[b]])
            evac_act(qT[:, :, tc * 128:(tc + 1) * 128], PS[b].rearrange("p (h t) -> p h t", h=4), [psk[b]], ["qT"])
            b = nps()
            for h in range(nh_k):
                K.op("pe", lambda e: e.transpose(PS[b][:, h * 128:(h + 1) * 128], src[:, 512 + h * 128:512 + (h + 1) * 128], ident),
                     reads=[srck, "ident"], writes=[psk[b]])
            kc0 = nctx * 128 + tc * 128
            evac_act(kT[:, 0:nh_k, kc0:kc0 + 128], PS[b][:, 0:nh_k * 128].rearrange("p (h t) -> p h t", h=nh_k), [psk[b]], ["kT"])
        if not is_s:
            seq = 0 if t0 == 0 else 1
            odst = (o_dv if which == "B" else o_sv)
            K.dma("sp", odst[seq, l, :, :], ztm[t0:t0 + T, zv:zv + vw], reads=["ztm"], writes=["okvv"])
        scale = (64 ** -0.5) if which == "B" else (128 ** -0.5)
        nqb = (T + 511) // 512
        acnt = [0]
        scnt = [0]
        for h in range(4):
            for qb in range(nqb):
                q0 = qb * 512
                Nq = min(512, T - q0)
                maps = (0, 1) if which == "B" else (0,)
                for m in maps:
                    K.tick()
                    bn, bd = (4, 5) if acnt[0] % 2 == 0 else (6, 7)
                    acnt[0] += 1
                    its = []
                    for j in range(nk):
                        lat = j - nctx
                        qa, qe = 0, Nq
                        if which == "C" and is_s and lat >= 0:
                            qa = max(q0, (lat - 1) * 128) - q0
                            qe = min(q0 + Nq, (lat + 2) * 128) - q0
                            if qe <= qa:
                                continue
                        its.append((j, lat, qa, qe))

                    def emit_score(k):
                        j, lat, qa, qe = its[k]
                        b = scnt[0] % 4
                        scnt[0] += 1
                        eb = k % 2
                        if which == "B":
                            lk = kT[m * 64:(m + 1) * 64, h, j * 128:(j + 1) * 128]
                            rq = qT[m * 64:(m + 1) * 64, h, q0 + qa:q0 + qe]
                        else:
                            lk = kT[:, h // 2, j * 128:(j + 1) * 128]
                            rq = qT[:, h, q0 + qa:q0 + qe]
                        K.op("pe", lambda e: e.matmul(PS[b][:, qa:qe], lhsT=lk, rhs=rq, start=True, stop=True),
                             reads=["kT", "qT"], writes=[psk[b]])
                        K.op("act", lambda e: e.activation(out=ebuf[eb][:, qa:qe], in_=PS[b][:, qa:qe], func=AF.Exp, scale=scale),
                             reads=[psk[b]], writes=[f"ebuf{eb}"])
                        if which == "C" and is_s and lat >= 0:
                            mo = (q0 + qa) - (lat - 1) * 128
                            K.op("pool", lambda e: e.tensor_tensor(out=ebuf[eb][:, qa:qe], in0=ebuf[eb][:, qa:qe], in1=band[:, mo:mo + (qe - qa)], op=ALU.mult),
                                 reads=[f"ebuf{eb}", "band"], writes=[f"ebuf{eb}"])

                    def emit_acc(k):
                        j, lat, qa, qe = its[k]
                        eb = k % 2
                        vsl = vv[:, j, h * 128:(h + 1) * 128] if which == "B" else vv[:, j, (h // 2) * 128:(h // 2 + 1) * 128]
                        K.op("pe", lambda e: e.matmul(PS[bn][:, qa:qe], lhsT=vsl, rhs=ebuf[eb][:, qa:qe], start=(k == 0), stop=(k == len(its) - 1)),
                             reads=["vv", f"ebuf{eb}"], writes=[psk[bn]])
                        K.op("pe", lambda e: e.matmul(PS[bd][:, qa:qe], lhsT=onesb, rhs=ebuf[eb][:, qa:qe], start=(k == 0), stop=(k == len(its) - 1)),
                             reads=["onesb", f"ebuf{eb}"], writes=[psk[bd]])

                    emit_score(0)
                    for k in range(len(its)):
                        if k + 1 < len(its):
                            emit_score(k + 1)
                        emit_acc(k)
                    if which == "C":
                        K.op("dve", lambda e: e.tensor_scalar(out=rden[:, 0:Nq], in0=PS[bd][:, 0:Nq], scalar1=esink[:, h:h + 1], scalar2=None, op0=ALU.add),
                             reads=[psk[bd], "esink"], writes=["rden"])
                        K.op("dve", lambda e: e.reciprocal(out=rden[:, 0:Nq], in_=rden[:, 0:Nq]), reads=["rden"], writes=["rden"])
                        K.op("dve", lambda e: e.tensor_tensor(out=hT[:, 8 + h, t0 + q0:t0 + q0 + Nq], in0=PS[bn][:, 0:Nq], in1=rden[:, 0:Nq], op=ALU.mult),
                             reads=[psk[bn], "rden"], writes=["hT"])
                    else:
                        K.op("dve", lambda e: e.reciprocal(out=rden[:, 0:Nq], in_=PS[bd][:, 0:Nq]), reads=[psk[bd]], writes=["rden"])
                        if m == 0:
                            K.op("dve", lambda e: e.tensor_tensor(out=acc[:, 0:Nq], in0=PS[bn][:, 0:Nq], in1=rden[:, 0:Nq], op=ALU.mult),
                                 reads=[psk[bn], "rden"], writes=["acc"])
                        else:
                            K.op("dve", lambda e: e.tensor_tensor(out=o1[:, 0:Nq], in0=PS[bn][:, 0:Nq], in1=rden[:, 0:Nq], op=ALU.mult),
                                 reads=[psk[bn], "rden"], writes=["o1"])
                            K.op("dve", lambda e: e.scalar_tensor_tensor(out=acc[:, 0:Nq], in0=o1[:, 0:Nq], scalar=lam_s[:, 2:3], in1=acc[:, 0:Nq],
                                                                         op0=ALU.mult, op1=ALU.add),
                                 reads=["o1", "lam_s", "acc"], writes=["acc"])
                if which == "B":
                    K.op("act", lambda e: e.activation(out=ebuf[0][:, 0:Nq], in_=acc[:, 0:Nq], func=AF.Square), reads=["acc"], writes=["ebuf0"])
                    b = scnt[0] % 4
                    scnt[0] += 1
                    K.op("pe", lambda e: e.matmul(PS[b][:, 0:Nq], lhsT=onesb, rhs=ebuf[0][:, 0:Nq], start=True, stop=True),
                         reads=["onesb", "ebuf0"], writes=[psk[b]])
                    K.op("act", lambda e: e.activation(out=rden[:, 0:Nq], in_=PS[b][:, 0:Nq], func=AF.Sqrt, scale=1.0 / 128, bias=epsT[:, 0:1]),
                         reads=[psk[b], "epsT"], writes=["rden"])
                    K.op("dve", lambda e: e.reciprocal(out=rden[:, 0:Nq], in_=rden[:, 0:Nq]), reads=["rden"], writes=["rden"])
                    K.op("dve", lambda e: e.scalar_tensor_tensor(out=hT[:, 4 + h, t0 + q0:t0 + q0 + Nq], in0=acc[:, 0:Nq], scalar=og_s[:, 0:1],
                                                                 in1=rden[:, 0:Nq], op0=ALU.mult, op1=ALU.mult),
                         reads=["acc", "og_s", "rden"], writes=["hT"])

    def mlstm(l, t0, T, is_s):
        nch = T // 128
        blocks = [(q, min(512, T - q)) for q in range(0, T, 512)]
        gi = [aw(0, 1024)[0:4, 0:T], aw(1024, 1024)[0:4, 0:T]]
        gf = [aw(2048, 1024)[0:4, 0:T], aw(3072, 1024)[0:4, 0:T]]
        bcs = [aw(4096, 1024)[0:4, 0:T], aw(5120, 1024)[0:4, 0:T]]
        av = [aw(6144, 1024)[0:4, 0:T], aw(7168, 1024)[0:4, 0:T]]
        cm = [aw(8192, 1024)[0:4, 0:T], aw(9216, 1024)[0:4, 0:T]]
        zrow = aw(10240, 1024)[0:4, 0:T]
        aT = [aw(11264, 32).rearrange("p (j h) -> p j h", h=4), aw(11296, 32).rearrange("p (j h) -> p j h", h=4)]
        wT = [aw(11328, 32).rearrange("p (j h) -> p j h", h=4), aw(11360, 32).rearrange("p (j h) -> p j h", h=4)]
        qTb_ = [awb(11392, 512)[:, 0:T], awb(19456, 512)[:, 0:T]]
        kTb_ = [awb(11904, 512)[:, 0:T], awb(19968, 512)[:, 0:T]]
        vtm_ = [awb(12416, 528).rearrange("p (j n) -> p j n", j=8), awb(20480, 528).rearrange("p (j n) -> p j n", j=8)]
        oT_ = [aw(12944, 1024)[:, 0:T], aw(21008, 1024)[:, 0:T]]
        cB = aw(13968, 1024)[:, 0:T]
        emB = aw(14992, 1024)[:, 0:T]
        Ebuf_ = [awb(16016, 512)[:, 0:T], awb(22032, 512)[:, 0:T]]
        Pbuf_ = [awb(16528, 512)[:, 0:T], awb(22544, 512)[:, 0:T]]
        hsum = aw(17040, 1024)[:, 0:T]
        tmpn = aw(18064, 512)
        qw = awb(18576, 512)[:, 0:T]
        C0b = awb(19088, 64)
        n0rep = awb(19152, 64)
        kwb = awb(19216, 64)
        ktm = aw(19280, 128)
        smal = aw(19408, 48)
        igb = [smal[0:4, 0:1], smal[0:4, 1:2]]
        fgb = [smal[0:4, 2:3], smal[0:4, 3:4]]
        m0c = [smal[0:4, 4:5], smal[0:4, 5:6]]
        mout = [smal[0:4, 6:7], smal[0:4, 7:8]]
        n0col = smal[:, 8:9]
        rev = lambda ap: ap[:, ::-1]
        scale = 128 ** -0.5
        seq = 0 if t0 == 0 else 1
        K.op("dve", lambda e: e.memset(zrow, 0.0), writes=["zrow"])
        for i_ in range(2):
            K.op("dve", lambda e: e.memset(vtm_[i_][:, :, 128:129], 1.0), writes=[f"vtm{i_}"])
        for d in range(2):
            K.dma("sp", gi[d], zAT[1536 + d * 4:1540 + d * 4, t0:t0 + T], reads=["zAT"], writes=[f"gi{d}"])
            K.dma("sp", gf[d], zAT[1544 + d * 4:1548 + d * 4, t0:t0 + T], reads=["zAT"], writes=[f"gf{d}"])
            K.dma("sp", igb[d], ig_b[l, d * 4:(d + 1) * 4].rearrange("(p o) -> p o", o=1), writes=["smal"])
            K.dma("sp", fgb[d], fg_b[l, d * 4:(d + 1) * 4].rearrange("(p o) -> p o", o=1), writes=["smal"])
            if is_s:
                K.dma("sp", m0c[d], st_m[l, d * 4:(d + 1) * 4].rearrange("(p o) -> p o", o=1), writes=["smal"])
            K.op("dve", lambda e: e.tensor_scalar(out=gi[d], in0=gi[d], scalar1=igb[d], scalar2=None, op0=ALU.add), reads=[f"gi{d}", "smal"], writes=[f"gi{d}"])
            K.op("act", lambda e: e.activation(out=gf[d], in_=gf[d], func=AF.Sigmoid, bias=fgb[d]), reads=[f"gf{d}", "smal"], writes=[f"gf{d}"])
            K.op("act", lambda e: e.activation(out=gf[d], in_=gf[d], func=AF.Ln), reads=[f"gf{d}"], writes=[f"gf{d}"])
            o_ = (lambda ap: ap) if d == 0 else rev
            K.op("dve", lambda e: e.tensor_tensor_scan(out=o_(bcs[d]), data0=o_(gf[d]), data1=zrow, initial=0.0, op0=ALU.add, op1=ALU.add),
                 reads=[f"gf{d}", "zrow"], writes=[f"bcs{d}"])
            K.op("dve", lambda e: e.tensor_tensor(out=av[d], in0=gi[d], in1=bcs[d], op=ALU.subtract), reads=[f"gi{d}", f"bcs{d}"], writes=[f"av{d}"])
            init = m0c[d] if is_s else 0.0
            K.op("dve", lambda e: e.tensor_tensor_scan(out=o_(cm[d]), data0=o_(av[d]), data1=o_(av[d]), initial=init, op0=ALU.max, op1=ALU.max),
                 reads=[f"av{d}", "smal"], writes=[f"cm{d}"])
            K.op("dve", lambda e: e.tensor_scalar(out=cm[d], in0=cm[d], scalar1=-1.0, scalar2=None, op0=ALU.mult), reads=[f"cm{d}"], writes=[f"cm{d}"])
            K.op("dve", lambda e: e.tensor_tensor(out=gf[d], in0=cm[d], in1=bcs[d], op=ALU.subtract), reads=[f"cm{d}", f"bcs{d}"], writes=[f"gf{d}"])
            K.op("act", lambda e: e.activation(out=gf[d], in_=gf[d], func=AF.Exp), reads=[f"gf{d}"], writes=[f"gf{d}"])
            if is_s:
                K.op("act", lambda e: e.activation(out=gi[d], in_=cm[d], func=AF.Exp, bias=m0c[d]), reads=[f"cm{d}", "smal"], writes=[f"gi{d}"])
            b = nps()
            for j in range(nch):
                K.op("pe", lambda e: e.transpose(PS[b][:, j * 4:(j + 1) * 4], av[d][:, j * 128:(j + 1) * 128], ident[0:4, 0:4]),
                     reads=[f"av{d}", "ident"], writes=[psk[b]])
            K.op("dve", lambda e: e.tensor_copy(out=aT[d][:, 0:nch, :], in_=PS[b][:, 0:nch * 4].rearrange("p (j h) -> p j h", h=4)),
                 reads=[psk[b]], writes=[f"aT{d}"])
            if not is_s:
                last = slice(T - 1, T) if d == 0 else slice(0, 1)
                K.op("dve", lambda e: e.tensor_tensor(out=mout[d], in0=bcs[d][:, last], in1=cm[d][:, last], op=ALU.subtract),
                     reads=[f"bcs{d}", f"cm{d}"], writes=["smal"])
                K.dma("sp", o_m[seq, l, d * 4:(d + 1) * 4].rearrange("(p o) -> p o", o=1), mout[d], reads=["smal"], writes=["o_m"])
                K.op("act", lambda e: e.activation(out=av[d], in_=av[d], func=AF.Exp, bias=cm[d][:, last]), reads=[f"av{d}", f"cm{d}"], writes=[f"av{d}"])
                b = nps()
                for j in range(nch):
                    K.op("pe", lambda e: e.transpose(PS[b][:, j * 4:(j + 1) * 4], av[d][:, j * 128:(j + 1) * 128], ident[0:4, 0:4]),
                         reads=[f"av{d}", "ident"], writes=[psk[b]])
                K.op("dve", lambda e: e.tensor_copy(out=wT[d][:, 0:nch, :], in_=PS[b][:, 0:nch * 4].rearrange("p (j h) -> p j h", h=4)),
                     reads=[psk[b]], writes=[f"wT{d}"])
        for h in range(4):
            hp = h % 2
            qTb, kTb, vtm, oT = qTb_[hp], kTb_[hp], vtm_[hp], oT_[hp]
            kqT, kkT, kvt, koT = f"qTb{hp}", f"kTb{hp}", f"vtm{hp}", f"oT{hp}"
            K.dma("pool", qTb, zAT[h * 128:(h + 1) * 128, t0:t0 + T], reads=["zAT"], writes=[kqT])
            K.dma("pool", kTb, zAT[512 + h * 128:512 + (h + 1) * 128, t0:t0 + T], reads=["zAT"], writes=[kkT])
            K.dma("pool", vtm[:, 0:nch, 0:128], ztm[t0:t0 + T, ZC["Av"] + h * 128:ZC["Av"] + (h + 1) * 128].rearrange("(j p) d -> p j d", p=128),
                  reads=["ztm"], writes=[kvt])
            K.dma("sp", oT, zAT[1024 + h * 128:1024 + (h + 1) * 128, t0:t0 + T], reads=["zAT"], writes=[koT])
            K.op("act", lambda e: e.activation(out=oT, in_=oT, func=AF.Sigmoid), reads=[koT], writes=[koT])
            for d in range(2):
                K.tick()
                selh = sel[0:4, h * 128:(h + 1) * 128]
                for bi, (q0, n) in enumerate(blocks):
                    b = nps() % 4
                    K.op("pe", lambda e: e.matmul(PS[b][:, 0:n], lhsT=selh, rhs=cm[d][:, q0:q0 + n], start=True, stop=True),
                         reads=["sel", f"cm{d}"], writes=[psk[b]])
                    K.op("act", lambda e: e.activation(out=cB[:, q0:q0 + n], in_=PS[b][:, 0:n], func=AF.Copy), reads=[psk[b]], writes=["cB"])
                    b = nps() % 4
                    K.op("pe", lambda e: e.matmul(PS[b][:, 0:n], lhsT=selh, rhs=gf[d][:, q0:q0 + n], start=True, stop=True),
                         reads=["sel", f"gf{d}"], writes=[psk[b]])
                    K.op("act", lambda e: e.activation(out=emB[:, q0:q0 + n], in_=PS[b][:, 0:n], func=AF.Copy), reads=[psk[b]], writes=["emB"])
                bnum = [4, 5]
                bden = [6, 7]
                if is_s:
                    K.dma("pool", C0b, st_C[l, d, h], writes=["C0b"])
                    K.dma("sp", n0col, st_n[l, d, h].rearrange("(p o) -> p o", o=1), writes=["n0col"])
                    K.op("dve", lambda e: e.tensor_copy(out=n0rep, in_=n0col.to_broadcast([128, 128])), reads=["n0col"], writes=["n0rep"])
                    for bi, (q0, n) in enumerate(blocks):
                        b = nps() % 4
                        K.op("pe", lambda e: e.matmul(PS[b][:, 0:n], lhsT=selh, rhs=gi[d][:, q0:q0 + n], start=True, stop=True),
                             reads=["sel", f"gi{d}"], writes=[psk[b]])
                        K.op("dve", lambda e: e.scalar_tensor_tensor(out=qw[:, q0:q0 + n], in0=qTb[:, q0:q0 + n], scalar=scale, in1=PS[b][:, 0:n],
                                                                     op0=ALU.mult, op1=ALU.mult),
                             reads=[kqT, psk[b]], writes=["qw"])
                        K.op("pe", lambda e: e.matmul(PS[bnum[bi]][:, 0:n], lhsT=C0b, rhs=qw[:, q0:q0 + n], start=True, stop=False),
                             reads=["C0b", "qw"], writes=[psk[bnum[bi]]])
                        K.op("pe", lambda e: e.matmul(PS[bden[bi]][:, 0:n], lhsT=n0rep, rhs=qw[:, q0:q0 + n], start=True, stop=False),
                             reads=["n0rep", "qw"], writes=[psk[bden[bi]]])
                order = list(range(nch)) if d == 0 else list(range(nch - 1, -1, -1))
                started = [is_s for _ in blocks]
                pieces = {}

                def stage1(k):
                    j = order[k]
                    Ebuf, Pbuf = Ebuf_[k % 2], Pbuf_[k % 2]
                    kE, kP = f"Ebuf{k % 2}", f"Pbuf{k % 2}"
                    qa, qe = (128 * j, T) if d == 0 else (0, 128 * (j + 1))
                    K.op("act", lambda e: e.activation(out=Ebuf[:, qa:qe], in_=cB[:, qa:qe], func=AF.Exp, bias=aT[d][:, j, h:h + 1]),
                         reads=["cB", f"aT{d}"], writes=[kE])
                    tm = tri[:, 0:128] if d == 0 else tri[:, 128:256]
                    K.op("pool", lambda e: e.tensor_tensor(out=Ebuf[:, 128 * j:128 * (j + 1)], in0=Ebuf[:, 128 * j:128 * (j + 1)], in1=tm, op=ALU.mult),
                         reads=[kE, "tri"], writes=[kE])
                    pcs = []
                    for bi, (q0, n) in enumerate(blocks):
                        pa, pe_ = max(qa, q0), min(qe, q0 + n)
                        if pe_ <= pa:
                            continue
                        b = nps() % 4
                        K.op("pe", lambda e: e.matmul(PS[b][:, pa - q0:pe_ - q0], lhsT=kTb[:, j * 128:(j + 1) * 128], rhs=qTb[:, pa:pe_], start=True, stop=True),
                             reads=[kkT, kqT], writes=[psk[b]])
                        K.op("dve", lambda e: e.scalar_tensor_tensor(out=Pbuf[:, pa:pe_], in0=PS[b][:, pa - q0:pe_ - q0], scalar=scale, in1=Ebuf[:, pa:pe_],
                                                                     op0=ALU.mult, op1=ALU.mult),
                             reads=[psk[b], kE], writes=[kP])
                        pcs.append((bi, q0, n, pa, pe_))
                    pieces[k] = pcs

                def stage2(k):
                    j = order[k]
                    Pbuf = Pbuf_[k % 2]
                    kP = f"Pbuf{k % 2}"
                    for (bi, q0, n, pa, pe_) in pieces[k]:
                        lastj = ((q0 + n) // 128 - 1) if d == 0 else (q0 // 128)
                        K.op("pe", lambda e: e.matmul(PS[bnum[bi]][:, pa - q0:pe_ - q0], lhsT=vtm[:, j, 0:128], rhs=Pbuf[:, pa:pe_],
                                                      start=(not started[bi]), stop=(j == lastj)),
                             reads=[kvt, kP], writes=[psk[bnum[bi]]])
                        K.op("pe", lambda e: e.matmul(PS[bden[bi]][:, pa - q0:pe_ - q0], lhsT=onesb, rhs=Pbuf[:, pa:pe_],
                                                      start=(not started[bi]), stop=(j == lastj)),
                             reads=["onesb", kP], writes=[psk[bden[bi]]])
                        started[bi] = True

                stage1(0)
                for k in range(nch):
                    if k + 1 < nch:
                        stage1(k + 1)
                    stage2(k)
                for bi, (q0, n) in enumerate(blocks):
                    K.op("act", lambda e: e.activation(out=tmpn[:, 0:n], in_=PS[bden[bi]][:, 0:n], func=AF.Abs),
                         reads=[psk[bden[bi]]], writes=["tmpn"])
                    K.op("dve", lambda e: e.tensor_tensor(out=tmpn[:, 0:n], in0=tmpn[:, 0:n], in1=emB[:, q0:q0 + n], op=ALU.max),
                         reads=["tmpn", "emB"], writes=["tmpn"])
                    K.op("dve", lambda e: e.reciprocal(out=tmpn[:, 0:n], in_=tmpn[:, 0:n]), reads=["tmpn"], writes=["tmpn"])
                    if d == 0:
                        K.op("dve", lambda e: e.tensor_tensor(out=hsum[:, q0:q0 + n], in0=PS[bnum[bi]][:, 0:n], in1=tmpn[:, 0:n], op=ALU.mult),
                             reads=[psk[bnum[bi]], "tmpn"], writes=["hsum"])
                    else:
                        K.op("dve", lambda e: e.tensor_tensor(out=tmpn[:, 0:n], in0=PS[bnum[bi]][:, 0:n], in1=tmpn[:, 0:n], op=ALU.mult),
                             reads=[psk[bnum[bi]], "tmpn"], writes=["tmpn"])
                        K.op("dve", lambda e: e.tensor_tensor(out=hsum[:, q0:q0 + n], in0=hsum[:, q0:q0 + n], in1=tmpn[:, 0:n], op=ALU.add),
                             reads=["hsum", "tmpn"], writes=["hsum"])
                if not is_s:
                    b = nps() % 4
                    for j in range(nch):
                        K.dma("sp", ktm, ztm[t0 + j * 128:t0 + (j + 1) * 128, ZC["Ak"] + h * 128:ZC["Ak"] + (h + 1) * 128], reads=["ztm"], writes=["ktm"])
                        K.op("dve", lambda e: e.tensor_scalar(out=kwb, in0=ktm, scalar1=wT[d][:, j, h:h + 1], scalar2=None, op0=ALU.mult),
                             reads=["ktm", f"wT{d}"], writes=["kwb"])
                        K.op("pe", lambda e: e.matmul(PS[b][:, 0:129], lhsT=kwb, rhs=vtm[:, j, 0:129], start=(j == 0), stop=(j == nch - 1)),
                             reads=["kwb", kvt], writes=[psk[b]])
                    s = srr[0]
                    srr[0] = (s + 1) % 4
                    K.op("act", lambda e: e.activation(out=stage[s][:, 0:129], in_=PS[b][:, 0:129], func=AF.Copy), reads=[psk[b]], writes=[f"stage{s}"])
                    K.dma("sp", o_C[seq, l, d, h], stage[s][:, 0:128], reads=[f"stage{s}"], writes=["o_C"])
                    K.dma("sp", o_n[seq, l, d, h].rearrange("(p o) -> p o", o=1), stage[s][:, 128:129], reads=[f"stage{s}"], writes=["o_n"])
            for bi, (q0, n) in enumerate(blocks):
                K.op("act", lambda e: e.activation(out=Pbuf_[0][:, q0:q0 + n], in_=hsum[:, q0:q0 + n], func=AF.Square), reads=["hsum"], writes=["Pbuf0"])
                b = nps() % 4
                K.op("pe", lambda e: e.matmul(PS[b][:, 0:n], lhsT=onesb, rhs=Pbuf_[0][:, q0:q0 + n], start=True, stop=True),
                     reads=["onesb", "Pbuf0"], writes=[psk[b]])
                K.op("act", lambda e: e.activation(out=tmpn[:, 0:n], in_=PS[b][:, 0:n], func=AF.Sqrt, scale=1.0 / 128, bias=epsT[:, 0:1]),
                     reads=[psk[b], "epsT"], writes=["tmpn"])
                K.op("dve", lambda e: e.reciprocal(out=tmpn[:, 0:n], in_=tmpn[:, 0:n]), reads=["tmpn"], writes=["tmpn"])
                K.op("dve", lambda e: e.scalar_tensor_tensor(out=tmpn[:, 0:n], in0=hsum[:, q0:q0 + n], scalar=mng[:, h:h + 1], in1=tmpn[:, 0:n],
                                                             op0=ALU.mult, op1=ALU.mult),
                     reads=["hsum", "mng", "tmpn"], writes=["tmpn"])
                K.op("dve", lambda e: e.tensor_tensor(out=hT[:, h, t0 + q0:t0 + q0 + n], in0=tmpn[:, 0:n], in1=oT[:, q0:q0 + n], op=ALU.mult),
                     reads=["tmpn", koT], writes=["hT"])

    zpad = arena[:, 19456:19456 + 1026]
    hcw = K.sb([128, 36], F32, "hcw")
    hcb = K.sb([128, 12], F32, "hcb")
    hsm = K.sb([128, 4], F32, "hsm")
    u2buf = arena[:, 20500:20500 + 2048].bitcast(BF16).rearrange("p (j c) -> p j c", c=512)

    def hy_filters(l, L):
        nch = L // 128
        featT = aw(0, 1024)[0:34, 0:L]
        h1s = aw(1024, 1024)[0:64, 0:L]
        h2s = aw(2048, 1024)[0:65, 0:L]
        ta = aw(3072, 512)[0:64, :]
        tki = aw(3584, 512)[0:64, :].bitcast(I32)
        tkf = aw(4096, 512)[0:64, :]
        w1a = aw(4608, 64)[0:34, :]
        w2t = aw(4672, 64)[0:64, :]
        w3a = aw(4736, 2048)[0:65, :]
        wint = aw(6784, 512)
        hfb = [aw(7296, 512), aw(7808, 512)]
        Gs = [awb(8320, 2048).rearrange("p (j c) -> p j c", c=512), awb(10368, 2048).rearrange("p (j c) -> p j c", c=512)]
        Gd = [awb(12416, 2048).rearrange("p (j c) -> p j c", c=512), awb(14464, 2048).rearrange("p (j c) -> p j c", c=512)]
        CSf = [awb(16512, 512).rearrange("p (j c) -> p j c", c=128), awb(17024, 512).rearrange("p (j c) -> p j c", c=128)]
        skp = aw(18560, 512)[0:1, :]
        frc = hsm[0:64, 0:1]
        b2c = hsm[0:64, 1:2]
        K.dma("sp", featT, k_feat[L], writes=["featT"])
        K.dma("sp", w1a[0:33, :], hy_w1[l], writes=["w1a"])
        K.dma("sp", w1a[33:34, :], hy_b1[l:l + 1, :], writes=["w1a"])
        K.dma("sp", w2t, hy_w2[l], writes=["w2t"])
        K.dma("sp", w3a[0:64, :], hy_w3[l], writes=["w3a"])
        K.dma("sp", w3a[64:65, :], hy_b3[l:l + 1, :], writes=["w3a"])
        K.dma("sp", frc, hy_fr[l].rearrange("(p o) -> p o", o=1), writes=["hsm"])
        K.dma("sp", b2c, hy_b2[l].rearrange("(p o) -> p o", o=1), writes=["hsm"])
        K.op("dve", lambda e: e.memset(h2s[64:65, :], 1.0), writes=["h2s"])
        K.end_atom()

        def sin_layer(dst, dkey, lhsT, lkey, rhs_t, rkey, add_b):
            for q0 in range(0, L, 512):
                n = min(512, L - q0)
                b = nps()
                K.op("pe", lambda e: e.matmul(PS[b][0:64, 0:n], lhsT=lhsT, rhs=rhs_t[:, q0:q0 + n], start=True, stop=True),
                     reads=[lkey, rkey], writes=[psk[b]])
                if add_b:
                    K.op("dve", lambda e: e.tensor_scalar(out=ta[:, 0:n], in0=PS[b][0:64, 0:n], scalar1=b2c, scalar2=frc, op0=ALU.add, op1=ALU.mult),
                         reads=[psk[b], "hsm"], writes=["ta"])
                else:
                    K.op("dve", lambda e: e.tensor_scalar(out=ta[:, 0:n], in0=PS[b][0:64, 0:n], scalar1=frc, scalar2=None, op0=ALU.mult),
                         reads=[psk[b], "hsm"], writes=["ta"])
                K.op("dve", lambda e: e.tensor_scalar(out=tki[:, 0:n], in0=ta[:, 0:n], scalar1=float(1 / (2 * math.pi)), scalar2=None, op0=ALU.mult), reads=["ta"], writes=["tki"])
                K.op("dve", lambda e: e.tensor_copy(out=tkf[:, 0:n], in_=tki[:, 0:n]), reads=["tki"], writes=["tkf"])
                K.op("dve", lambda e: e.scalar_tensor_tensor(out=ta[:, 0:n], in0=tkf[:, 0:n], scalar=float(-2 * math.pi), in1=ta[:, 0:n], op0=ALU.mult, op1=ALU.add),
                     reads=["tkf", "ta"], writes=["ta"])
                K.op("dve", lambda e: e.tensor_scalar(out=tkf[:, 0:n], in0=ta[:, 0:n], scalar1=float(math.pi), scalar2=float(-2 * math.pi), op0=ALU.is_gt, op1=ALU.mult), reads=["ta"], writes=["tkf"])
                K.op("dve", lambda e: e.tensor_tensor(out=ta[:, 0:n], in0=ta[:, 0:n], in1=tkf[:, 0:n], op=ALU.add), reads=["ta", "tkf"], writes=["ta"])
                K.op("dve", lambda e: e.tensor_scalar(out=tkf[:, 0:n], in0=ta[:, 0:n], scalar1=float(-math.pi), scalar2=float(2 * math.pi), op0=ALU.is_lt, op1=ALU.mult), reads=["ta"], writes=["tkf"])
                K.op("dve", lambda e: e.tensor_tensor(out=ta[:, 0:n], in0=ta[:, 0:n], in1=tkf[:, 0:n], op=ALU.add), reads=["ta", "tkf"], writes=["ta"])
                K.op("act", lambda e: e.activation(out=dst[0:64, q0:q0 + n], in_=ta[:, 0:n], func=AF.Sin), reads=["ta"], writes=[dkey])
                K.end_atom()

        sin_layer(h1s, "h1s", w1a, "w1a", featT, "featT", False)
        sin_layer(h2s, "h2s", w2t, "w2t", h1s, "h1s", True)
        for tc in range(nch):
            K.dma("sp", wint, k_win[L][tc * 128:(tc + 1) * 128, :], writes=["wint"])
            for o in range(2):
                for dr in range(2):
                    g = o * 2 + dr
                    b = nps()
                    K.op("pe", lambda e: e.matmul(PS[b], lhsT=h2s[:, tc * 128:(tc + 1) * 128], rhs=w3a[:, g * 512:(g + 1) * 512], start=True, stop=True),
                         reads=["h2s", "w3a"], writes=[psk[b]])
                    K.op("dve", lambda e: e.tensor_tensor(out=hfb[dr], in0=PS[b], in1=wint, op=ALU.mult), reads=[psk[b], "wint"], writes=[f"hfb{dr}"])
                if tc == 0:
                    K.dma("sp", skp, hy_bias[l, o:o + 1, :], writes=["skp"])
                    K.op("dve", lambda e: e.tensor_tensor(out=hfb[0][0:1, :], in0=hfb[0][0:1, :], in1=skp, op=ALU.add), reads=["hfb0", "skp"], writes=["hfb0"])
                    K.op("dve", lambda e: e.memset(hfb[1][0:1, :], 0.0), reads=["hfb1"], writes=["hfb1"])
                K.op("dve", lambda e: e.tensor_tensor(out=Gs[o][:, tc, :], in0=hfb[0], in1=hfb[1], op=ALU.add), reads=["hfb0", "hfb1"], writes=["Gs"])
                K.op("dve", lambda e: e.tensor_tensor(out=Gd[o][:, tc, :], in0=hfb[1], in1=hfb[0], op=ALU.subtract), reads=["hfb0", "hfb1"], writes=["Gd"])
                K.end_atom()
        for fc in range(nch):
            for ri in range(2):
                K.dma("pool", CSf[ri][:, 0:nch, :], k_dft[L][ri, :, fc * 128:(fc + 1) * 128].rearrange("(tc p) f -> p tc f", p=128), writes=[f"CSf{ri}"])
            for o in range(2):
                for ri in range(2):
                    src = Gs[o] if ri == 0 else Gd[o]
                    b = nps()
                    for tc in range(nch):
                        K.op("pe", lambda e: e.matmul(PS[b], lhsT=CSf[ri][:, tc, :], rhs=src[:, tc, :], start=(tc == 0), stop=(tc == nch - 1)),
                             reads=[f"CSf{ri}", "Gs", "Gd"], writes=[psk[b]])
                    s = srr[0]
                    srr[0] = (s + 1) % 4
                    K.op("act", lambda e: e.activation(out=stage[s], in_=PS[b], func=AF.Copy, scale=1.0 / L), reads=[psk[b]], writes=[f"stage{s}"])
                    K.dma("sp", hyG[L][o, ri, fc * 128:(fc + 1) * 128, :], stage[s], reads=[f"stage{s}"], writes=[f"hyG{L}"])
                    K.end_atom()

    def hyena(l, t0, T, is_s):
        L = T
        nch = L // 128
        x12 = [awb(0, 2048).rearrange("p (c t) -> p c t", c=4), awb(2048, 2048).rearrange("p (c t) -> p c t", c=4)]
        u_tm = awb(4096, 2048).rearrange("p (j c) -> p j c", c=512)
        Yre = awb(6144, 2048).rearrange("p (j c) -> p j c", c=512)
        Yim = awb(8192, 2048).rearrange("p (j c) -> p j c", c=512)
        CSb = [awb(10240, 2048).rearrange("p (j t) -> p j t", t=512), awb(12288, 2048).rearrange("p (j t) -> p j t", t=512)]
        CSf_ = [[awb(14336, 512).rearrange("p (j c) -> p j c", c=128), awb(14848, 512).rearrange("p (j c) -> p j c", c=128)],
                [awb(18432, 512).rearrange("p (j c) -> p j c", c=128), awb(18944, 512).rearrange("p (j c) -> p j c", c=128)]]
        Gt_ = [[aw(15360, 512), aw(15872, 512)], [aw(22548, 512), aw(23060, 512)]]
        tt_ = [aw(16384, 512), aw(16896, 512), aw(17408, 512), aw(17920, 512)]
        u2 = awb(17408 + 1024, 1024 - 0).rearrange("p (j c) -> p j c", c=512) if False else u2buf
        cvo = aw(16384, 1024)
        K.op("dve", lambda e: e.memset(zpad, 0.0), reads=["zpad"], writes=["zpad"])

        def to_tm(srcT, skey, cc):
            for g in range((nch + 3) // 4):
                k = min(4, nch - g * 4)
                b = nps()
                for i2 in range(k):
                    tc = g * 4 + i2
                    K.op("pe", lambda e: e.transpose(PS[b][:, i2 * 128:(i2 + 1) * 128], srcT[:, tc * 128:(tc + 1) * 128], ident),
                         reads=[skey, "ident"], writes=[psk[b]])
                evac(u_tm[:, g * 4:g * 4 + k, cc * 128:(cc + 1) * 128], PS[b][:, 0:k * 128].rearrange("p (j c) -> p j c", c=128), [psk[b]], ["u_tm"])

        for ch in range(12):
            K.dma("sp", zpad[:, 1:T + 1], zDT[ch * 128:(ch + 1) * 128, t0:t0 + T], reads=["zDT"], writes=["zpad"])
            dst = cvo[:, 0:T] if ch < 4 else x12[(ch - 4) // 4][:, ch % 4, 0:T]
            dkey = "tt" if ch < 4 else "x12"
            K.op("act", lambda e: e.activation(out=cvo[:, 0:T], in_=zpad[:, 1:T + 1], func=AF.Identity, scale=hcw[:, 12 + ch:13 + ch], bias=hcb[:, ch:ch + 1]),
                 reads=["zpad", "hcw", "hcb"], writes=["tt"])
            K.op("dve", lambda e: e.scalar_tensor_tensor(out=cvo[:, 0:T], in0=zpad[:, 0:T], scalar=hcw[:, ch:ch + 1], in1=cvo[:, 0:T], op0=ALU.mult, op1=ALU.add),
                 reads=["zpad", "hcw", "tt"], writes=["tt"])
            K.op("dve", lambda e: e.scalar_tensor_tensor(out=dst, in0=zpad[:, 2:T + 2], scalar=hcw[:, 24 + ch:25 + ch], in1=cvo[:, 0:T], op0=ALU.mult, op1=ALU.add),
                 reads=["zpad", "hcw", "tt"], writes=[dkey])
            if ch < 4:
                to_tm(cvo[:, 0:T], "tt", ch)
        for o in range(2):
            for fc in range(nch):
                K.tick()
                CSf, Gt = CSf_[fc % 2], Gt_[fc % 2]
                kcs = [f"CSf{fc % 2}{ri}" for ri in range(2)]
                kgt = [f"Gt{fc % 2}{ri}" for ri in range(2)]
                for ri in range(2):
                    K.dma("pool", CSf[ri][:, 0:nch, :], k_dft[L][ri, :, fc * 128:(fc + 1) * 128].rearrange("(tc p) f -> p tc f", p=128), writes=[kcs[ri]])
                    K.dma("sp", Gt[ri], hyG[L][o, ri, fc * 128:(fc + 1) * 128, :], reads=[f"hyG{L}"], writes=[kgt[ri]])
                bu = [nps(), nps()]
                for ri in range(2):
                    for tc in range(nch):
                        K.op("pe", lambda e: e.matmul(PS[bu[ri]], lhsT=CSf[ri][:, tc, :], rhs=u_tm[:, tc, :], start=(tc == 0), stop=(tc == nch - 1)),
                             reads=[kcs[ri], "u_tm"], writes=[psk[bu[ri]]])
                K.op("dve", lambda e: e.tensor_tensor(out=tt_[0], in0=PS[bu[0]], in1=Gt[0], op=ALU.mult), reads=[psk[bu[0]], kgt[0]], writes=["tt"])
                K.op("dve", lambda e: e.tensor_tensor(out=tt_[1], in0=PS[bu[1]], in1=Gt[1], op=ALU.mult), reads=[psk[bu[1]], kgt[1]], writes=["tt"])
                K.op("dve", lambda e: e.tensor_tensor(out=tt_[2], in0=PS[bu[1]], in1=Gt[0], op=ALU.mult), reads=[psk[bu[1]], kgt[0]], writes=["tt"])
                K.op("dve", lambda e: e.tensor_tensor(out=tt_[3], in0=PS[bu[0]], in1=Gt[1], op=ALU.mult), reads=[psk[bu[0]], kgt[1]], writes=["tt"])
                K.op("pool", lambda e: e.tensor_tensor(out=Yre[:, fc, :], in0=tt_[0], in1=tt_[1], op=ALU.add), reads=["tt"], writes=["Yre"])
                K.op("pool", lambda e: e.tensor_tensor(out=Yim[:, fc, :], in0=tt_[2], in1=tt_[3], op=ALU.subtract), reads=["tt"], writes=["Yim"])
            for q0 in range(0, T, 512):
                n = min(512, T - q0)
                for ri in range(2):
                    K.dma("pool", CSb[ri][:, 0:nch, 0:n], k_dft[L][2 + ri, :, q0:q0 + n].rearrange("(fc p) t -> p fc t", p=128), writes=[f"CSb{ri}"])
                for cc in range(4):
                    b = nps()
                    for fc in range(nch):
                        K.op("pe", lambda e: e.matmul(PS[b][:, 0:n], lhsT=Yre[:, fc, cc * 128:(cc + 1) * 128], rhs=CSb[0][:, fc, 0:n], start=(fc == 0), stop=False),
                             reads=["Yre", "CSb0"], writes=[psk[b]])
                        K.op("pe", lambda e: e.matmul(PS[b][:, 0:n], lhsT=Yim[:, fc, cc * 128:(cc + 1) * 128], rhs=CSb[1][:, fc, 0:n], start=False, stop=(fc == nch - 1)),
                             reads=["Yim", "CSb1"], writes=[psk[b]])
                    if o == 0:
                        K.op("dve", lambda e: e.tensor_tensor(out=cvo[:, q0:q0 + n], in0=PS[b][:, 0:n], in1=x12[0][:, cc, q0:q0 + n], op=ALU.mult),
                             reads=[psk[b], "x12"], writes=["tt"])
                        for i2 in range(n // 128):
                            tc = q0 // 128 + i2
                            b2 = nps()
                            K.op("pe", lambda e: e.transpose(PS[b2][:, 0:128], cvo[:, tc * 128:(tc + 1) * 128], ident), reads=["tt", "ident"], writes=[psk[b2]])
                            evac(u2[:, tc, cc * 128:(cc + 1) * 128], PS[b2][:, 0:128], [psk[b2]], ["u2"])
                    else:
                        K.op("dve", lambda e: e.tensor_tensor(out=hT[:, 12 + cc, t0 + q0:t0 + q0 + n], in0=PS[b][:, 0:n], in1=x12[1][:, cc, q0:q0 + n], op=ALU.mult),
                             reads=[psk[b], "x12"], writes=["hT"])
            if o == 0:
                K.op("pool", lambda e: e.tensor_copy(out=u_tm[:, 0:nch, :], in_=u2[:, 0:nch, :]), reads=["u2", "u_tm"], writes=["u_tm"])

    def mix_params(l):
        lam_init = 0.8 - 0.6 * math.exp(-0.3 * l)
        K.dma("sp", lam_t, dlam[l:l + 1].rearrange("o a b -> o (a b)").partition_broadcast(128), writes=["lam_t"])
        K.op("dve", lambda e: e.tensor_tensor(out=lam_t[:, 0:64], in0=lam_t[:, 0:64], in1=lam_t[:, 64:128], op=ALU.mult), reads=["lam_t"], writes=["lam_t"])
        K.op("dve", lambda e: e.tensor_tensor(out=lam_t[:, 128:192], in0=lam_t[:, 128:192], in1=lam_t[:, 192:256], op=ALU.mult), reads=["lam_t"], writes=["lam_t"])
        K.op("dve", lambda e: e.reduce_sum(out=lam_s[:, 0:2], in_=lam_t.rearrange("p (a b) -> p a b", b=128)[:, :, 0:64], axis=AX.X),
             reads=["lam_t"], writes=["lam_s"])
        K.op("act", lambda e: e.activation(out=lam_s[:, 0:2], in_=lam_s[:, 0:2], func=AF.Exp), reads=["lam_s"], writes=["lam_s"])
        K.op("dve", lambda e: e.tensor_tensor(out=lam_s[:, 2:3], in0=lam_s[:, 1:2], in1=lam_s[:, 0:1], op=ALU.subtract), reads=["lam_s"], writes=["lam_s"])
        K.op("dve", lambda e: e.tensor_scalar(out=lam_s[:, 2:3], in0=lam_s[:, 2:3], scalar1=-lam_init, scalar2=None, op0=ALU.add), reads=["lam_s"], writes=["lam_s"])
        load_colT(og_s[:, 0:1], dout_g[l:l + 1, :], 1, "og_s")
        K.op("dve", lambda e: e.tensor_scalar(out=og_s[:, 0:1], in0=og_s[:, 0:1], scalar1=1.0 - lam_init, scalar2=None, op0=ALU.mult), reads=["og_s"], writes=["og_s"])
        K.dma("sp", esink, ssink[l:l + 1, :].partition_broadcast(128), writes=["esink"])
        K.op("act", lambda e: e.activation(out=esink, in_=esink, func=AF.Exp), reads=["esink"], writes=["esink"])
        load_colT(mng, mnorm_g[l].rearrange("(c p) -> c p", p=128), 4, "mng")
        for r in range(3):
            load_colT(hcw[:, r * 12:(r + 1) * 12], hy_cw[l, r].rearrange("(c p) -> c p", p=128), 12, "hcw")
        load_colT(hcb, hy_cb[l].rearrange("(c p) -> c p", p=128), 12, "hcb")

    def MIX(l):
        mix_params(l)
        for which in "BCAD":
            if which not in MIXSEL:
                continue
            for (t0, T, is_s) in SEGS:
                if which in "BC":
                    attn_core(l, t0, T, is_s, which)
                elif which == "A":
                    mlstm(l, t0, T, is_s)
                else:
                    hyena(l, t0, T, is_s)
            K.fence()

    def mod_load(l, g):
        i = g % 2
        wv_ = w_mod[l].rearrange("(kc p) n -> p kc n", p=128)
        K.dma("pool", wbuf[i], wv_[:, :, g * 512:(g + 1) * 512], writes=[f"wbuf{i}"])

    def mod_compute(l, g):
        i = g % 2
        b = g % 4
        for j in range(4):
            for kc in range(16):
                K.op("pe", lambda e: e.matmul(PS[b][:, j * 2:j * 2 + 2], lhsT=wbuf[i][:, kc, j * 128:(j + 1) * 128],
                                              rhs=scb.rearrange("p (j k) -> p k j", j=2)[:, kc, :],
                                              start=(kc == 0), stop=(kc == 15)),
                     reads=[f"wbuf{i}", "scb"], writes=[psk[b]])
        K.op("dve", lambda e: e.tensor_tensor(out=modTs[l][:, g * 4:(g + 1) * 4, :], in0=PS[b][:, 0:8].rearrange("p (c j) -> p c j", j=2),
                                              in1=bmTs[l][:, g * 4:(g + 1) * 4].unsqueeze(2).to_broadcast([128, 4, 2]), op=ALU.add),
             reads=[psk[b], f"bmT{l}"], writes=[f"modT{l}"])
        for which, gT, gk, glast in ((0, n1Ts[l], f"n1T{l}", 7), (1, n2Ts[l], f"n2T{l}", 19)):
            if g == glast:
                base = 16 if which == 0 else 64
                K.op("dve", lambda e: e.scalar_tensor_tensor(out=nscales[l][:, which], in0=modTs[l][:, base:base + 16, :], scalar=1.0,
                                                             in1=gT.unsqueeze(2).to_broadcast([128, 16, 2]),
                                                             op0=ALU.add, op1=ALU.mult),
                     reads=[f"modT{l}", gk], writes=[f"nscale{l}"])

    def mod_atoms(l, g0, g1):
        seq = []
        for g in range(g0, g1):
            seq.append(("L", g))
        out_ = []
        ng = g1 - g0
        for k in range(ng + 1):
            if k < ng:
                out_.append(("L", g0 + k))
            if k >= 1:
                out_.append(("C", g0 + k - 1))
        for kind, g in out_:
            if kind == "L":
                mod_load(l, g)
            else:
                mod_compute(l, g)
            K.end_atom()

    for l_ in range(DEPTH):
        load_colT(bmTs[l_], b_mod[l_].rearrange("(c p) -> c p", p=128), 96, f"bmT{l_}")
        load_colT(n1Ts[l_], norm1_g[l_].rearrange("(c p) -> c p", p=128), 16, f"n1T{l_}")
        load_colT(n2Ts[l_], norm2_g[l_].rearrange("(c p) -> c p", p=128), 16, f"n2T{l_}")
    mod_load(0, 0)
    for g in range(8):
        if g + 1 < 8:
            mod_load(0, g + 1)
        mod_compute(0, g)
    K.record()
    mod_atoms(0, 8, 24)
    mod_atoms(1, 0, 24)
    K.stop_record()
    mod_queue = K.aux
    K.aux = []

    for l in range(DEPTH):
        modT = modTs[l]
        nscale = nscales[l]
        mk = f"modT{l}"
        nk_ = f"nscale{l}"
        for tb in range(3):
            t0, cj, sh = norm_block(l, 0, tb, False)
            for c in range(16):
                K.op("dve", lambda e: e.tensor_tensor(out=tmpf, in0=xblk[:, c, :], in1=rstd, op=ALU.mult),
                     reads=["xblk", "rstd"], writes=["tmpf"])
                K.op("act", lambda e: e.activation(out=hT[:, c, t0:t0 + TB], in_=tmpf, func=AF.Identity,
                                                   scale=nscale[:, 0, c, cj:cj + 1], bias=modT[:, sh * 16 + c, cj:cj + 1]),
                     reads=["tmpf", nk_, mk], writes=["hT"])
        if debug:
            dbg[f"hT{l}"] = (hT, [128, 16, NT], "hT")

        K.fence()
        if "D" in MIXSEL:
            K.record()
            for L in (256, 1024):
                hy_filters(l, L)
            K.stop_record()
        wv = w_in[l].rearrange("(kc p) n -> p kc n", p=128)
        groups = [
            (0, 512, (zAT, 0), None), (512, 512, (zAT, 512), ZC["Ak"]), (1024, 512, None, ZC["Av"]),
            (1536, 512, (zAT, 1024), None), (2048, 16, (zAT, 1536), None),
            (2064, 512, None, ZC["Bq"]), (2576, 512, None, ZC["Bk"]), (3088, 512, None, ZC["Bv"]),
            (3600, 512, None, ZC["Cq"]), (4112, 512, None, ZC["Ck"]),
            (4624, 512, (zDT, 0), None), (5136, 512, (zDT, 512), None), (5648, 512, (zDT, 1024), None),
        ]
        for (c0, ncols, fm, tm) in groups:
            i = load_w(wv[:, :, c0:c0 + ncols], ncols)
            if fm is not None:
                dst, r0 = fm
                for j in range((ncols + 127) // 128):
                    m = min(128, ncols - j * 128)
                    for tb in range(3):
                        b = nps()
                        for kc in range(16):
                            K.op("pe", lambda e: e.matmul(PS[b][0:m, :], lhsT=wbuf[i][:, kc, j * 128:j * 128 + m],
                                                          rhs=hT[:, kc, tb * TB:(tb + 1) * TB], start=(kc == 0), stop=(kc == 15)),
                                 reads=[f"wbuf{i}", "hT"], writes=[psk[b]])
                        s = srr[0]
                        srr[0] = (s + 1) % 4
                        evac(stage[s][0:m, :], PS[b][0:m, :], [psk[b]], [f"stage{s}"])
                        K.dma("sp", dst[r0 + j * 128:r0 + j * 128 + m, tb * TB:(tb + 1) * TB], stage[s][0:m, :],
                              reads=[f"stage{s}"], writes=[dst.tensor.name])
                        K.tick()
            if tm is not None:
                for tc in range(NT // 128):
                    b = nps()
                    for kc in range(16):
                        K.op("pe", lambda e: e.matmul(PS[b][:, 0:ncols], lhsT=hT[:, kc, tc * 128:(tc + 1) * 128],
                                                      rhs=wbuf[i][:, kc, 0:ncols], start=(kc == 0), stop=(kc == 15)),
                             reads=[f"wbuf{i}", "hT"], writes=[psk[b]])
                    s = srr[0]
                    srr[0] = (s + 1) % 4
                    evac(stage[s][:, 0:ncols], PS[b][:, 0:ncols], [psk[b]], [f"stage{s}"])
                    K.dma("sp", ztm[tc * 128:(tc + 1) * 128, tm:tm + ncols], stage[s][:, 0:ncols],
                          reads=[f"stage{s}"], writes=["ztm"])
                    K.tick()


        K.flush()
        if debug and debug < 0 and l == 0:
            break
        K.fence()
        if l == 0:
            K.aux = mod_queue
        MIX(l)
        K.flush()
        K.fence()
        if debug:
            dbg[f"mixT{l}"] = 1
            K.dma("pool", dbg_mix[l].rearrange("(c p) t -> p c t", p=128), hT, reads=["hT"], writes=["dbgmix"])
            K.fence()

        wv = w_out[l].rearrange("(kc p) n -> p kc n", p=128)
        for tb in range(3):
            K.dma("sp", xres[:, :, tb * TB:(tb + 1) * TB], xs[:, tb * TB:(tb + 1) * TB].rearrange("(c p) t -> p c t", p=128),
                  reads=["xs"], writes=["xblk"])
        for g in range(4):
            i = load_w(wv[:, :, g * 512:(g + 1) * 512], 512)
            for j in range(4):
                oc = g * 4 + j
                for tb in range(3):
                    t0 = tb * TB
                    cj = 0 if tb == 0 else 1
                    b = nps()
                    for kc in range(16):
                        K.op("pe", lambda e: e.matmul(PS[b], lhsT=wbuf[i][:, kc, j * 128:(j + 1) * 128],
                                                      rhs=hT[:, kc, t0:t0 + TB], start=(kc == 0), stop=(kc == 15)),
                             reads=[f"wbuf{i}", "hT"], writes=[psk[b]])
                    K.op("dve", lambda e: e.scalar_tensor_tensor(out=xres[:, oc, t0:t0 + TB], in0=PS[b], scalar=modT[:, 32 + oc, cj:cj + 1],
                                                                 in1=xres[:, oc, t0:t0 + TB], op0=ALU.mult, op1=ALU.add),
                         reads=[psk[b], mk, "xblk"], writes=["xblk"])
        if debug:
            for tb in range(3):
                K.dma("sp", xs[:, tb * TB:(tb + 1) * TB].rearrange("(c p) t -> p c t", p=128), xres[:, :, tb * TB:(tb + 1) * TB],
                      reads=["xblk"], writes=["xs"])
        for tb in range(3):
            t0, cj, _ = norm_block(l, 1, tb, True)
            for c in range(16):
                K.op("dve", lambda e: e.tensor_tensor(out=tmpf, in0=xres[:, c, t0:t0 + TB], in1=rstd, op=ALU.mult),
                     reads=["xblk", "rstd"], writes=["tmpf"])
                K.op("act", lambda e: e.activation(out=hT[:, c, t0:t0 + TB], in_=tmpf, func=AF.Identity,
                                                   scale=nscale[:, 1, c, cj:cj + 1], bias=modT[:, 48 + c, cj:cj + 1]),
                     reads=["tmpf", nk_, mk], writes=["hT"])
        if debug == 2:
            break
        K.fence()

        for r in range(3):
            load_colT(fcw[:, r * 88:(r + 1) * 88], f_cw[l, r].rearrange("(c p) -> c p", p=128), 88, "fcw")
        load_colT(fcb, f_cb[l].rearrange("(c p) -> c p", p=128), 88, "fcb")
        wvu = w_up[l].rearrange("(kc p) n -> p kc n", p=128)
        urr = 0
        dpar = 0
        for g in range(22):
            i = wrr[0]
            wrr[0] ^= 1
            K.dma("pool", wbuf[i][:, :, 0:256], wvu[:, :, g * 256:(g + 1) * 256], writes=[f"wbuf{i}"])
            K.dma("pool", wbuf[i][:, :, 256:512], wvu[:, :, D_FF + g * 256:D_FF + (g + 1) * 256], writes=[f"wbuf{i}"])
            K.dma("pool", wdn, w_dn[l, g * 256:(g + 1) * 256, :].rearrange("(k p) n -> p k n", p=128), writes=["wdn"])
            for jj in range(2):
                j = g * 2 + jj
                for half in range(2):
                    fch = j + 44 * half
                    wc = half * 256 + jj * 128
                    bset = [0, 1, 2] if half == 0 else [3, 4, 5]
                    for tb in range(3):
                        b = bset[tb]
                        for kc in range(16):
                            K.op("pe", lambda e: e.matmul(PS[b], lhsT=wbuf[i][:, kc, wc:wc + 128], rhs=hT[:, kc, tb * TB:(tb + 1) * TB],
                                                          start=(kc == 0), stop=(kc == 15)),
                                 reads=[f"wbuf{i}", "hT"], writes=[psk[b]])
                    for tb in range(3):
                        b = bset[tb]
                        t0 = tb * TB
                        u = ubuf[urr]
                        uk = f"ubuf{urr}"
                        urr ^= 1
                        ranges = [(0, 256), (256, 512)] if tb == 0 else [(0, 512)]
                        K.op("act", lambda e: e.activation(out=u, in_=PS[b], func=AF.Identity, scale=fcw[:, 88 + fch:88 + fch + 1],
                                                           bias=fcb[:, fch:fch + 1]),
                             reads=[psk[b], "fcw", "fcb"], writes=[uk])
                        for (ra, rb) in ranges:
                            K.op("dve", lambda e: e.scalar_tensor_tensor(out=u[:, ra + 1:rb], in0=PS[b][:, ra:rb - 1],
                                                                         scalar=fcw[:, fch:fch + 1], in1=u[:, ra + 1:rb],
                                                                         op0=ALU.mult, op1=ALU.add),
                                 reads=[psk[b], "fcw", uk], writes=[uk])
                            K.op("dve", lambda e: e.scalar_tensor_tensor(out=u[:, ra:rb - 1], in0=PS[b][:, ra + 1:rb],
                                                                         scalar=fcw[:, 176 + fch:176 + fch + 1], in1=u[:, ra:rb - 1],
                                                                         op0=ALU.mult, op1=ALU.add),
                                 reads=[psk[b], "fcw", uk], writes=[uk])
                        if tb == 1:
                            bn_ = bset[2]
                            K.op("dve", lambda e: e.scalar_tensor_tensor(out=u[:, TB - 1:TB], in0=PS[bn_][:, 0:1],
                                                                         scalar=fcw[:, 176 + fch:176 + fch + 1], in1=u[:, TB - 1:TB],
                                                                         op0=ALU.mult, op1=ALU.add),
                                 reads=[psk[bn_], "fcw", uk], writes=[uk])
                        if tb == 2:
                            bp_ = bset[1]
                            K.op("dve", lambda e: e.scalar_tensor_tensor(out=u[:, 0:1], in0=PS[bp_][:, TB - 1:TB],
                                                                         scalar=fcw[:, fch:fch + 1], in1=u[:, 0:1],
                                                                         op0=ALU.mult, op1=ALU.add),
                                 reads=[psk[bp_], "fcw", uk], writes=[uk])
                        if half == 0:
                            K.op("act", lambda e: e.activation(out=actTg[:, jj, t0:t0 + TB], in_=u, func=AF.Silu), reads=[uk], writes=["actTg"])
                        else:
                            K.op("pool", lambda e: e.tensor_tensor(out=actTg[:, jj, t0:t0 + TB], in0=actTg[:, jj, t0:t0 + TB], in1=u, op=ALU.mult),
                                 reads=[uk, "actTg"], writes=["actTg"])
            for oc in range(16):
                for tb in range(3):
                    t0 = tb * TB
                    cj = 0 if tb == 0 else 1
                    b = 6 + dpar
                    dpar ^= 1
                    for kc in range(2):
                        K.op("pe", lambda e: e.matmul(PS[b], lhsT=wdn[:, kc, oc * 128:(oc + 1) * 128], rhs=actTg[:, kc, t0:t0 + TB],
                                                      start=(kc == 0), stop=(kc == 1)),
                             reads=["wdn", "actTg"], writes=[psk[b]])
                    K.op("dve", lambda e: e.scalar_tensor_tensor(out=xres[:, oc, t0:t0 + TB], in0=PS[b], scalar=modT[:, 80 + oc, cj:cj + 1],
                                                                 in1=xres[:, oc, t0:t0 + TB], op0=ALU.mult, op1=ALU.add),
                         reads=[psk[b], mk, "xblk"], writes=["xblk"])
        K.fence()
        if l < DEPTH - 1 or debug:
            for tb in range(3):
                K.dma("sp", xs[:, tb * TB:(tb + 1) * TB].rearrange("(c p) t -> p c t", p=128), xres[:, :, tb * TB:(tb + 1) * TB],
                      reads=["xblk"], writes=["xs"])
        if l == DEPTH - 1:
            for tq in range(NT // 128):
                for g in range(4):
                    b = nps()
                    for jq in range(4):
                        fc = g * 4 + jq
                        K.op("pe", lambda e: e.transpose(PS[b][:, jq * 128:(jq + 1) * 128], xres[:, fc, tq * 128:(tq + 1) * 128], ident),
                             reads=["xblk", "ident"], writes=[psk[b]])
                    evac(yt_[tq % 2][:, g * 512:(g + 1) * 512], PS[b], [psk[b]], [f"ytile{tq % 2}"])
                K.dma("sp", y_out[tq * 128:(tq + 1) * 128, :], yt_[tq % 2], reads=[f"ytile{tq % 2}"], writes=["y_out"])
        K.fence()

    K.finish()
    return nc, dbg


def host_consts():
    c = {}
    c["k_ident"] = np.eye(128, dtype=np.float32)
    sel = np.zeros((8, 8, 128), np.float32)
    for p in range(8):
        sel[p, p, :] = 1.0
    c["k_sel"] = sel.reshape(8, 8 * 128)
    s = np.arange(128)[:, None]
    t = np.arange(128)[None, :]
    c["k_tri"] = np.concatenate([(s <= t), (s >= t)], axis=1).astype(np.float32)
    tq = np.arange(384)[None, :] - 128
    c["k_band"] = (np.abs(s - tq) <= 128).astype(np.float32)
    for dh, nm in ((64, "k_rope64"), (128, "k_rope128")):
        n_tok = 1024
        t_row = np.repeat(np.arange(n_tok // 64, dtype=np.float32), 64)
        t_col = np.tile(np.arange(64, dtype=np.float32), n_tok // 64)
        n_freq = dh // 4
        inv = (10000.0 ** (-np.arange(n_freq, dtype=np.float32) / n_freq)).astype(np.float32)
        ang = np.concatenate([t_row[:, None] * inv, t_col[:, None] * inv], axis=-1).astype(np.float32)
        c[nm] = np.concatenate([np.cos(ang), np.sin(ang)], axis=1).astype(np.float32)
    for L in (256, 1024):
        tt = np.linspace(0.0, 1.0, L, dtype=np.float32)[:, None]
        w = (np.float32(2.0 * math.pi / L) * np.arange(L, dtype=np.float32))[:, None]
        bands = np.linspace(1e-4, 15, 16, dtype=np.float32)[None, :]
        feats = np.concatenate([tt, np.cos(bands * w), -np.sin(bands * w), np.ones((L, 1), np.float32)], axis=-1)
        c[f"k_feat{L}"] = np.ascontiguousarray(feats.T).astype(np.float32)
        rates = np.abs(np.linspace(HY_FAST, HY_SLOW, 512, dtype=np.float32))
        c[f"k_win{L}"] = np.exp(-tt * rates).astype(np.float32)
        f = np.arange(L, dtype=np.float64)[:, None] + 0.5
        tt64 = np.arange(L, dtype=np.float64)[None, :]
        th = np.pi * f * tt64 / L
        C = np.cos(th)
        S = np.sin(th)
        c[f"k_dft{L}"] = np.stack([C.T, S.T, C, S]).astype(np.float32)
    return c


def make_in_maps(inputs, n_cores=8):
    consts = host_consts()
    maps = []
    f = lambda a: np.ascontiguousarray(a, dtype=np.float32)
    for c in range(n_cores):
        b = c % 4
        m = dict(consts)
        m["xin"] = f(np.concatenate([inputs["x_prompt"][2 * c], inputs["x_prompt"][2 * c + 1], inputs["x_sample"][b]], axis=0))
        m["cond2"] = f(np.stack([inputs["c_ctx"], inputs["c"][b]]))
        m["c_dk"] = f(inputs["cache_diff_k"][b].reshape(DEPTH, PAST, 512))
        m["c_dv"] = f(inputs["cache_diff_v"][b].reshape(DEPTH, PAST, 512))
        m["c_sk"] = f(inputs["cache_swa_k"][b].reshape(DEPTH, PAST, 256))
        m["c_sv"] = f(inputs["cache_swa_v"][b].reshape(DEPTH, PAST, 256))
        m["st_C"] = f(inputs["state_mlstm_C"][b])
        m["st_n"] = f(inputs["state_mlstm_n"][b])
        m["st_m"] = f(inputs["state_mlstm_m"][b].reshape(DEPTH, 8))
        for k in ("w_mod", "b_mod", "norm1_g", "norm2_g", "w_in", "mlstm_norm_g", "diff_qn_g", "diff_kn_g", "diff_lam",
                  "diff_out_g", "swa_qn_g", "swa_kn_g", "swa_sink", "hy_conv_w", "hy_conv_b", "hy_w1", "hy_b1", "hy_freq",
                  "hy_w2", "hy_b2", "hy_w3", "hy_b3", "hy_bias", "w_out", "ffn_w_up", "ffn_conv_w", "ffn_conv_b", "ffn_w_down"):
            m[k] = f(inputs[k])
        m["mlstm_ig_b"] = f(inputs["mlstm_ig_b"].reshape(DEPTH, 8))
        m["mlstm_fg_b"] = f(inputs["mlstm_fg_b"].reshape(DEPTH, 8))
        maps.append(m)
    return maps


_CACHE = {}


def kernel(**inputs):
    inputs = {k: np.asarray(v) for k, v in inputs.items()}
    if "nc" not in _CACHE:
        _CACHE["nc"] = build(debug=False)[0]
    nc = _CACHE["nc"]
    maps = make_in_maps(inputs, 8)
    res = run_bass_kernel_spmd(nc, maps, core_ids=list(range(8)))
    R = res.results
    y_prompt = np.zeros((16, 256, D), np.float32)
    y_sample = np.zeros((4, 1024, D), np.float32)
    ndk = np.zeros((16, DEPTH, 256, 4, 2, 64), np.float32)
    ndv = np.zeros((16, DEPTH, 256, 4, 128), np.float32)
    nsk = np.zeros((16, DEPTH, 256, 2, 128), np.float32)
    nsv = np.zeros((16, DEPTH, 256, 2, 128), np.float32)
    nC = np.zeros((16, DEPTH, 2, 4, 128, 128), np.float32)
    nn = np.zeros((16, DEPTH, 2, 4, 128), np.float32)
    nm = np.zeros((16, DEPTH, 2, 4), np.float32)
    for c in range(8):
        r = R[c]
        y = np.asarray(r["y_out"])
        y_prompt[2 * c] = y[0:256]
        y_prompt[2 * c + 1] = y[256:512]
        if c < 4:
            y_sample[c] = y[512:1536]
        for s in range(2):
            b = 2 * c + s
            ndk[b] = np.asarray(r["o_dk"])[s].reshape(DEPTH, 256, 4, 2, 64)
            ndv[b] = np.asarray(r["o_dv"])[s].reshape(DEPTH, 256, 4, 128)
            nsk[b] = np.asarray(r["o_sk"])[s].reshape(DEPTH, 256, 2, 128)
            nsv[b] = np.asarray(r["o_sv"])[s].reshape(DEPTH, 256, 2, 128)
            nC[b] = np.asarray(r["o_C"])[s]
            nn[b] = np.asarray(r["o_n"])[s]
            nm[b] = np.asarray(r["o_m"])[s].reshape(DEPTH, 2, 4)
    return (y_prompt, y_sample, ndk, ndv, nsk, nsv, nC, nn, nm)
```
